# Optimizing a Trainium2 kernel written in Bass

```python
import math
import jax, jax.numpy as jnp
from jax import lax
import numpy as np

D_MODEL = 1024
BATCH = 16
SEQ = 256
DEPTH = 2
DEC_BATCH = 2
DEC_SEQ = 4096
PAST_LEN = 256

GRID_W = 64
Q_BLOCK = 128
ROPE_BASE = 10000.0
EPS = 1e-6
HEAD_DIM = 64
W_A = D_MODEL // 4
W_B = D_MODEL // 4
W_C = D_MODEL // 4
W_D = D_MODEL - W_A - W_B - W_C
A_HEADS = W_A // HEAD_DIM
A_HALF = HEAD_DIM // 2
B_HEADS = W_B // HEAD_DIM
B_KV_HEADS = B_HEADS // 2
B_KV_DIM = B_KV_HEADS * HEAD_DIM
C_HEADS = W_C // HEAD_DIM
C_GROUPS = 2
C_STATE = 64
C_CONV = 3
C_CHUNK = 128
C_BC_DIM = C_GROUPS * C_STATE
C_CONV_DIM = W_C + 2 * C_BC_DIM
D_GROUP = 16
D_GROUPS = W_D // D_GROUP
D_STATE = 64
FFN_DIM = ((8 * D_MODEL // 3 + 127) // 128) * 128
FFN_CONV = 3
IN_SIZES = (W_A, W_A, W_A, W_B, B_KV_DIM, B_KV_DIM, W_C, W_C, C_BC_DIM, C_BC_DIM, 2 * C_HEADS, W_D)
IN_DIM = sum(IN_SIZES)

kernel_name = 'hybrid_flow_prefix_step'

F32 = jnp.float32


def rmsnorm(x, g):
    xf = x.astype(F32)
    y = xf * lax.rsqrt(jnp.mean(xf * xf, axis=-1, keepdims=True) + EPS)
    return (y * g.astype(F32)).astype(x.dtype)


def grid_angles(L, dim):
    rows = L // GRID_W
    row = jnp.repeat(jnp.arange(rows), GRID_W).astype(F32)
    col = jnp.tile(jnp.arange(GRID_W), rows).astype(F32)
    n = dim // 4
    freqs = ROPE_BASE ** (-jnp.arange(n, dtype=F32) / n)
    return jnp.concatenate([row[:, None] * freqs, col[:, None] * freqs], axis=-1)


def apply_rope(x, ang):
    d = x.shape[-1]
    xf = x.astype(F32)
    cos = jnp.cos(ang)[None, :, None, :]
    sin = jnp.sin(ang)[None, :, None, :]
    x1, x2 = xf[..., : d // 2], xf[..., d // 2:]
    return jnp.concatenate([x1 * cos - x2 * sin, x1 * sin + x2 * cos], axis=-1).astype(x.dtype)


def over_query_blocks(fn, q):
    b, L = q.shape[:2]
    nb = L // Q_BLOCK
    blocks = jnp.moveaxis(q.reshape(b, nb, Q_BLOCK, *q.shape[2:]), 1, 0)
    out = lax.map(fn, blocks)
    return jnp.moveaxis(out, 0, 1).reshape(b, L, *out.shape[3:])


def diff_attention(q, k, v, lam):
    scale = A_HALF ** -0.5
    k1, k2 = k[..., :A_HALF], k[..., A_HALF:]

    def block(qb):
        s1 = jnp.einsum('bqhd,bkhd->bhqk', qb[..., :A_HALF], k1).astype(F32) * scale
        s2 = jnp.einsum('bqhd,bkhd->bhqk', qb[..., A_HALF:], k2).astype(F32) * scale
        p = jax.nn.softmax(s1, axis=-1) - lam * jax.nn.softmax(s2, axis=-1)
        return jnp.einsum('bhqk,bkhd->bqhd', p.astype(v.dtype), v)

    return over_query_blocks(block, q)


def gqa_attention(q, k, v):
    b = q.shape[0]
    hq, d = q.shape[2], q.shape[3]
    hk = k.shape[2]
    g = hq // hk
    scale = d ** -0.5

    def block(qb):
        qg = qb.reshape(b, Q_BLOCK, hk, g, d)
        s = jnp.einsum('bqkgd,bskd->bkgqs', qg, k).astype(F32) * scale
        p = jax.nn.softmax(s, axis=-1)
        o = jnp.einsum('bkgqs,bskd->bqkgd', p.astype(v.dtype), v)
        return o.reshape(b, Q_BLOCK, hq, d)

    return over_query_blocks(block, q)


def dwconv(x, w, bias):
    width, ch = w.shape
    pad = width // 2
    y = lax.conv_general_dilated(x, w[:, None, :].astype(x.dtype), window_strides=(1,),
                                 padding=[(pad, pad)], dimension_numbers=('NWC', 'WIO', 'NWC'),
                                 feature_group_count=ch)
    return y + bias.astype(x.dtype)


def ssd_scan(x, dt, A, Bm, Cm, h0):
    b, L, h, p = x.shape
    n = Bm.shape[-1]
    T = L // C_CHUNK
    xc = (x.astype(F32) * dt[..., None]).reshape(b, T, C_CHUNK, h, p)
    Bc = Bm.astype(F32).reshape(b, T, C_CHUNK, h, n)
    Cc = Cm.astype(F32).reshape(b, T, C_CHUNK, h, n)
    a_cum = jnp.cumsum((dt * A).reshape(b, T, C_CHUNK, h), axis=2)
    seg = a_cum[:, :, :, None, :] - a_cum[:, :, None, :, :]
    mask = jnp.tril(jnp.ones((C_CHUNK, C_CHUNK), bool))[None, None, :, :, None]
    decay = jnp.where(mask, jnp.exp(jnp.where(mask, seg, 0.0)), 0.0)
    gmat = jnp.einsum('btlhn,btshn->btlsh', Cc, Bc) * decay
    y_diag = jnp.einsum('btlsh,btshp->btlhp', gmat, xc)
    decay_to_end = jnp.exp(a_cum[:, :, -1:, :] - a_cum)
    chunk_states = jnp.einsum('btshn,btsh,btshp->bthpn', Bc, decay_to_end, xc)
    chunk_decay = jnp.exp(a_cum[:, :, -1, :])

    def step(hc, inp):
        st, dec = inp
        return hc * dec[..., None, None] + st, hc

    h_final, h_enter = lax.scan(step, h0.astype(F32),
                                (jnp.moveaxis(chunk_states, 1, 0), jnp.moveaxis(chunk_decay, 1, 0)))
    h_enter = jnp.moveaxis(h_enter, 0, 1)
    y_off = jnp.einsum('btlhn,bthpn,btlh->btlhp', Cc, h_enter, jnp.exp(a_cum))
    return (y_diag + y_off).reshape(b, L, h, p), h_final


def ssd_mixer(z, xs, Bm, Cm, dt, conv_w, conv_b, dt_bias, a_log, d_skip, norm_g, h0):
    b, L, _ = xs.shape
    xbc = jax.nn.silu(dwconv(jnp.concatenate([xs, Bm, Cm], axis=-1), conv_w, conv_b))
    xs, Bm, Cm = jnp.split(xbc, [W_C, W_C + C_BC_DIM], axis=-1)
    xh = xs.reshape(b, L, C_HEADS, HEAD_DIM)
    rep = C_HEADS // C_GROUPS
    Bh = jnp.repeat(Bm.reshape(b, L, C_GROUPS, C_STATE), rep, axis=2)
    Ch = jnp.repeat(Cm.reshape(b, L, C_GROUPS, C_STATE), rep, axis=2)
    dtf = jax.nn.softplus(dt.astype(F32).reshape(b, L, 2, C_HEADS) + dt_bias.astype(F32))
    A = -jnp.exp(a_log.astype(F32))
    y_f, h_f = ssd_scan(xh, dtf[:, :, 0], A[0], Bh, Ch, h0[:, 0])
    flip = lambda t: jnp.flip(t, axis=1)
    y_b, h_b = ssd_scan(flip(xh), flip(dtf[:, :, 1]), A[1], flip(Bh), flip(Ch), h0[:, 1])
    y = y_f + flip(y_b) + d_skip.astype(F32)[:, None] * xh.astype(F32)
    y = y.reshape(b, L, W_C).astype(z.dtype)
    return rmsnorm(y * jax.nn.silu(z), norm_g), jnp.stack([h_f, h_b], axis=1)


def s5_combine(e1, e2):
    a1, b1 = e1
    a2, b2 = e2
    return a1 * a2, a2 * b1 + b2


def s5_mixer(u, lam_re, lam_im, log_step, b_ri, c_ri, d_skip, w_glu, h0):
    b, L, _ = u.shape
    uf = u.astype(F32)
    lam = lax.complex(lam_re.astype(F32), lam_im.astype(F32))
    delta = jnp.exp(log_step.astype(F32))[..., None]
    a_bar = jnp.exp(lam * delta)
    bmat = lax.complex(b_ri[..., 0].astype(F32), b_ri[..., 1].astype(F32))
    b_bar = ((a_bar - 1.0) / lam)[..., None] * bmat
    cmat = lax.complex(c_ri[..., 0].astype(F32), c_ri[..., 1].astype(F32))
    h0c = lax.complex(h0[..., 0].astype(F32), h0[..., 1].astype(F32))
    ug = uf.reshape(b, L, D_GROUPS, D_GROUP).astype(jnp.complex64)
    bu = jnp.einsum('dgnc,blgc->dblgn', b_bar, ug)
    bu_f = bu[0].at[:, 0].add(a_bar[0] * h0c[:, 0])
    _, xf = lax.associative_scan(s5_combine, (jnp.broadcast_to(a_bar[0], bu_f.shape), bu_f), axis=1)
    bu_b = bu[1].at[:, -1].add(a_bar[1] * h0c[:, 1])
    _, xb = lax.associative_scan(s5_combine, (jnp.broadcast_to(a_bar[1], bu_b.shape), bu_b), axis=1,
                                 reverse=True)
    y = (jnp.real(jnp.einsum('gcn,blgn->blgc', cmat[0], xf))
         + jnp.real(jnp.einsum('gcn,blgn->blgc', cmat[1], xb)))
    y = y.reshape(b, L, W_D) + d_skip.astype(F32) * uf
    y = jax.nn.gelu(y).astype(u.dtype)
    ga, gb = jnp.split(y @ w_glu, 2, axis=-1)
    state = jnp.stack([xf[:, -1], xb[:, 0]], axis=1)
    return ga * jax.nn.sigmoid(gb), jnp.stack([jnp.real(state), jnp.imag(state)], axis=-1)


def adaln(cond, w_mod, b_mod):
    return (jax.nn.silu(cond) @ w_mod + b_mod).reshape(cond.shape[0], 6, D_MODEL)


def trunk_layer(x, mod, P, layer_idx, ctx):
    latent = ctx is not None
    b, L, _ = x.shape
    shift1, scale1, gate1, shift2, scale2, gate2 = (mod[:, i][:, None, :] for i in range(6))
    h = rmsnorm(x, P['g_pre1']) * (1.0 + scale1) + shift1
    offsets = np.cumsum(IN_SIZES)[:-1].tolist()
    aq, ak, av, bq, bk, bv, cz, cx, cb, cc, cdt, du = jnp.split(h @ P['w_in'], offsets, axis=-1)
    aq = aq.reshape(b, L, A_HEADS, HEAD_DIM)
    ak = ak.reshape(b, L, A_HEADS, HEAD_DIM)
    av = av.reshape(b, L, A_HEADS, HEAD_DIM)
    bq = rmsnorm(bq.reshape(b, L, B_HEADS, HEAD_DIM), P['b_qnorm'])
    bk = rmsnorm(bk.reshape(b, L, B_KV_HEADS, HEAD_DIM), P['b_knorm'])
    bv = bv.reshape(b, L, B_KV_HEADS, HEAD_DIM)
    if latent:
        ctx_ak, ctx_av, ctx_bk, ctx_bv, ssd_h0, s5_h0 = ctx
        ang_a = grid_angles(L, A_HALF)
        ang_b = grid_angles(L, HEAD_DIM)

        def rope_diff(t):
            return jnp.concatenate([apply_rope(t[..., :A_HALF], ang_a), apply_rope(t[..., A_HALF:], ang_a)], axis=-1)

        a_q = rope_diff(aq)
        a_k = jnp.concatenate([rope_diff(ak), ctx_ak.astype(ak.dtype)], axis=1)
        a_v = jnp.concatenate([av, ctx_av.astype(av.dtype)], axis=1)
        b_q = apply_rope(bq, ang_b)
        b_k = jnp.concatenate([apply_rope(bk, ang_b), ctx_bk.astype(bk.dtype)], axis=1)
        b_v = jnp.concatenate([bv, ctx_bv.astype(bv.dtype)], axis=1)
    else:
        a_q, a_k, a_v, b_q, b_k, b_v = aq, ak, av, bq, bk, bv
        ssd_h0 = jnp.zeros((b, 2, C_HEADS, HEAD_DIM, C_STATE), F32)
        s5_h0 = jnp.zeros((b, 2, D_GROUPS, D_STATE, 2), F32)
    lam_init = 0.8 - 0.6 * math.exp(-0.3 * layer_idx)
    lp = P['a_lam'].astype(F32)
    lam = jnp.exp(jnp.sum(lp[0] * lp[1])) - jnp.exp(jnp.sum(lp[2] * lp[3])) + lam_init
    ya = diff_attention(a_q, a_k, a_v, lam)
    ya = (rmsnorm(ya, P['a_subln']) * (1.0 - lam_init)).reshape(b, L, W_A)
    yb = gqa_attention(b_q, b_k, b_v).reshape(b, L, W_B)
    yc, ssd_state = ssd_mixer(cz, cx, cb, cc, cdt, P['c_conv_w'], P['c_conv_b'], P['c_dt_bias'],
                              P['c_a_log'], P['c_d'], P['c_norm'], ssd_h0)
    yd, s5_state = s5_mixer(du, P['d_lam_re'], P['d_lam_im'], P['d_log_step'], P['d_b'], P['d_c'],
                            P['d_d'], P['d_glu'], s5_h0)
    y = jnp.concatenate([ya, yb, yc.astype(ya.dtype), yd.astype(ya.dtype)], axis=-1) @ P['w_out']
    x = x + gate1 * rmsnorm(y, P['g_post1'])
    h = rmsnorm(x, P['g_pre2']) * (1.0 + scale2) + shift2
    g, v = jnp.split(dwconv(h @ P['w_up'], P['ffn_conv_w'], P['ffn_conv_b']), 2, axis=-1)
    x = x + gate2 * rmsnorm((jax.nn.silu(g) * v) @ P['w_down'], P['g_post2'])
    if latent:
        return x, None
    return x, (ak, av, bk, bv, ssd_state, s5_state)


def setup_inputs(seed: int = 0) -> dict:
    key = jax.random.key(seed)
    ks = iter(jax.random.split(key, 64))

    def nrm(shape, scale):
        return jax.random.normal(next(ks), shape, F32) * scale

    def gain(shape):
        return 1.0 + nrm(shape, 0.02)

    def unif(shape, lo, hi):
        return jax.random.uniform(next(ks), shape, F32, lo, hi)

    D, L, F = D_MODEL, DEPTH, FFN_DIM
    dt0 = jnp.exp(unif((L, 2, C_HEADS), math.log(1e-3), math.log(1e-1)))
    n = jnp.arange(D_STATE, dtype=F32)
    return {
        'x_prompt': nrm((BATCH, SEQ, D), 1.0),
        'x_sample': nrm((DEC_BATCH, DEC_SEQ, D), 1.0),
        'cache_a_k': nrm((DEC_BATCH, L, PAST_LEN, A_HEADS, HEAD_DIM), 1.0),
        'cache_a_v': nrm((DEC_BATCH, L, PAST_LEN, A_HEADS, HEAD_DIM), 1.0),
        'cache_b_k': nrm((DEC_BATCH, L, PAST_LEN, B_KV_HEADS, HEAD_DIM), 1.0),
        'cache_b_v': nrm((DEC_BATCH, L, PAST_LEN, B_KV_HEADS, HEAD_DIM), 1.0),
        'state_ssd': nrm((DEC_BATCH, L, 2, C_HEADS, HEAD_DIM, C_STATE), 0.1),
        'state_s5': nrm((DEC_BATCH, L, 2, D_GROUPS, D_STATE, 2), 0.05),
        'c': nrm((DEC_BATCH, D), 1.0),
        'c_ctx': nrm((D,), 1.0),
        'w_mod': nrm((L, D, 6 * D), 0.3 * D ** -0.5),
        'b_mod': nrm((L, 6 * D), 0.02),
        'g_pre1': gain((L, D)),
        'g_post1': gain((L, D)),
        'g_pre2': gain((L, D)),
        'g_post2': gain((L, D)),
        'w_in': nrm((L, D, IN_DIM), D ** -0.5),
        'a_lam': nrm((L, 4, A_HALF), 0.1),
        'a_subln': gain((L, HEAD_DIM)),
        'b_qnorm': gain((L, HEAD_DIM)),
        'b_knorm': gain((L, HEAD_DIM)),
        'c_conv_w': nrm((L, C_CONV, C_CONV_DIM), C_CONV ** -0.5),
        'c_conv_b': nrm((L, C_CONV_DIM), 0.02),
        'c_dt_bias': dt0 + jnp.log(-jnp.expm1(-dt0)),
        'c_a_log': jnp.log(unif((L, 2, C_HEADS), 1.0, 16.0)),
        'c_d': 1.0 + nrm((L, C_HEADS), 0.1),
        'c_norm': gain((L, W_C)),
        'd_lam_re': -0.5 + nrm((L, 2, D_GROUPS, D_STATE), 0.01),
        'd_lam_im': math.pi * n + nrm((L, 2, D_GROUPS, D_STATE), 0.01),
        'd_log_step': unif((L, 2, D_GROUPS), math.log(1e-3), math.log(1e-1)),
        'd_b': nrm((L, 2, D_GROUPS, D_STATE, D_GROUP, 2), (2 * D_GROUP) ** -0.5),
        'd_c': nrm((L, 2, D_GROUPS, D_GROUP, D_STATE, 2), (2 * D_STATE) ** -0.5),
        'd_d': nrm((L, W_D), 0.5),
        'd_glu': nrm((L, W_D, 2 * W_D), W_D ** -0.5),
        'w_out': nrm((L, D, D), D ** -0.5),
        'w_up': nrm((L, D, 2 * F), D ** -0.5),
        'ffn_conv_w': nrm((L, FFN_CONV, 2 * F), FFN_CONV ** -0.5),
        'ffn_conv_b': nrm((L, 2 * F), 0.02),
        'w_down': nrm((L, F, D), F ** -0.5),
    }


def reference(x_prompt, x_sample, cache_a_k, cache_a_v, cache_b_k, cache_b_v, state_ssd, state_s5,
              c, c_ctx, w_mod, b_mod, g_pre1, g_post1, g_pre2, g_post2, w_in, a_lam, a_subln,
              b_qnorm, b_knorm, c_conv_w, c_conv_b, c_dt_bias, c_a_log, c_d, c_norm,
              d_lam_re, d_lam_im, d_log_step, d_b, d_c, d_d, d_glu, w_out, w_up,
              ffn_conv_w, ffn_conv_b, w_down):
    def layer_params(l):
        return {
            'g_pre1': g_pre1[l], 'g_post1': g_post1[l], 'g_pre2': g_pre2[l], 'g_post2': g_post2[l],
            'w_in': w_in[l], 'a_lam': a_lam[l], 'a_subln': a_subln[l],
            'b_qnorm': b_qnorm[l], 'b_knorm': b_knorm[l],
            'c_conv_w': c_conv_w[l], 'c_conv_b': c_conv_b[l], 'c_dt_bias': c_dt_bias[l],
            'c_a_log': c_a_log[l], 'c_d': c_d[l], 'c_norm': c_norm[l],
            'd_lam_re': d_lam_re[l], 'd_lam_im': d_lam_im[l], 'd_log_step': d_log_step[l],
            'd_b': d_b[l], 'd_c': d_c[l], 'd_d': d_d[l], 'd_glu': d_glu[l],
            'w_out': w_out[l], 'w_up': w_up[l], 'ffn_conv_w': ffn_conv_w[l],
            'ffn_conv_b': ffn_conv_b[l], 'w_down': w_down[l],
        }

    y_prompt = x_prompt
    ctx_lists = ([], [], [], [], [], [])
    for l in range(DEPTH):
        mod = adaln(c_ctx[None, :], w_mod[l], b_mod[l])
        y_prompt, new = trunk_layer(y_prompt, mod, layer_params(l), l, None)
        for lst, t in zip(ctx_lists, new):
            lst.append(t)

    y_sample = x_sample
    for l in range(DEPTH):
        mod = adaln(c, w_mod[l], b_mod[l])
        ctx = (cache_a_k[:, l], cache_a_v[:, l], cache_b_k[:, l], cache_b_v[:, l],
               state_ssd[:, l], state_s5[:, l])
        y_sample, _ = trunk_layer(y_sample, mod, layer_params(l), l, ctx)

    new_a_k = jnp.stack(ctx_lists[0], axis=1)
    new_a_v = jnp.stack(ctx_lists[1], axis=1)
    new_b_k = jnp.stack(ctx_lists[2], axis=1)
    new_b_v = jnp.stack(ctx_lists[3], axis=1)
    new_ssd = jnp.stack(ctx_lists[4], axis=1)
    new_s5 = jnp.stack(ctx_lists[5], axis=1)
    return (y_prompt, y_sample, new_a_k, new_a_v, new_b_k, new_b_v, new_ssd, new_s5)
```

```python
import math
from contextlib import ExitStack
import numpy as np
import concourse.bass as bass
import concourse.mybir as mybir
from concourse.bass_utils import run_bass_kernel_spmd

F32 = mybir.dt.float32
BF16 = mybir.dt.bfloat16
I32 = mybir.dt.int32
ALU = mybir.AluOpType
AF = mybir.ActivationFunctionType
AX = mybir.AxisListType

D = 1024
DEPTH = 2
EPS = 1e-6
PI = math.pi


class Buf:
    __slots__ = ("w", "r")

    def __init__(self):
        self.w = None
        self.r = {}


class A:
    def __init__(self, ap, bufs):
        self.ap = ap
        self.bufs = bufs

    def __getitem__(self, k):
        return A(self.ap[k], self.bufs)

    def unsqueeze(self, a):
        return A(self.ap.unsqueeze(a), self.bufs)

    def bc(self, shape):
        return A(self.ap.to_broadcast(list(shape)), self.bufs)

    def rr(self, s, **kw):
        return A(self.ap.rearrange(s, **kw), self.bufs)

    def pb(self, n):
        return A(self.ap.partition_broadcast(n), self.bufs)


class T:
    def __init__(self, tensor, ap=None):
        self.t = tensor
        self.apx = ap
        self.buf = Buf()
        self.keys = {}

    def __getitem__(self, k):
        base = self.apx if self.apx is not None else self.t
        return A(base[k], [self.buf])

    def k(self, key):
        if key not in self.keys:
            v = T(self.t, self.apx)
            self.keys[key] = v
        return self.keys[key]


class Pair:
    def __init__(self, parts):
        self.parts = parts

    def __getitem__(self, key):
        p, ri = key[0], key[1]
        return self.parts[ri][(p,) + tuple(key[2:])]


class Eng:
    def __init__(self, prog, name, obj, is_dma=False, nlanes=8):
        self.e = obj
        self.is_dma = is_dma
        self.seen = {}
        if is_dma:
            self.lanes = [prog.new_sem("%s_l%d" % (name, i)) for i in range(nlanes)]
            self.lane_cnt = [0] * nlanes
            self.next = 0
        else:
            self.sem = prog.new_sem("s_" + name)
            self.n = 0


class Prog:
    def __init__(self, nc, es):
        self.nc = nc
        self.es = es
        self.sems = {}
        self.ninst = 0
        self.engs = {}
        self.engs["pe"] = Eng(self, "pe", nc.tensor)
        self.engs["act"] = Eng(self, "act", nc.scalar)
        self.engs["dve"] = Eng(self, "dve", nc.vector)
        self.engs["pool"] = Eng(self, "pool", nc.gpsimd)
        self.engs["q0"] = Eng(self, "q0", nc.sync, True, 12)
        self.engs["q1"] = Eng(self, "q1", nc.gpsimd, True, 8)
        self.stream = {"pe": "pe", "act": "act", "dve": "dve", "pool": "pool", "q0": "q0", "q1": "pool"}

    def new_sem(self, name):
        self.sems[name] = self.es.enter_context(self.nc.semaphore(name))
        return name

    def _wait(self, st, key, val):
        s = self.engs[st]
        if s.seen.get(key, 0) >= val:
            return
        s.e.wait_ge(self.sems[key], val)
        s.seen[key] = val

    def _deps(self, st, r, w, skip=None):
        deps = {}
        for b in r:
            if b.w is not None:
                deps[b.w[0]] = max(deps.get(b.w[0], 0), b.w[1])
        for b in w:
            if b.w is not None:
                deps[b.w[0]] = max(deps.get(b.w[0], 0), b.w[1])
            for k, v in b.r.items():
                deps[k] = max(deps.get(k, 0), v)
        for k, v in deps.items():
            if k != skip:
                self._wait(st, k, v)

    def _mark(self, key, val, r, w):
        for b in r:
            b.r[key] = max(b.r.get(key, 0), val)
        for b in w:
            b.w = (key, val)
            b.r = {}

    def op(self, eng, fn, r, w):
        e = self.engs[eng]
        self._deps(self.stream[eng], r, w, skip=(e.sem if eng == "pe" else None))
        inst = fn()
        e.n += 1
        inst.then_inc(self.sems[e.sem], 1)
        self._mark(e.sem, e.n, r, w)
        self.ninst += 1

    def dma(self, q, out, in_, **kw):
        e = self.engs[q]
        st = self.stream[q]
        lane = e.next
        e.next = (e.next + 1) % len(e.lanes)
        key = e.lanes[lane]
        if e.lane_cnt[lane] > 0:
            self._wait(st, key, 16 * e.lane_cnt[lane])
        self._deps(st, in_.bufs, out.bufs)
        inst = e.e.dma_start(out=out.ap, in_=in_.ap, **kw)
        e.lane_cnt[lane] += 1
        inst.then_inc(self.sems[key], 16)
        self._mark(key, 16 * e.lane_cnt[lane], in_.bufs, out.bufs)
        self.ninst += 1

    def barrier(self):
        tg = []
        for e in self.engs.values():
            if e.is_dma:
                for i, k in enumerate(e.lanes):
                    if e.lane_cnt[i]:
                        tg.append((k, 16 * e.lane_cnt[i]))
            elif e.n:
                tg.append((e.sem, e.n))
        for st in ("pe", "act", "dve", "pool", "q0"):
            for k, v in tg:
                self._wait(st, k, v)


STUB_C = False
STUB_D = False


def build(NS):
    nc = bass.Bass("TRN2", target_bir_lowering=False)
    TT = 512 + NS
    NBLK = TT // 256
    NCH = TT // 128
    LK = NS + 256
    L = DEPTH
    es = ExitStack()
    P = Prog(nc, es)

    def din(name, shape, dt=F32):
        return T(nc.dram_tensor(name, list(shape), dt, kind="ExternalInput").ap())

    def dout(name, shape, dt=F32):
        return T(nc.dram_tensor(name, list(shape), dt, kind="ExternalOutput").ap())

    def dscr(name, shape, dt=F32):
        return T(nc.dram_tensor(name, list(shape), dt, kind="Internal").ap())

    xin = din("xin", [TT, D])
    cond = din("cond", [2, D])
    w_mod = din("w_mod", [L, D, 6 * D]); b_mod = din("b_mod", [L, 6 * D])
    g_pre1 = din("g_pre1", [L, D]); g_post1 = din("g_post1", [L, D])
    g_pre2 = din("g_pre2", [L, D]); g_post2 = din("g_post2", [L, D])
    w_in = din("w_in", [L, D, 2696])
    a_lam = din("a_lam", [L, 128]); a_subln = din("a_subln", [L, 64])
    b_qnorm = din("b_qnorm", [L, 64]); b_knorm = din("b_knorm", [L, 64])
    c_conv_w = din("c_conv_w", [L, 3, 512]); c_conv_b = din("c_conv_b", [L, 512])
    c_dt_bias = din("c_dt_bias", [L, 8]); c_a_log = din("c_a_log", [L, 8]); c_d = din("c_d", [L, 4])
    c_norm = din("c_norm", [L, 256])
    d_lam_re = din("d_lam_re", [L, 2, 16, 64]); d_lam_im = din("d_lam_im", [L, 2, 16, 64])
    d_log_step = din("d_log_step", [L, 2, 16])
    d_b = din("d_b", [L, 2, 16, 64, 16, 2]); d_c = din("d_c", [L, 2, 256, 128])
    d_d = din("d_d", [L, 256]); d_glu = din("d_glu", [L, 256, 512])
    w_out = din("w_out", [L, D, D]); w_up = din("w_up", [L, D, 5632])
    ffn_conv_w = din("ffn_conv_w", [L, 3, 5632]); ffn_conv_b = din("ffn_conv_b", [L, 5632])
    w_down = din("w_down", [L, 2816, D])
    cak = din("cak", [L, 256, 256]); cav = din("cav", [L, 256, 256])
    cbk = din("cbk", [L, 256, 128]); cbv = din("cbv", [L, 256, 128])
    sssd = din("sssd", [L, 2, 4, 64, 64]); ss5 = din("ss5", [L, 2, 16, 64, 2])
    c_ident = din("c_ident", [128, 128])
    c_misc = din("c_misc", [128, 64])
    c_mats = din("c_mats", [11, 128, 128])
    c_erow = din("c_erow", [2, 128, 129])
    c_m16 = din("c_m16", [128, 8, 128])
    ropeA = din("ropeA", [2, 128, NS]); ropeB = din("ropeB", [2, 128, NS])

    yout = dout("yout", [TT, D])
    o_ak = dout("o_ak", [2, L, 256, 256]); o_av = dout("o_av", [2, L, 256, 256])
    o_bk = dout("o_bk", [2, L, 256, 128]); o_bv = dout("o_bv", [2, L, 256, 128])
    o_ssd = dout("o_ssd", [2, L, 2, 4, 64, 64]); o_s5 = dout("o_s5", [2, L, 2, 16, 64, 2])

    XT = [dscr("XT%d" % i, [D, TT]) for i in range(3)]
    XM = dscr("XM", [D, TT])
    NPJ = 16
    PJ = dscr("PJ", [NPJ * 128, TT])
    VT = dscr("VT", [TT, 384])
    YC = dscr("YC", [D, TT], BF16)

    seqs = [(0, 256, 0, False, 0), (256, 256, 0, False, 1), (512, NS, 1, True, -1)]

    def bufs(*xs):
        out = []
        for x in xs:
            if isinstance(x, A):
                out += x.bufs
        return out

    def apx(x):
        return x.ap if isinstance(x, A) else x

    def mm(out, lhsT, rhs, start=True, stop=True):
        P.op("pe", lambda: nc.tensor.matmul(out.ap, lhsT=lhsT.ap, rhs=rhs.ap, start=start, stop=stop),
             bufs(lhsT, rhs), bufs(out))

    def act(out, in_, func, bias=0.0, scale=1.0):
        P.op("act", lambda: nc.scalar.activation(out=out.ap, in_=in_.ap, func=func, bias=apx(bias), scale=apx(scale)),
             bufs(in_, bias, scale), bufs(out))

    def engobj(e):
        return nc.vector if e == "dve" else nc.gpsimd

    def tt(e, out, in0, in1, op):
        P.op(e, lambda: engobj(e).tensor_tensor(out=out.ap, in0=in0.ap, in1=in1.ap, op=op), bufs(in0, in1), bufs(out))

    def ts(e, out, in0, s1, s2, op0, op1=None):
        if op1 is None:
            P.op(e, lambda: engobj(e).tensor_scalar(out=out.ap, in0=in0.ap, scalar1=apx(s1), scalar2=None, op0=op0),
                 bufs(in0, s1), bufs(out))
        else:
            P.op(e, lambda: engobj(e).tensor_scalar(out=out.ap, in0=in0.ap, scalar1=apx(s1), scalar2=apx(s2), op0=op0, op1=op1),
                 bufs(in0, s1, s2), bufs(out))

    def stt(e, out, in0, sc, in1, op0, op1):
        P.op(e, lambda: engobj(e).scalar_tensor_tensor(out=out.ap, in0=in0.ap, scalar=apx(sc), in1=in1.ap, op0=op0, op1=op1),
             bufs(in0, sc, in1), bufs(out))

    def cp(e, out, in_):
        if e == "act":
            act(out, in_, AF.Copy)
        else:
            P.op(e, lambda: engobj(e).tensor_copy(out=out.ap, in_=in_.ap), bufs(in_), bufs(out))

    def memset(e, out, val):
        P.op(e, lambda: engobj(e).memset(out.ap, val), [], bufs(out))

    def recip(out, in_):
        P.op("dve", lambda: nc.vector.reciprocal(out=out.ap, in_=in_.ap), bufs(in_), bufs(out))

    def dma(out, in_, q="q0", **kw):
        P.dma(q, out, in_, **kw)

    uid = [0]

    def sb(stack, name, shape, dt=F32):
        uid[0] += 1
        return T(stack.enter_context(nc.sbuf_tensor("%s_%d" % (name, uid[0]), list(shape), dt)))

    def pst(stack, name, shape, dt=F32):
        uid[0] += 1
        return T(stack.enter_context(nc.psum_tensor("%s_%d" % (name, uid[0]), list(shape), dt)))

    ident = sb(es, "ident", [128, 128]); dma(ident[:], c_ident[:, :])
    identb = sb(es, "identb", [128, 128], BF16)
    misc = sb(es, "misc", [128, 64]); dma(misc[:], c_misc[:, :])
    cm = sb(es, "cmats", [128, 11, 128]); dma(cm[:], c_mats[:, :, :].rr("m p n -> p m n"))
    cmb = sb(es, "cmatsb", [128, 11, 128], BF16)
    m16 = sb(es, "m16", [128, 8, 128]); dma(m16[:], c_m16[:, :, :])
    cp("dve", identb[:], ident[:])
    cp("dve", cmb[:], cm[:])
    M_ONES1024, M_BLK64, M_ONES256, M_PERMA, M_PERMB, M_TRII, M_TRIE, M_NEGF, M_POSB, M_ONES, M_TRIGE = range(11)
    mod_sb = sb(es, "mod_sb", [128, L, 2, 48])
    gains = sb(es, "gains", [128, L, 4, 8])
    for l in range(L):
        for i, g in enumerate((g_pre1, g_post1, g_pre2, g_post2)):
            dma(gains[:, l, i, :], g[l, :].rr("(k p) -> p k", p=128), allow_slow_non_contiguous=True)

    def stage_mod():
        with ExitStack() as s:
            cT = sb(s, "cT", [128, 8, 2])
            sT = sb(s, "sT", [128, 8, 2])
            for c_ in range(2):
                dma(cT[:, :, c_], cond[c_, :].rr("(k p) -> p k", p=128), allow_slow_non_contiguous=True)
            act(sT[:], cT[:], AF.Silu)
            bm = sb(s, "bm", [128, L, 48])
            for l_ in range(L):
                dma(bm[:, l_, :], b_mod[l_, :].rr("(m p) -> p m", p=128), allow_slow_non_contiguous=True)
            wk = [sb(s, "wmk%d" % i, [128, 8, 1536]) for i in range(2)]
            pm = pst(s, "pm", [128, 48, 2])
            gi_ = 0
            for l in range(L):
                for grp in range(4):
                    w = wk[gi_ % 2]; gi_ += 1
                    for k in range(8):
                        dma(w[:, k, :], w_mod[l, k * 128:(k + 1) * 128, grp * 1536:(grp + 1) * 1536], q=("q0" if k % 2 == 0 else "q1"))
                    for mi in range(12):
                        m = grp * 12 + mi
                        for k in range(8):
                            mm(pm[:, m, :], w[:, k, mi * 128:(mi + 1) * 128], sT[:, k, :], start=(k == 0), stop=(k == 7))
                for c in range(2):
                    tt("dve", mod_sb[:, l, c, :], pm[:, :, c], bm[:, l, :], ALU.add)
                for c in range(2):
                    for (six, gi) in ((1, 0), (4, 2)):
                        ts("dve", mod_sb[:, l, c, six * 8:(six + 1) * 8], mod_sb[:, l, c, six * 8:(six + 1) * 8], 1.0, None, ALU.add)
                        tt("dve", mod_sb[:, l, c, six * 8:(six + 1) * 8], mod_sb[:, l, c, six * 8:(six + 1) * 8], gains[:, l, gi, :], ALU.mult)
                    for (six, gi) in ((2, 1), (5, 3)):
                        tt("dve", mod_sb[:, l, c, six * 8:(six + 1) * 8], mod_sb[:, l, c, six * 8:(six + 1) * 8], gains[:, l, gi, :], ALU.mult)
        P.barrier()

    def stage_t_in():
        with ExitStack() as s:
            xt = [sb(s, "tin%d" % i, [128, D]) for i in range(2)]
            xo = [sb(s, "tio%d" % i, [128, 8, 128]) for i in range(2)]
            pp = [pst(s, "tip%d" % i, [128, 4, 128]) for i in range(2)]
            for c in range(NCH):
                a = xt[c % 2]; o = xo[c % 2]
                dma(a[:], xin[c * 128:(c + 1) * 128, :])
                for hlf in range(2):
                    p_ = pp[hlf]
                    for j in range(4):
                        k = hlf * 4 + j
                        P.op("pe", lambda: nc.tensor.transpose(p_[:, j, :].ap, a[:, k * 128:(k + 1) * 128].ap, ident[:].ap),
                             bufs(a[:], ident[:]), bufs(p_[:]))
                    cp("act" if hlf else "dve", o[:, hlf * 4:(hlf + 1) * 4, :], p_[:])
                dma(XT[0].k(c // 2)[:, c * 128:(c + 1) * 128].rr("(k p) n -> p k n", p=128), o[:], q="q1")
        P.barrier()

    def stage_t_out():
        with ExitStack() as s:
            xi = [sb(s, "toi%d" % i, [128, 8, 128]) for i in range(2)]
            xo = [sb(s, "too%d" % i, [128, D]) for i in range(2)]
            pp = [pst(s, "top%d" % i, [128, 512]) for i in range(2)]
            for c in range(NCH):
                a = xi[c % 2]; o = xo[c % 2]
                dma(a[:], XT[2].k(c // 2)[:, c * 128:(c + 1) * 128].rr("(k p) n -> p k n", p=128))
                for hlf in range(2):
                    p_ = pp[hlf]
                    for j in range(4):
                        k = hlf * 4 + j
                        P.op("pe", lambda: nc.tensor.transpose(p_[:, j * 128:(j + 1) * 128].ap, a[:, k, :].ap, ident[:].ap),
                             bufs(a[:], ident[:]), bufs(p_[:]))
                    cp("act" if hlf else "dve", o[:, hlf * 512:(hlf + 1) * 512], p_[:])
                dma(yout[c * 128:(c + 1) * 128, :], o[:], q="q1")
        P.barrier()

    def rstd_from_ps(s_out, ps_in):
        act(s_out, ps_in, AF.Sqrt, bias=misc[:, 15:16], scale=1.0)
        recip(s_out, s_out)

    def load_weight_bf16(s, name, src_rows, ncols, nk, dst=None, chunk=1024):
        w = dst if dst is not None else sb(s, name, [128, nk, ncols], BF16)
        with ExitStack() as s2:
            stg = [sb(s2, name + "_st%d" % i, [128, chunk]) for i in range(2)]
            i = 0
            for k in range(nk):
                for c0 in range(0, ncols, chunk):
                    c1 = min(ncols, c0 + chunk)
                    st = stg[i % 2]
                    dma(st[:, 0:c1 - c0], src_rows(k)[:, c0:c1], q=("q0" if i % 2 == 0 else "q1"))
                    cp("pool" if i % 2 == 0 else "dve", w[:, k, c0:c1], st[:, 0:c1 - c0])
                    i += 1
            P.barrier()
        return w

    def load_weight_bg(s, s_stage, name, src_rows, ncols, nk, chunk=512):
        w = sb(s, name, [128, nk, ncols], BF16)
        stg = [sb(s_stage, name + "_st%d" % i, [128, chunk]) for i in range(2)]
        i = 0
        for k in range(nk):
            for c0 in range(0, ncols, chunk):
                c1 = min(ncols, c0 + chunk)
                st = stg[i % 2]
                dma(st[:, 0:c1 - c0], src_rows(k)[:, c0:c1], q="q1")
                cp("pool", w[:, k, c0:c1], st[:, 0:c1 - c0])
                i += 1
        return w

    def stage_inproj(l):
        with ExitStack() as s:
            win = load_weight_bf16(s, "win", lambda k: w_in.k(0)[l, k * 128:(k + 1) * 128, :], 2696, 8)
            gkb = sb(s, "gkb", [128, 64])
            dma(gkb[:], b_knorm[l, :].pb(128))
            xT = [sb(s, "ip_x%d" % i, [128, 8, 256]) for i in range(2)]
            sq = sb(s, "ip_sq", [128, 8, 256], BF16)
            rs = sb(s, "ip_rs", [128, 256])
            hn = sb(s, "ip_hn", [128, 8, 256])
            hT = sb(s, "ip_h", [128, 8, 256], BF16)
            pj = [sb(s, "ip_pj%d" % i, [128, NPJ, 256]) for i in range(2)]
            vt = [sb(s, "ip_vt%d" % i, [128, 768]) for i in range(2)]
            kn = sb(s, "ip_kn", [128, 128]); kq = sb(s, "ip_kq", [128, 128]); kr = sb(s, "ip_kr", [128, 2])
            pms = pst(s, "ip_pms", [128, 256])
            pps = [pst(s, "ip_pp%d" % i, [128, 256]) for i in range(5)]
            pvs = [pst(s, "ip_pv%d" % i, [128, 384]) for i in range(2)]
            tile_cols = [(i * 128, 128) for i in range(13)] + [(13 * 128, 8), (13 * 128 + 8, 128), (13 * 128 + 136, 128)]
            VOFF = 13 * 128 + 8 + 256
            cnt = 0
            for b in range(NBLK):
                c0 = b * 256
                ci = 0 if b < 2 else 1
                x = xT[b % 2]
                dma(x[:], XT[l].k(b)[:, c0:c0 + 256].rr("(k p) n -> p k n", p=128))
                act(sq[:], x[:], AF.Square)
                for k in range(8):
                    mm(pms[:], cmb[:, M_ONES1024, :], sq[:, k, :], start=(k == 0), stop=(k == 7))
                rstd_from_ps(rs[:], pms[:])
                tt("dve", hn[:], x[:], rs[:].unsqueeze(1).bc([128, 8, 256]), ALU.mult)
                for k in range(8):
                    act(hT[:, k, :], hn[:, k, :], AF.Identity, bias=mod_sb[:, l, ci, k:k + 1], scale=mod_sb[:, l, ci, 8 + k:9 + k])
                o = pj[b % 2]
                for m, (cc0, cw) in enumerate(tile_cols):
                    pp = pps[cnt % 5]; cnt += 1
                    for k in range(8):
                        mm(pp[0:cw, :], win[:, k, cc0:cc0 + cw], hT[:, k, :], start=(k == 0), stop=(k == 7))
                    cp("act" if m % 2 else "dve", o[0:cw, m, :], pp[0:cw, :])
                dma(PJ.k(b)[:, c0:c0 + 256].rr("(m p) n -> p m n", p=128), o[:], q="q1")
                for t2 in range(2):
                    v = vt[t2]
                    nhalf = 2 if b < 2 else 1
                    for h2 in range(nhalf):
                        pv = pvs[h2]
                        for k in range(8):
                            mm(pv[:], hT[:, k, t2 * 128:(t2 + 1) * 128], win[:, k, VOFF + h2 * 384:VOFF + (h2 + 1) * 384],
                               start=(k == 0), stop=(k == 7))
                        cp("dve" if h2 else "act", v[:, h2 * 384:(h2 + 1) * 384], pv[:])
                    r0 = c0 + t2 * 128
                    dma(VT.k(b)[r0:r0 + 128, :], v[:, 0:384], q="q1")
                    if b < 2:
                        pr = t2 * 128
                        dma(o_av[b, l, pr:pr + 128, :], v[:, 0:256], q="q1")
                        dma(o_bv[b, l, pr:pr + 128, :], v[:, 256:384], q="q1")
                        dma(o_ak[b, l, pr:pr + 128, :], v[:, 384:640], q="q1")
                        tt("dve", kq[:], v[:, 640:768], v[:, 640:768], ALU.mult)
                        P.op("dve", lambda: nc.vector.tensor_reduce(out=kr[:].ap, in_=kq[:].rr("p (h d) -> p h d", h=2).ap,
                                                                    axis=AX.X, op=ALU.add), bufs(kq[:]), bufs(kr[:]))
                        act(kr[:], kr[:], AF.Sqrt, bias=misc[:, 15:16], scale=1.0 / 64)
                        recip(kr[:], kr[:])
                        tt("dve", kn[:].rr("p (h d) -> p h d", h=2), v[:, 640:768].rr("p (h d) -> p h d", h=2),
                           kr[:].unsqueeze(2).bc([128, 2, 64]), ALU.mult)
                        tt("dve", kn[:].rr("p (h d) -> p h d", h=2), kn[:].rr("p (h d) -> p h d", h=2),
                           gkb[:].unsqueeze(1).bc([128, 2, 64]), ALU.mult)
                        dma(o_bk[b, l, pr:pr + 128, :], kn[:], q="q1")
        P.barrier()


    def stage_attn(l):
        lam_init = 0.8 - 0.6 * math.exp(-0.3 * l)
        with ExitStack() as s:
            alb = sb(s, "alb", [128, 128]); dma(alb[:], a_lam[l, :].pb(128))
            al2 = sb(s, "al2", [128, 2, 32]); lamc = sb(s, "lamc", [128, 4])
            tt("dve", al2[:, 0, :], alb[:, 0:32], alb[:, 32:64], ALU.mult)
            tt("dve", al2[:, 1, :], alb[:, 64:96], alb[:, 96:128], ALU.mult)
            P.op("dve", lambda: nc.vector.tensor_reduce(out=lamc[:, 0:2].ap, in_=al2[:].ap, axis=AX.X, op=ALU.add), bufs(al2[:]), bufs(lamc[:]))
            act(lamc[:, 0:2], lamc[:, 0:2], AF.Exp)
            tt("dve", lamc[:, 2:3], lamc[:, 1:2], lamc[:, 0:1], ALU.subtract)
            ts("dve", lamc[:, 3:4], lamc[:, 2:3], -lam_init, None, ALU.add)
            gcol = sb(s, "gcol", [128, 3])
            for hh in range(2):
                dma(gcol[hh * 64:(hh + 1) * 64, 0:1], a_subln[l, :].rr("(d o) -> d o", o=1), allow_slow_non_contiguous=True)
                dma(gcol[hh * 64:(hh + 1) * 64, 1:2], b_qnorm[l, :].rr("(d o) -> d o", o=1), allow_slow_non_contiguous=True)
                dma(gcol[hh * 64:(hh + 1) * 64, 2:3], b_knorm[l, :].rr("(d o) -> d o", o=1), allow_slow_non_contiguous=True)
            ts("dve", gcol[:, 0:1], gcol[:, 0:1], 1.0 - lam_init, None, ALU.mult)
            KT = sb(s, "KT", [128, 3, LK], BF16)
            VP = sb(s, "VP", [128, LK // 128, 8, 128], BF16)
            memset("pool", VP[:], 1.0)
            xk = [sb(s, "at_xk%d" % i, [128, 4, 512]) for i in range(2)]
            rp = [sb(s, "at_rp%d" % i, [128, 4, 512]) for i in range(2)]
            sqb = sb(s, "at_sq", [128, 2, 512], BF16)
            rsb = sb(s, "at_rs", [128, 2, 512])
            tmp = sb(s, "at_tmp", [128, 512])
            vld = [sb(s, "at_v%d" % i, [128, 384]) for i in range(2)]
            ckl = sb(s, "at_ck", [128, 384])
            Qzs = [sb(s, "Qz%d" % i, [128, 12, 512], BF16) for i in range(2)]
            Pb = [sb(s, "at_P%d" % i, [128, 512], BF16) for i in range(3)]
            yab = sb(s, "at_ya", [128, 2, 512]); ybb = sb(s, "at_yb", [128, 2, 512])
            o1 = sb(s, "at_o1", [128, 512]); o2 = sb(s, "at_o2", [128, 512]); rr_ = sb(s, "at_rr", [128, 512])
            yob = sb(s, "at_yo", [128, 4, 512], BF16)
            pS = [pst(s, "at_pS%d" % i, [128, 512]) for i in range(3)]
            pA = [pst(s, "at_pA%d" % i, [128, 512]) for i in range(2)]
            pM = pst(s, "at_pM", [128, 512])

            def headnorm(x2, ntile, gidx, out2, n=256):
                act(sqb[:, 0:ntile, 0:n], x2, AF.Square)
                for t in range(ntile):
                    mm(pM[:, 0:n], cmb[:, M_BLK64, :], sqb[:, t, 0:n])
                    rstd_from_ps(rsb[:, t, 0:n], pM[:, 0:n])
                    stt("dve", out2[:, t, :], x2[:, t, :], gcol[:, gidx:gidx + 1], rsb[:, t, 0:n], ALU.mult, ALU.mult)

            def rope(xa, tab, midx, n=256):
                mm(pM[:, 0:n], cm[:, midx, :], xa)
                tt("dve", tmp[:, 0:n], pM[:, 0:n], tab[:, 1, :], ALU.mult)
                tt("dve", xa, xa, tab[:, 0, :], ALU.mult)
                tt("dve", xa, xa, tmp[:, 0:n], ALU.add)

            for (col0, Ls, ci, is_s, pidx) in seqs:
                nkt = (Ls + (256 if is_s else 0)) // 128
                for bb in range(Ls // 256):
                    b = (col0 + bb * 256) // 256
                    c0 = col0 + bb * 256
                    x = xk[bb % 2]
                    dma(x[:, 0:2, 0:256], PJ.k(b)[2 * 128:4 * 128, c0:c0 + 256].rr("(m p) n -> p m n", p=128))
                    dma(x[:, 2, 0:256], PJ.k(b)[6 * 128:7 * 128, c0:c0 + 256])
                    headnorm(x[:, 2:3, 0:256], 1, 2, x[:, 2:3, 0:256])
                    if is_s:
                        r_ = rp[bb % 2]
                        dma(r_[:, 0:2, 0:256], ropeA[:, :, bb * 256:(bb + 1) * 256].rr("c p n -> p c n"))
                        dma(r_[:, 2:4, 0:256], ropeB[:, :, bb * 256:(bb + 1) * 256].rr("c p n -> p c n"))
                        rope(x[:, 0, 0:256], r_[:, 0:2, 0:256], M_PERMA)
                        rope(x[:, 1, 0:256], r_[:, 0:2, 0:256], M_PERMA)
                        rope(x[:, 2, 0:256], r_[:, 2:4, 0:256], M_PERMB)
                    cp("act", KT[:, :, bb * 256:(bb + 1) * 256], x[:, 0:3, 0:256])
                    for t2 in range(2):
                        kt = bb * 2 + t2
                        v = vld[t2]
                        r0 = c0 + t2 * 128
                        dma(v[:], VT.k(b)[r0:r0 + 128, :])
                        for h in range(4):
                            o_ = 0 if h % 2 == 0 else 64
                            cp("pool", VP[:, kt, h, o_:o_ + 64], v[:, h * 64:(h + 1) * 64])
                        for g in range(2):
                            for par in range(2):
                                cp("pool", VP[:, kt, 4 + g * 2 + par, par * 64:par * 64 + 64], v[:, 256 + g * 64:256 + (g + 1) * 64])
                if is_s:
                    for t2 in range(2):
                        kt = Ls // 128 + t2
                        dma(ckl[:, 0:256], cak[l, t2 * 128:(t2 + 1) * 128, :])
                        dma(ckl[:, 256:384], cbk[l, t2 * 128:(t2 + 1) * 128, :])
                        for m in range(3):
                            P.op("pe", lambda: nc.tensor.transpose(pM[:, 0:128].ap, ckl[:, m * 128:(m + 1) * 128].ap, ident[:].ap),
                                 bufs(ckl[:], ident[:]), bufs(pM[:]))
                            cp("act", KT[:, m, Ls + t2 * 128:Ls + (t2 + 1) * 128], pM[:, 0:128])
                        v = vld[t2]
                        dma(v[:, 0:256], cav[l, t2 * 128:(t2 + 1) * 128, :])
                        dma(v[:, 256:384], cbv[l, t2 * 128:(t2 + 1) * 128, :])
                        for h in range(4):
                            o_ = 0 if h % 2 == 0 else 64
                            cp("pool", VP[:, kt, h, o_:o_ + 64], v[:, h * 64:(h + 1) * 64])
                        for g in range(2):
                            for par in range(2):
                                cp("pool", VP[:, kt, 4 + g * 2 + par, par * 64:par * 64 + 64], v[:, 256 + g * 64:256 + (g + 1) * 64])
                QB = 512 if is_s else 256
                cS = 0; cA = 0; cP = 0

                def prep_q(bb):
                    c0 = col0 + bb * QB
                    x = xk[bb % 2]
                    Qz = Qzs[bb % 2]
                    for h_ in range(QB // 256):
                        b = (c0 + h_ * 256) // 256
                        hs = slice(h_ * 256, (h_ + 1) * 256)
                        dma(x[:, 0:2, hs], PJ.k(b)[0:256, c0 + h_ * 256:c0 + (h_ + 1) * 256].rr("(m p) n -> p m n", p=128))
                        dma(x[:, 2:4, hs], PJ.k(b)[4 * 128:6 * 128, c0 + h_ * 256:c0 + (h_ + 1) * 256].rr("(m p) n -> p m n", p=128))
                    headnorm(x[:, 2:4, 0:QB], 2, 1, x[:, 2:4, 0:QB], n=QB)
                    if is_s:
                        r_ = rp[bb % 2]
                        dma(r_[:, 0:2, 0:QB], ropeA[:, :, bb * QB:(bb + 1) * QB].rr("c p n -> p c n"))
                        dma(r_[:, 2:4, 0:QB], ropeB[:, :, bb * QB:(bb + 1) * QB].rr("c p n -> p c n"))
                        rope(x[:, 0, 0:QB], r_[:, 0:2, 0:QB], M_PERMA, n=QB)
                        rope(x[:, 1, 0:QB], r_[:, 0:2, 0:QB], M_PERMA, n=QB)
                        rope(x[:, 2, 0:QB], r_[:, 2:4, 0:QB], M_PERMB, n=QB)
                        rope(x[:, 3, 0:QB], r_[:, 2:4, 0:QB], M_PERMB, n=QB)
                    jobs = []
                    for t in range(2):
                        for hl in range(2):
                            for half in range(2):
                                j = len(jobs)
                                ts("dve", Qz[:, j, 0:QB], x[:, t, 0:QB], misc[:, 2 + hl * 2 + half:3 + hl * 2 + half], None, ALU.mult)
                                jobs.append((j, t, t * 2 + hl, 32 ** -0.5))
                    for qh in range(4):
                        j = len(jobs)
                        t = qh % 2; g = qh // 2
                        ts("dve", Qz[:, j, 0:QB], x[:, 2 + t, 0:QB], misc[:, 10 + g:11 + g], None, ALU.mult)
                        jobs.append((j, 2, 4 + g * 2 + (qh % 2), 64 ** -0.5))
                    return jobs

                nQ = Ls // QB
                jobs_next = prep_q(0)
                for bb in range(nQ):
                    c0 = col0 + bb * QB
                    Qz = Qzs[bb % 2]
                    jobs = jobs_next
                    if bb + 1 < nQ:
                        jobs_next = prep_q(bb + 1)
                    steps = [(jb, kt) for jb in jobs for kt in range(nkt)]
                    Sbuf = {}
                    AHEAD = 2

                    def issue_S(i):
                        (j, ktile, var, scl), kt = steps[i]
                        S = pS[i % 3]
                        mm(S[:, 0:QB], KT[:, ktile, kt * 128:(kt + 1) * 128], Qz[:, j, 0:QB])

                    for i in range(min(AHEAD, len(steps))):
                        issue_S(i)
                    acc = None
                    for i, ((j, ktile, var, scl), kt) in enumerate(steps):
                        if i + AHEAD < len(steps):
                            issue_S(i + AHEAD)
                        if kt == 0:
                            acc = pA[cA % 2]; cA += 1
                        S = pS[i % 3]
                        pb_ = Pb[i % 3]
                        act(pb_[:, 0:QB], S[:, 0:QB], AF.Exp, scale=scl)
                        mm(acc[:, 0:QB], VP[:, kt, var, :], pb_[:, 0:QB], start=(kt == 0), stop=(kt == nkt - 1))
                        if kt != nkt - 1:
                            continue
                        par = (var % 2) if var < 4 else ((var - 4) % 2)
                        nr = slice(64 * par, 64 * par + 64); sr = slice(64 - 64 * par, 128 - 64 * par)
                        recip(rr_[nr, 0:QB], acc[sr, 0:QB])
                        if j < 8:
                            half = j % 2
                            dst = o1 if half == 0 else o2
                            tt("dve", dst[nr, 0:QB], acc[nr, 0:QB], rr_[nr, 0:QB], ALU.mult)
                            if half == 1:
                                t = j // 4
                                stt("dve", yab[nr, t, 0:QB], o2[nr, 0:QB], lamc[nr, 3:4], o1[nr, 0:QB], ALU.mult, ALU.add)
                        else:
                            qh = j - 8
                            tt("dve", ybb[nr, qh // 2, 0:QB], acc[nr, 0:QB], rr_[nr, 0:QB], ALU.mult)
                    headnorm(yab[:, :, 0:QB], 2, 0, yab[:, :, 0:QB], n=QB)
                    cp("act", yob[:, 0:2, 0:QB], yab[:, :, 0:QB])
                    cp("act", yob[:, 2:4, 0:QB], ybb[:, :, 0:QB])
                    for h_ in range(QB // 256):
                        b = (c0 + h_ * 256) // 256
                        dma(YC.k(("a", b))[0:512, c0 + h_ * 256:c0 + (h_ + 1) * 256].rr("(m p) n -> p m n", p=128),
                            yob[:, :, h_ * 256:(h_ + 1) * 256], q="q1")
        P.barrier()

    def stage_outproj(l):
        with ExitStack() as s:
            wo = load_weight_bf16(s, "wo", lambda k: w_out.k(0)[l, k * 128:(k + 1) * 128, :], D, 8)
            yc = [sb(s, "op_yc%d" % i, [128, 8, 256], BF16) for i in range(2)]
            xo = [sb(s, "op_x%d" % i, [128, 8, 256]) for i in range(2)]
            ys_ = [sb(s, "op_y%d" % i, [128, 8, 256]) for i in range(2)]; sqs_ = [sb(s, "op_sq%d" % i, [128, 8, 256], BF16) for i in range(2)]; rss_ = [sb(s, "op_rs%d" % i, [128, 256]) for i in range(2)]
            pps = [pst(s, "op_pp%d" % i, [128, 256]) for i in range(3)]
            pmss = [pst(s, "op_pms%d" % i, [128, 256]) for i in range(2)]
            cnt = 0
            for b in range(NBLK):
                c0 = b * 256; ci = 0 if b < 2 else 1
                a = yc[b % 2]; x = xo[b % 2]; y = ys_[b % 2]; sq = sqs_[b % 2]; rs = rss_[b % 2]; pms = pmss[b % 2]
                dma(a[:, 0:4, :], YC.k(("a", b))[0:512, c0:c0 + 256].rr("(m p) n -> p m n", p=128))
                dma(a[:, 4:6, :], YC.k(("c", b))[512:768, c0:c0 + 256].rr("(m p) n -> p m n", p=128))
                dma(a[:, 6:8, :], YC.k(("d", b))[768:1024, c0:c0 + 256].rr("(m p) n -> p m n", p=128))
                dma(x[:], XT[l].k(b)[:, c0:c0 + 256].rr("(k p) n -> p k n", p=128))
                for m in range(8):
                    pp = pps[cnt % 3]; cnt += 1
                    for k in range(8):
                        mm(pp[:], wo[:, k, m * 128:(m + 1) * 128], a[:, k, :], start=(k == 0), stop=(k == 7))
                    cp("act" if m % 2 else "dve", y[:, m, :], pp[:])
                act(sq[:], y[:], AF.Square)
                for k in range(8):
                    mm(pms[:], cmb[:, M_ONES1024, :], sq[:, k, :], start=(k == 0), stop=(k == 7))
                rstd_from_ps(rs[:], pms[:])
                tt("dve", y[:], y[:], rs[:].unsqueeze(1).bc([128, 8, 256]), ALU.mult)
                for k in range(8):
                    stt("dve", x[:, k, :], y[:, k, :], mod_sb[:, l, ci, 16 + k:17 + k], x[:, k, :], ALU.mult, ALU.add)
                dma(XM.k(b)[:, c0:c0 + 256].rr("(k p) n -> p k n", p=128), x[:], q="q1")
        P.barrier()

    def stage_ffn(l, wu):
        with ExitStack() as s:
            wd = load_weight_bf16(s, "wd", lambda k: w_down.k(0)[l, k * 128:(k + 1) * 128, :], D, 22)
            cw = sb(s, "ff_cw", [128, 3, 44]); cb = sb(s, "ff_cb", [128, 44])
            for w_ in range(3):
                dma(cw[:, w_, :], ffn_conv_w[l, w_, :].rr("(m p) -> p m", p=128), allow_slow_non_contiguous=True)
            dma(cb[:], ffn_conv_b[l, :].rr("(m p) -> p m", p=128), allow_slow_non_contiguous=True)
            xh = [sb(s, "ff_x%d" % i, [128, 8, 258]) for i in range(2)]
            sq = sb(s, "ff_sq", [128, 8, 258], BF16); rs = sb(s, "ff_rs", [128, 258])
            hn = sb(s, "ff_hn", [128, 8, 258]); h2 = sb(s, "ff_h2", [128, 8, 258], BF16)
            ug = [sb(s, "ff_ug%d" % i, [128, 258]) for i in range(2)]
            uv = [sb(s, "ff_uv%d" % i, [128, 258]) for i in range(2)]
            cg = sb(s, "ff_cg", [128, 256]); cv = sb(s, "ff_cv", [128, 256]); sg = sb(s, "ff_sg", [128, 256])
            aT = sb(s, "ff_a", [128, 22, 256], BF16)
            y = sb(s, "ff_y", [128, 8, 256])
            pms = pst(s, "ff_pms", [128, 258])
            pps = [pst(s, "ff_pp%d" % i, [128, 258]) for i in range(7)]
            cnt = 0
            for (col0, Ls, ci, is_s, pidx) in seqs:
                for bb in range(Ls // 256):
                    b = (col0 + bb * 256) // 256
                    c0 = col0 + bb * 256
                    x = xh[b % 2]
                    first = bb == 0; last = bb == Ls // 256 - 1
                    lo = 1 if first else 0; hi = 257 if last else 258
                    if first:
                        memset("pool", x[:, :, 0:1], 0.0)
                    if last:
                        memset("pool", x[:, :, 257:258], 0.0)
                    srcT = XM.k(b) if (first and last) else XM.k("all")
                    for bq in range(max(0, b - 1), min(NBLK, b + 2)):
                        pass
                    a_in = A(XM.t[:, c0 - 1 + lo:c0 - 1 + hi].rearrange("(k p) n -> p k n", p=128),
                             [XM.k(b).buf] + ([XM.k(b - 1).buf] if not first else []) + ([XM.k(b + 1).buf] if not last else []))
                    dma(x[:, :, lo:hi], a_in)
                    act(sq[:], x[:], AF.Square)
                    for k in range(8):
                        mm(pms[:], cmb[:, M_ONES1024, :], sq[:, k, :], start=(k == 0), stop=(k == 7))
                    rstd_from_ps(rs[:], pms[:])
                    tt("dve", hn[:], x[:], rs[:].unsqueeze(1).bc([128, 8, 258]), ALU.mult)
                    for k in range(8):
                        act(h2[:, k, :], hn[:, k, :], AF.Identity, bias=mod_sb[:, l, ci, 24 + k:25 + k], scale=mod_sb[:, l, ci, 32 + k:33 + k])
                    if first:
                        memset("pool", h2[:, :, 0:1], 0.0)
                    if last:
                        memset("pool", h2[:, :, 257:258], 0.0)
                    for m in range(22):
                        for (which, mt, ubuf, cdst) in ((0, m, ug[m % 2], cg), (1, 22 + m, uv[m % 2], cv)):
                            pp = pps[cnt % 7]; cnt += 1
                            for k in range(8):
                                mm(pp[:], wu[:, k, mt * 128:(mt + 1) * 128], h2[:, k, :], start=(k == 0), stop=(k == 7))
                            cp("act", ubuf[:], pp[:])
                            e_ = "dve"
                            ts(e_, cdst[:], ubuf[:, 0:256], cw[:, 0, mt:mt + 1], cb[:, mt:mt + 1], ALU.mult, ALU.add)
                            stt(e_, cdst[:], ubuf[:, 1:257], cw[:, 1, mt:mt + 1], cdst[:], ALU.mult, ALU.add)
                            stt(e_, cdst[:], ubuf[:, 2:258], cw[:, 2, mt:mt + 1], cdst[:], ALU.mult, ALU.add)
                        act(sg[:], cg[:], AF.Silu)
                        tt("dve", aT[:, m, :], sg[:], cv[:], ALU.mult)
                    for mo in range(8):
                        pp = pps[cnt % 7]; cnt += 1
                        for k in range(22):
                            mm(pp[:, 0:256], wd[:, k, mo * 128:(mo + 1) * 128], aT[:, k, :], start=(k == 0), stop=(k == 21))
                        cp("act" if mo % 2 else "dve", y[:, mo, :], pp[:, 0:256])
                    act(sq[:, :, 0:256], y[:], AF.Square)
                    for k in range(8):
                        mm(pms[:, 0:256], cmb[:, M_ONES1024, :], sq[:, k, 0:256], start=(k == 0), stop=(k == 7))
                    rstd_from_ps(rs[:, 0:256], pms[:, 0:256])
                    tt("dve", y[:], y[:], rs[:, 0:256].unsqueeze(1).bc([128, 8, 256]), ALU.mult)
                    for k in range(8):
                        stt("dve", y[:, k, :], y[:, k, :], mod_sb[:, l, ci, 40 + k:41 + k], x[:, k, 1:257], ALU.mult, ALU.add)
                    dma(XT[l + 1].k(b)[:, c0:c0 + 256].rr("(k p) n -> p k n", p=128), y[:], q="q1")
        P.barrier()


    def stage_ssd(l):
        with ExitStack() as s:
            cw = sb(s, "sd_cw", [128, 3, 4]); cb = sb(s, "sd_cb", [128, 4])
            for w_ in range(3):
                dma(cw[:, w_, :], c_conv_w[l, w_, :].rr("(m p) -> p m", p=128), allow_slow_non_contiguous=True)
            dma(cb[:], c_conv_b[l, :].rr("(m p) -> p m", p=128), allow_slow_non_contiguous=True)
            dtb = sb(s, "sd_dtb", [8, 1]); dma(dtb[:], c_dt_bias[l, :].rr("(d o) -> d o", o=1), allow_slow_non_contiguous=True)
            An = sb(s, "sd_An", [128, 8]); dma(An[:], c_a_log[l, :].pb(128))
            act(An[:], An[:], AF.Exp)
            ts("dve", An[:], An[:], -1.0, None, ALU.mult)
            Dsk = sb(s, "sd_D", [128, 4]); dma(Dsk[:], c_d[l, :].pb(128))
            cn = sb(s, "sd_cn", [128, 2]); dma(cn[:], c_norm[l, :].rr("(m p) -> p m", p=128), allow_slow_non_contiguous=True)
            onesf = sb(s, "sd_1", [128, 128]); memset("pool", onesf[:], 1.0)
            HT = [sb(s, "sd_HT%d" % i, [128, 2, 64]) for i in range(2)]
            HTb = [sb(s, "sd_HTb%d" % i, [128, 2, 64], BF16) for i in range(2)]
            ysum = sb(s, "sd_ys", [128, NS // 128, 256])
            u = sb(s, "sd_u", [128, 4, 130]); xc = sb(s, "sd_xc", [128, 4, 128])
            zT = sb(s, "sd_z", [128, 2, 128]); dT = sb(s, "sd_dT", [8, 128])
            Xt = sb(s, "sd_Xt", [128, 256]); Btb = sb(s, "sd_Btb", [128, 128], BF16); dtt = sb(s, "sd_dtt", [128, 8])
            at = sb(s, "sd_at", [128, 8]); cums = sb(s, "sd_cums", [128, 3, 8]); nac = sb(s, "sd_nac", [128, 8])
            ecol = sb(s, "sd_e", [128, 8]); dcol = sb(s, "sd_d", [128, 8]); cdec = sb(s, "sd_cd", [128, 8])
            BCb = sb(s, "sd_BCb", [128, 2, 128], BF16)
            BCm = sb(s, "sd_BCm", [128, 2, 2, 128], BF16)
            CBs = sb(s, "sd_CB", [128, 2, 128])
            abc4 = sb(s, "sd_abc", [128, 4, 128]); Dm4 = sb(s, "sd_Dm", [128, 4, 128]); Gb4 = sb(s, "sd_Gb", [128, 4, 128], BF16)
            Xdt4 = sb(s, "sd_Xdt", [128, 4, 64], BF16); Xw4 = sb(s, "sd_Xw", [128, 4, 64], BF16); tmpy4 = sb(s, "sd_ty", [128, 4, 64])
            gz = sb(s, "sd_gz", [128, 2, 128]); sqz = sb(s, "sd_sq", [128, 2, 128], BF16); rz = sb(s, "sd_rz", [128, 128])
            yo = sb(s, "sd_yo", [128, 2, 128], BF16); hfo = sb(s, "sd_hf", [64, 64])
            pT = pst(s, "sd_pT", [128, 512]); pC = pst(s, "sd_pC", [128, 3, 8]); pCB = pst(s, "sd_pCB", [128, 2, 128])
            pD4 = pst(s, "sd_pD", [128, 4, 128]); pY4 = pst(s, "sd_pY", [128, 4, 128]); pS4 = pst(s, "sd_pS4", [128, 4, 64])
            pF = pst(s, "sd_pF", [128, 2, 128]); pM = pst(s, "sd_pM", [128, 128])
            PJa = PJ.k("all")

            def tr(out, in_, idn):
                P.op("pe", lambda: nc.tensor.transpose(out.ap, in_.ap, idn.ap), bufs(in_, idn), bufs(out))

            for (col0, Ls, ci, is_s, pidx) in seqs:
                nC = Ls // 128
                for dr in range(2):
                    if is_s:
                        for h in range(4):
                            g = h // 2; hh = h % 2
                            dma(hfo[:], sssd[l, dr, h, :, :])
                            tr(pS4[0:64, 0, :], hfo[:], ident[0:64, 0:64])
                            cp("dve", HT[dr][64 * g:64 * g + 64, hh, :], pS4[0:64, 0, :])
                    else:
                        memset("pool", HT[dr][:], 0.0)
                    cp("act", HTb[dr][:], HT[dr][:])

                def prep(c):
                    g0 = col0 + c * 128
                    lo = 1 if c == 0 else 0
                    hi = 129 if c == nC - 1 else 130
                    if c == 0:
                        memset("pool", u[:, :, 0:1], 0.0)
                    if c == nC - 1:
                        memset("pool", u[:, :, 129:130], 0.0)
                    dma(u[:, :, lo:hi], PJa[9 * 128:13 * 128, g0 - 1 + lo:g0 - 1 + hi].rr("(m p) n -> p m n", p=128))
                    dma(zT[:], PJa[7 * 128:9 * 128, g0:g0 + 128].rr("(m p) n -> p m n", p=128))
                    dma(dT[:], PJa[13 * 128:13 * 128 + 8, g0:g0 + 128])
                    for m in range(4):
                        ts("dve", xc[:, m, :], u[:, m, 0:128], cw[:, 0, m:m + 1], cb[:, m:m + 1], ALU.mult, ALU.add)
                        stt("dve", xc[:, m, :], u[:, m, 1:129], cw[:, 1, m:m + 1], xc[:, m, :], ALU.mult, ALU.add)
                        stt("dve", xc[:, m, :], u[:, m, 2:130], cw[:, 2, m:m + 1], xc[:, m, :], ALU.mult, ALU.add)
                    act(xc[:], xc[:], AF.Silu)
                    act(dT[:], dT[:], AF.Exp, bias=dtb[:, 0:1], scale=1.0)
                    act(dT[:], dT[:], AF.Ln, bias=1.0, scale=1.0)
                    for m in range(3):
                        tr(pT[:, m * 128:(m + 1) * 128], xc[:, m, :], ident[:])
                    tr(pT[:, 384:392], dT[:], ident[0:8, 0:8])
                    cp("act", Xt[:], pT[:, 0:256])
                    cp("act", Btb[:], pT[:, 256:384])
                    cp("dve", dtt[:], pT[:, 384:392])
                    tt("dve", at[:], dtt[:], An[:], ALU.mult)
                    mm(pC[:, 0, :], cm[:, M_TRII, :], at[:])
                    mm(pC[:, 1, :], cm[:, M_TRIE, :], at[:])
                    mm(pC[:, 2, :], cm[:, M_ONES, :], at[:])
                    cp("dve", cums[:], pC[:])
                    ts("dve", nac[:], cums[:, 0, :], -1.0, None, ALU.mult)
                    act(ecol[:, 0:4], cums[:, 0, 0:4], AF.Exp)
                    tt("dve", ecol[:, 4:8], cums[:, 2, 4:8], cums[:, 1, 4:8], ALU.subtract)
                    act(ecol[:, 4:8], ecol[:, 4:8], AF.Exp)
                    tt("dve", dcol[:, 0:4], cums[:, 2, 0:4], cums[:, 0, 0:4], ALU.subtract)
                    act(dcol[:, 0:4], dcol[:, 0:4], AF.Exp)
                    act(dcol[:, 4:8], cums[:, 1, 4:8], AF.Exp)
                    act(cdec[:], cums[:, 2, :], AF.Exp)
                    cp("act", BCb[:], xc[:, 2:4, :])
                    for g in range(2):
                        ts("dve", BCm[:, 0, g, :], xc[:, 2, :], misc[:, 10 + g:11 + g], None, ALU.mult)
                        ts("dve", BCm[:, 1, g, :], xc[:, 3, :], misc[:, 10 + g:11 + g], None, ALU.mult)
                    for g in range(2):
                        mm(pCB[:, g, :], BCm[:, 0, g, :], BCb[:, 1, :])
                    cp("dve", CBs[:], pCB[:])

                def heads(c, dr, first_pass):
                    H4 = range(4)
                    gs_ = [slice(64 * (h // 2), 64 * (h // 2) + 64) for h in H4]
                    cols = [dr * 4 + h for h in H4]
                    for h in H4:
                        act(abc4[:, h, :], onesf[:], AF.Identity, scale=at[:, cols[h]:cols[h] + 1])
                    for h in H4:
                        if dr == 0:
                            mm(pD4[:, h, :], abc4[:, h, :], cm[:, M_TRII, :], start=True, stop=False)
                            mm(pD4[:, h, :], ident[:], cm[:, M_NEGF, :], start=False, stop=True)
                        else:
                            mm(pD4[:, h, :], abc4[:, h, :], cm[:, M_TRIE, :], start=True, stop=False)
                            mm(pD4[:, h, :], ident[:], cm[:, M_POSB, :], start=False, stop=True)
                    for h in H4:
                        if dr == 0:
                            act(Dm4[:, h, :], pD4[:, h, :], AF.Exp, bias=nac[:, cols[h]:cols[h] + 1], scale=1.0)
                        else:
                            act(Dm4[:, h, :], pD4[:, h, :], AF.Exp, bias=cums[:, 1, cols[h]:cols[h] + 1], scale=-1.0)
                    for h in H4:
                        tt("dve", Gb4[:, h, :], CBs[:, h // 2, :], Dm4[:, h, :], ALU.mult)
                        ts("dve", Xdt4[:, h, :], Xt[:, h * 64:(h + 1) * 64], dtt[:, cols[h]:cols[h] + 1], None, ALU.mult)
                    for h in H4:
                        mm(pY4[:, h, 0:64], Gb4[:, h, :], Xdt4[:, h, :])
                        mm(pY4[:, h, 64:128], BCm[:, 1, h // 2, :], HTb[dr][:, h % 2, :])
                    for h in H4:
                        act(tmpy4[:, h, :], pY4[:, h, 64:128], AF.Identity, scale=ecol[:, cols[h]:cols[h] + 1])
                    for h in H4:
                        ysl = ysum[:, c, h * 64:(h + 1) * 64]
                        if first_pass:
                            tt("dve", ysl, tmpy4[:, h, :], pY4[:, h, 0:64], ALU.add)
                            stt("dve", ysl, Xt[:, h * 64:(h + 1) * 64], Dsk[:, h:h + 1], ysl, ALU.mult, ALU.add)
                        else:
                            tt("dve", ysl, ysl, tmpy4[:, h, :], ALU.add)
                            tt("dve", ysl, ysl, pY4[:, h, 0:64], ALU.add)
                    for h in H4:
                        ts("dve", Xw4[:, h, :], Xdt4[:, h, :], dcol[:, cols[h]:cols[h] + 1], None, ALU.mult)
                    for h in H4:
                        mm(pS4[:, h, :], Btb[:], Xw4[:, h, :])
                    for h in H4:
                        stt("dve", HT[dr][gs_[h], h % 2, :], HT[dr][gs_[h], h % 2, :], cdec[gs_[h], cols[h]:cols[h] + 1], pS4[gs_[h], h, :], ALU.mult, ALU.add)
                    for h in H4:
                        cp("act", HTb[dr][gs_[h], h % 2, :], HT[dr][gs_[h], h % 2, :])

                for c in range(nC):
                    prep(c)
                    heads(c, 0, True)
                for c in range(nC - 1, -1, -1):
                    prep(c)
                    heads(c, 1, False)
                    g0 = col0 + c * 128
                    b = g0 // 256
                    for m in range(2):
                        tr(pF[:, m, :], ysum[:, c, m * 128:(m + 1) * 128], ident[:])
                    act(gz[:], zT[:], AF.Silu)
                    tt("dve", gz[:], gz[:], pF[:], ALU.mult)
                    act(sqz[:], gz[:], AF.Square)
                    for m in range(2):
                        mm(pM[:], cmb[:, M_ONES256, :], sqz[:, m, :], start=(m == 0), stop=(m == 1))
                    rstd_from_ps(rz[:], pM[:])
                    for m in range(2):
                        stt("dve", yo[:, m, :], gz[:, m, :], cn[:, m:m + 1], rz[:], ALU.mult, ALU.mult)
                    dma(YC.k(("c", b))[512:768, g0:g0 + 128].rr("(m p) n -> p m n", p=128), yo[:], q="q1")
                if not is_s:
                    for dr in range(2):
                        for h in range(4):
                            g = h // 2; hh = h % 2
                            gs = slice(64 * g, 64 * g + 64)
                            tr(pM[0:64, :], HT[dr][:, hh, :], ident[:])
                            cp("dve", hfo[:], pM[0:64, 64 * g:64 * g + 64])
                            dma(o_ssd[pidx, l, dr, h, :, :], hfo[:], q="q1")
        P.barrier()

    def stage_s5(l):
        TWO_PI = 2.0 * PI
        with ExitStack() as s:
            W1 = [Pair([sb(s, "s5_W1_%d%d" % (d, ri), [128, 1024]) for ri in range(2)]) for d in range(2)]
            W2 = [Pair([sb(s, "s5_W2_%d%d" % (d, ri), [128, 8, 136]) for ri in range(2)]) for d in range(2)]
            BD = [[[sb(s, "s5_BD%d%d%d" % (d, ct, ri), [128, 512], BF16) for ri in range(2)] for ct in range(2)] for d in range(2)]
            Cm = [[sb(s, "s5_Cm%d%d" % (d, ri), [128, 8, 128], BF16) for ri in range(2)] for d in range(2)]
            ddc = sb(s, "s5_dd", [128, 2]); dma(ddc[:], d_d[l, :].rr("(m p) -> p m", p=128), allow_slow_non_contiguous=True)
            wg = load_weight_bf16(s, "s5_wg", lambda k: d_glu.k(0)[l, k * 128:(k + 1) * 128, :], 512, 2, chunk=512)

            def tr(out, in_, idn):
                P.op("pe", lambda: nc.tensor.transpose(out.ap, in_.ap, idn.ap), bufs(in_, idn), bufs(out))

            with ExitStack() as s2:
                W = 1032
                tf = sb(s2, "s5_tf", [128, W]); ti = sb(s2, "s5_ti", [128, W], I32); tm = sb(s2, "s5_tm", [128, W])
                rr_ = sb(s2, "s5_r", [128, W]); ph = sb(s2, "s5_ph", [128, W]); mg = sb(s2, "s5_mg", [128, W])
                sn = sb(s2, "s5_sn", [128, W]); cs = sb(s2, "s5_cs", [128, W])
                lre = sb(s2, "s5_lre", [128, 1024]); lim = sb(s2, "s5_lim", [128, 1024]); dl = sb(s2, "s5_dl", [128, 16])
                lrc = sb(s2, "s5_lrc", [128, 8]); lic = sb(s2, "s5_lic", [128, 8]); dlc = sb(s2, "s5_dlc", [128, 8])
                er = sb(s2, "s5_er", [128, 2, 129])
                for d in range(2):
                    dma(er[:, d, :], c_erow[d, :, :])
                Bn = sb(s2, "s5_Bn", [64, 16, 32]); Cn = sb(s2, "s5_Cn", [128, 2, 128])
                lr_r = sb(s2, "s5_lrr", [128, 64]); li_r = sb(s2, "s5_lir", [128, 64]); dl_r = sb(s2, "s5_dlr", [128, 1])
                q = [sb(s2, "s5_q%d" % i, [128, 64]) for i in range(10)]
                tmpC = sb(s2, "s5_tmpC", [128, 128])
                pTr = pst(s2, "s5_pTr", [128, 128])

                def trig(n):
                    for dst, off in ((sn, 0.0), (cs, PI / 2)):
                        ts("dve", rr_[:, 0:n], ph[:, 0:n], off, None, ALU.add)
                        ts("dve", tf[:, 0:n], rr_[:, 0:n], 1.0 / TWO_PI, None, ALU.mult)
                        cp("dve", ti[:, 0:n], tf[:, 0:n])
                        cp("dve", tf[:, 0:n], ti[:, 0:n])
                        stt("dve", rr_[:, 0:n], tf[:, 0:n], -TWO_PI, rr_[:, 0:n], ALU.mult, ALU.add)
                        ts("dve", tm[:, 0:n], rr_[:, 0:n], PI, -TWO_PI, ALU.is_gt, ALU.mult)
                        tt("dve", rr_[:, 0:n], rr_[:, 0:n], tm[:, 0:n], ALU.add)
                        ts("dve", tm[:, 0:n], rr_[:, 0:n], -PI, TWO_PI, ALU.is_lt, ALU.mult)
                        tt("dve", rr_[:, 0:n], rr_[:, 0:n], tm[:, 0:n], ALU.add)
                        act(dst[:, 0:n], rr_[:, 0:n], AF.Sin)

                for d in range(2):
                    dma(lre[:], d_lam_re[l, d, :, :].rr("g n -> (g n)").pb(128))
                    dma(lim[:], d_lam_im[l, d, :, :].rr("g n -> (g n)").pb(128))
                    dma(dl[:], d_log_step[l, d, :].pb(128))
                    act(dl[:], dl[:], AF.Exp)
                    dlb = dl[:].unsqueeze(2).bc([128, 16, 64])
                    tt("dve", lre[:].rr("p (g n) -> p g n", g=16), lre[:].rr("p (g n) -> p g n", g=16), dlb, ALU.mult)
                    tt("dve", lim[:].rr("p (g n) -> p g n", g=16), lim[:].rr("p (g n) -> p g n", g=16), dlb, ALU.mult)
                    ts("dve", ph[:, 0:1024], lim[:], misc[:, d:d + 1], None, ALU.mult)
                    act(mg[:, 0:1024], lre[:], AF.Exp, scale=misc[:, 16 + d:17 + d])
                    trig(1024)
                    tt("dve", W1[d][:, 0, :], mg[:, 0:1024], cs[:, 0:1024], ALU.mult)
                    stt("dve", W1[d][:, 1, :], mg[:, 0:1024], -1.0, sn[:, 0:1024], ALU.mult, ALU.mult)
                    for t in range(8):
                        dma(lrc[:, t:t + 1], d_lam_re[l, d, 2 * t:2 * t + 2, :].rr("g n -> (g n)").rr("(p o) -> p o", o=1), allow_slow_non_contiguous=True)
                        dma(lic[:, t:t + 1], d_lam_im[l, d, 2 * t:2 * t + 2, :].rr("g n -> (g n)").rr("(p o) -> p o", o=1), allow_slow_non_contiguous=True)
                        for g2 in range(2):
                            dma(dlc[64 * g2:64 * g2 + 64, t:t + 1], d_log_step[l, d, 2 * t + g2:2 * t + g2 + 1].pb(64))
                    act(dlc[:], dlc[:], AF.Exp)
                    tt("dve", lrc[:], lrc[:], dlc[:], ALU.mult)
                    tt("dve", lic[:], lic[:], dlc[:], ALU.mult)
                    erb = er[:, d, :].unsqueeze(1).bc([128, 8, 129])
                    tt("dve", ph[:, 0:1032].rr("p (t c) -> p t c", t=8), lic[:].unsqueeze(2).bc([128, 8, 129]), erb, ALU.mult)
                    tt("dve", mg[:, 0:1032].rr("p (t c) -> p t c", t=8), lrc[:].unsqueeze(2).bc([128, 8, 129]), erb, ALU.mult)
                    act(mg[:, 0:1032], mg[:, 0:1032], AF.Exp)
                    trig(1032)
                    tt("dve", W2[d][:, 0, :, 0:129], mg[:, 0:1032].rr("p (t c) -> p t c", t=8), cs[:, 0:1032].rr("p (t c) -> p t c", t=8), ALU.mult)
                    tt("dve", W2[d][:, 1, :, 0:129], mg[:, 0:1032].rr("p (t c) -> p t c", t=8), sn[:, 0:1032].rr("p (t c) -> p t c", t=8), ALU.mult)
                    dma(Bn[:], d_b[l, d, :, :, :, :].rr("g n c r -> n g (c r)"))
                    for ct in range(2):
                        dma(lr_r[:], d_lam_re[l, d, 8 * ct:8 * ct + 8, :].unsqueeze(1).bc([8, 16, 64]))
                        dma(li_r[:], d_lam_im[l, d, 8 * ct:8 * ct + 8, :].unsqueeze(1).bc([8, 16, 64]))
                        dma(dl_r[:], d_log_step[l, d, 8 * ct:8 * ct + 8].unsqueeze(1).unsqueeze(2).bc([8, 16, 1]))
                        act(dl_r[:], dl_r[:], AF.Exp)
                        ts("dve", ph[:, 0:64], li_r[:], dl_r[:, 0:1], None, ALU.mult)
                        act(mg[:, 0:64], lr_r[:], AF.Exp, scale=dl_r[:, 0:1])
                        trig(64)
                        tt("dve", q[0][:], mg[:, 0:64], cs[:, 0:64], ALU.mult)
                        ts("dve", q[0][:], q[0][:], -1.0, None, ALU.add)
                        tt("dve", q[1][:], mg[:, 0:64], sn[:, 0:64], ALU.mult)
                        tt("dve", q[2][:], lr_r[:], lr_r[:], ALU.mult)
                        tt("dve", q[3][:], li_r[:], li_r[:], ALU.mult)
                        tt("dve", q[2][:], q[2][:], q[3][:], ALU.add)
                        recip(q[2][:], q[2][:])
                        tt("dve", q[3][:], q[0][:], lr_r[:], ALU.mult)
                        tt("dve", q[4][:], q[1][:], li_r[:], ALU.mult)
                        tt("dve", q[3][:], q[3][:], q[4][:], ALU.add)
                        tt("dve", q[3][:], q[3][:], q[2][:], ALU.mult)
                        tt("dve", q[4][:], q[1][:], lr_r[:], ALU.mult)
                        tt("dve", q[5][:], q[0][:], li_r[:], ALU.mult)
                        tt("dve", q[4][:], q[4][:], q[5][:], ALU.subtract)
                        tt("dve", q[4][:], q[4][:], q[2][:], ALU.mult)
                        for r in range(2):
                            tr(pTr[:, 0:64], Bn[:, 8 * ct:8 * ct + 8, r:32:2], ident[0:64, 0:64])
                            cp("dve", q[6 + r][:], pTr[:, 0:64])
                        tt("dve", q[8][:], q[3][:], q[6][:], ALU.mult)
                        tt("dve", q[9][:], q[4][:], q[7][:], ALU.mult)
                        tt("dve", q[8][:], q[8][:], q[9][:], ALU.subtract)
                        tt("dve", q[9][:], q[3][:], q[7][:], ALU.mult)
                        tt("dve", q[5][:], q[4][:], q[6][:], ALU.mult)
                        tt("dve", q[9][:], q[9][:], q[5][:], ALU.add)
                        for ri in range(2):
                            tt("dve", BD[d][ct][ri][:].rr("p (g n) -> p g n", g=8), q[8 + ri][:].unsqueeze(1).bc([128, 8, 64]),
                               misc[:, 20:28].unsqueeze(2).bc([128, 8, 64]), ALU.mult)
                    dma(Cn[:], d_c[l, d, :, :].rr("(ct p) k -> p ct k", p=128))
                    for ct in range(2):
                        for ri in range(2):
                            tr(pTr[0:64, :], Cn[:, ct, ri:128:2], ident[:])
                            sgn = 1.0 if ri == 0 else -1.0
                            ts("dve", tmpC[0:64, :], pTr[0:64, :], sgn, None, ALU.mult)
                            ts("dve", tmpC[64:128, :], pTr[0:64, :], sgn, None, ALU.mult)
                            for tl in range(4):
                                t = 4 * ct + tl
                                for g2 in range(2):
                                    gl = 2 * tl + g2
                                    ps_ = slice(64 * g2, 64 * g2 + 64)
                                    tt("dve", Cm[d][ri][ps_, t, :], tmpC[ps_, :], m16[ps_, gl, :], ALU.mult)
                P.barrier()

            ub = sb(s, "s5_ub", [128, 2, NS], BF16)
            ys = sb(s, "s5_ys", [128, 2, NS])
            SL = []
            for ct in range(2):
                SL.append(dict(
                    bu=Pair([sb(s, "s5_bu%d%d" % (ct, ri), [128, 512]) for ri in range(2)]),
                    tA=[sb(s, "s5_t%d%d" % (ct, i), [128, 512]) for i in range(4)],
                    Z=Pair([sb(s, "s5_Z%d%d" % (ct, ri), [128, 512], BF16) for ri in range(2)]),
                    Sc=Pair([sb(s, "s5_Sc%d%d" % (ct, ri), [128, 4, 128]) for ri in range(2)]),
                    X=Pair([sb(s, "s5_X%d%d" % (ct, ri), [128, 4, 128], BF16) for ri in range(2)]),
                    cq=[sb(s, "s5_cq%d%d" % (ct, i), [128, 4]) for i in range(4)]))
            car = [[sb(s, "s5_car%d%d" % (d, ct), [128, 2, 4]) for ct in range(2)] for d in range(2)]
            h0t = sb(s, "s5_h0", [128, 2, 8]); cq0 = [sb(s, "s5_cqi%d" % i, [128, 4]) for i in range(2)]
            xfin = [sb(s, "s5_xf%d" % d, [128, 8, 2]) for d in range(2)]
            uf = sb(s, "s5_uf", [128, 2, 256]); ubs = sb(s, "s5_ubs", [128, 2, 256])
            yv = sb(s, "s5_yv", [128, 2, 256]); y2 = sb(s, "s5_y2", [128, 2, 256]); yg = sb(s, "s5_yg", [128, 2, 256], BF16)
            sgm = sb(s, "s5_sg", [128, 2, 256]); yd = sb(s, "s5_yd", [128, 2, 256], BF16)
            PJa = PJ.k("all")
            v4 = lambda a_: a_.rr("p (a b) -> p a b", a=4)
            for (col0, Ls, ci, is_s, pidx) in seqs:
                nC = Ls // 128
                for bb in range(Ls // 256):
                    dma(ubs[:], PJa[14 * 128:16 * 128, col0 + bb * 256:col0 + (bb + 1) * 256].rr("(m p) n -> p m n", p=128))
                    cp("act", ub[:, :, bb * 256:(bb + 1) * 256], ubs[:])
                for d in range(2):
                    a1c = 1 if d == 0 else 126
                    if is_s:
                        for t in range(8):
                            for r in range(2):
                                dma(h0t[:, r, t:t + 1], A(ss5.t[l, d, 2 * t:2 * t + 2, :, r].rearrange("g n -> (g n)").rearrange("(p o) -> p o", o=1), [ss5.buf]),
                                    allow_slow_non_contiguous=True)
                        for ct in range(2):
                            tsl = slice(4 * ct, 4 * ct + 4)
                            tt("dve", cq0[0][:], W2[d][:, 0, tsl, a1c], h0t[:, 0, tsl], ALU.mult)
                            tt("dve", cq0[1][:], W2[d][:, 1, tsl, a1c], h0t[:, 1, tsl], ALU.mult)
                            tt("dve", car[d][ct][:, 0, :], cq0[0][:], cq0[1][:], ALU.subtract)
                            tt("dve", cq0[0][:], W2[d][:, 0, tsl, a1c], h0t[:, 1, tsl], ALU.mult)
                            tt("dve", cq0[1][:], W2[d][:, 1, tsl, a1c], h0t[:, 0, tsl], ALU.mult)
                            tt("dve", car[d][ct][:, 1, :], cq0[0][:], cq0[1][:], ALU.add)
                    else:
                        for ct in range(2):
                            memset("pool", car[d][ct][:], 0.0)
                with ExitStack() as s3:
                    pBU = [pst(s3, "s5_pBU%d" % i, [128, 512]) for i in range(2)]
                    pSS = [pst(s3, "s5_pS%d" % i, [128, 2, 4, 128]) for i in range(2)]
                    pYY = [pst(s3, "s5_pY%d" % i, [128, 128]) for i in range(2)]
                    for d in range(2):
                        order = range(nC) if d == 0 else range(nC - 1, -1, -1)
                        last = 127 if d == 0 else 0
                        tri = M_TRII if d == 0 else M_TRIGE
                        for c in order:
                            is_final = (c == nC - 1) if d == 0 else (c == 0)
                            C2 = range(2)
                            for ri in range(2):
                                for ct in C2:
                                    mm(pBU[ct][:], ub[:, ct, c * 128:(c + 1) * 128], BD[d][ct][ri][:])
                                for ct in C2:
                                    cp("act", SL[ct]["bu"][:, ri, :], pBU[ct][:])
                            for ct in C2:
                                bu = SL[ct]["bu"]; tA = SL[ct]["tA"]
                                w1r = W1[d][:, 0, ct * 512:(ct + 1) * 512]; w1i = W1[d][:, 1, ct * 512:(ct + 1) * 512]
                                tt("dve", tA[0][:], bu[:, 0, :], w1r, ALU.mult)
                                tt("dve", tA[1][:], bu[:, 1, :], w1i, ALU.mult)
                                tt("dve", tA[2][:], bu[:, 0, :], w1i, ALU.mult)
                                tt("dve", tA[3][:], bu[:, 1, :], w1r, ALU.mult)
                            for ct in C2:
                                tA = SL[ct]["tA"]; Z = SL[ct]["Z"]
                                tt("pool", Z[:, 0, :], tA[0][:], tA[1][:], ALU.subtract)
                                tt("pool", Z[:, 1, :], tA[2][:], tA[3][:], ALU.add)
                            for ct in C2:
                                Z = SL[ct]["Z"]
                                for ri in range(2):
                                    for tl in range(4):
                                        mm(pSS[ct][:, ri, tl, :], Z[:, ri, tl * 128:(tl + 1) * 128], cmb[:, tri, :])
                            for ct in C2:
                                Sc = SL[ct]["Sc"]; cr = car[d][ct]
                                for ri in range(2):
                                    tt("dve", Sc[:, ri, :, :], pSS[ct][:, ri, :, :], cr[:, ri, :].unsqueeze(2).bc([128, 4, 128]), ALU.add)
                            for ct in C2:
                                Sc = SL[ct]["Sc"]; cq = SL[ct]["cq"]; cr = car[d][ct]
                                for (colx, dst_re, dst_im, doit) in ((127 if d == 0 else 0, xfin[d][:, 4 * ct:4 * ct + 4, 0], xfin[d][:, 4 * ct:4 * ct + 4, 1], is_final and not is_s),
                                                                     (128, cr[:, 0, :], cr[:, 1, :], True)):
                                    if not doit:
                                        continue
                                    ar = W2[d][:, 0, 4 * ct:4 * ct + 4, colx]; ai = W2[d][:, 1, 4 * ct:4 * ct + 4, colx]
                                    sr = Sc[:, 0, :, last]; si = Sc[:, 1, :, last]
                                    tt("dve", cq[0][:], ar, sr, ALU.mult)
                                    tt("dve", cq[1][:], ai, si, ALU.mult)
                                    tt("dve", cq[2][:], ar, si, ALU.mult)
                                    tt("dve", cq[3][:], ai, sr, ALU.mult)
                                    tt("dve", dst_re, cq[0][:], cq[1][:], ALU.subtract)
                                    tt("dve", dst_im, cq[2][:], cq[3][:], ALU.add)
                            for ct in C2:
                                Sc = SL[ct]["Sc"]; tA = SL[ct]["tA"]
                                w2r = W2[d][:, 0, 4 * ct:4 * ct + 4, 0:128]; w2i = W2[d][:, 1, 4 * ct:4 * ct + 4, 0:128]
                                tt("dve", v4(tA[0][:]), w2r, Sc[:, 0, :, :], ALU.mult)
                                tt("dve", v4(tA[1][:]), w2i, Sc[:, 1, :, :], ALU.mult)
                                tt("dve", v4(tA[2][:]), w2r, Sc[:, 1, :, :], ALU.mult)
                                tt("dve", v4(tA[3][:]), w2i, Sc[:, 0, :, :], ALU.mult)
                            for ct in C2:
                                tA = SL[ct]["tA"]; X = SL[ct]["X"]
                                tt("pool", X[:, 0, :, :], v4(tA[0][:]), v4(tA[1][:]), ALU.subtract)
                                tt("pool", X[:, 1, :, :], v4(tA[2][:]), v4(tA[3][:]), ALU.add)
                            for ct in C2:
                                X = SL[ct]["X"]
                                i_ = 0
                                for ri in range(2):
                                    for tl in range(4):
                                        mm(pYY[ct][:], Cm[d][ri][:, 4 * ct + tl, :], X[:, ri, tl, :], start=(i_ == 0), stop=(i_ == 7))
                                        i_ += 1
                            for ct in C2:
                                ysl = ys[:, ct, c * 128:(c + 1) * 128]
                                if d == 0:
                                    cp("act", ysl, pYY[ct][:])
                                else:
                                    tt("dve", ysl, ysl, pYY[ct][:], ALU.add)
                if not is_s:
                    for d in range(2):
                        for t in range(8):
                            for r in range(2):
                                dma(A(o_s5.t[pidx, l, d, 2 * t:2 * t + 2, :, r].rearrange("g n -> (g n)").rearrange("(p o) -> p o", o=1), [o_s5.buf]),
                                    xfin[d][:, t, r:r + 1], q="q1", allow_slow_non_contiguous=True)
                with ExitStack() as s3:
                    pG = pst(s3, "s5_pG", [128, 4, 256])
                    for bb in range(Ls // 256):
                        c0 = col0 + bb * 256
                        b = c0 // 256
                        sl = slice(bb * 256, (bb + 1) * 256)
                        dma(uf[:], PJa[14 * 128:16 * 128, c0:c0 + 256].rr("(m p) n -> p m n", p=128))
                        for ct in range(2):
                            stt("dve", yv[:, ct, :], uf[:, ct, :], ddc[:, ct:ct + 1], ys[:, ct, sl], ALU.mult, ALU.add)
                        act(y2[:], yv[:], AF.Square)
                        ts("dve", y2[:], y2[:], 0.044715, 1.0, ALU.mult, ALU.add)
                        tt("dve", y2[:], y2[:], yv[:], ALU.mult)
                        act(sgm[:], y2[:], AF.Sigmoid, scale=1.5957691216057308)
                        tt("dve", yg[:], yv[:], sgm[:], ALU.mult)
                        for mo in range(4):
                            for k in range(2):
                                mm(pG[:, mo, :], wg[:, k, mo * 128:(mo + 1) * 128], yg[:, k, :], start=(k == 0), stop=(k == 1))
                        act(sgm[:], pG[:, 2:4, :], AF.Sigmoid)
                        tt("dve", yd[:], sgm[:], pG[:, 0:2, :], ALU.mult)
                        dma(YC.k(("d", b))[768:1024, c0:c0 + 256].rr("(m p) n -> p m n", p=128), yd[:], q="q1")
        P.barrier()

    def stage_stub(l, which):
        with ExitStack() as s:
            z = sb(s, "stubz", [128, 2, 256], BF16)
            memset("pool", z[:], 0.0)
            r0 = 512 if which == "c" else 768
            for b in range(NBLK):
                c0 = b * 256
                dma(YC.k((which, b))[r0:r0 + 256, c0:c0 + 256].rr("(m p) n -> p m n", p=128), z[:], q="q1")
        P.barrier()

    P.marks = []

    def mark(name):
        P.marks.append((name, P.engs["pe"].n, P.engs["act"].n, P.engs["dve"].n))

    stage_mod(); mark("mod")
    stage_t_in(); mark("t_in")
    for l in range(L):
        stage_inproj(l); mark('inproj%d' % l)
        stage_attn(l); mark('attn%d' % l)
        if STUB_C:
            stage_stub(l, "c")
        else:
            stage_ssd(l); mark('ssd%d' % l)
        if STUB_D:
            stage_stub(l, "d")
        else:
            stage_s5(l); mark('s5_%d' % l)
        with ExitStack() as sw:
            with ExitStack() as sw2:
                wu_ = load_weight_bg(sw, sw2, "wu", lambda k: w_up.k(0)[l, k * 128:(k + 1) * 128, :], 5632, 8)
                stage_outproj(l); mark('outproj%d' % l)
            stage_ffn(l, wu_); mark('ffn%d' % l)
    stage_t_out(); mark('t_out')
    P.barrier()
    es.close()
    return nc, P


def _consts(NS):
    p = np.arange(128)
    misc = np.zeros((128, 64), np.float32)
    misc[:, 0] = p
    misc[:, 1] = 127 - p
    for hl in range(2):
        for half in range(2):
            lo = 64 * hl + 32 * half
            misc[lo:lo + 32, 2 + hl * 2 + half] = 1.0
    misc[0:64, 10] = 1.0
    misc[64:128, 11] = 1.0
    misc[:, 13] = 1.0
    misc[:, 15] = EPS
    mats = np.zeros((11, 128, 128), np.float32)
    mats[0] = 1.0 / 1024
    mats[1] = (p[:, None] // 64 == p[None, :] // 64) / 64.0
    mats[2] = 1.0 / 256
    pa = (p // 32) * 32 + ((p % 32) + 16) % 32
    pb = (p // 64) * 64 + ((p % 64) + 32) % 64
    mats[3][pa, p] = 1.0
    mats[4][pb, p] = 1.0
    mats[5] = (p[:, None] <= p[None, :])
    mats[6] = (p[:, None] < p[None, :])
    mats[7] = np.where(p[None, :] < p[:, None], -30000.0, 0.0)
    mats[8] = np.where(p[:, None] < p[None, :], 30000.0, 0.0)
    mats[9] = 1.0
    mats[10] = (p[:, None] >= p[None, :])
    misc[:, 16] = -p
    misc[:, 17] = -(127 - p)
    for q in range(8):
        misc[q * 16:(q + 1) * 16, 20 + q] = 1.0
    erow = np.zeros((2, 128, 129), np.float32)
    erow[0, :, 0:128] = np.arange(128)[None, :]
    erow[1, :, 0:128] = (127 - np.arange(128))[None, :]
    erow[:, :, 128] = 128.0
    m16 = np.zeros((128, 8, 128), np.float32)
    for q in range(8):
        m16[:, q, q * 16:(q + 1) * 16] = 1.0
    t = np.arange(NS)
    row = (t // 64).astype(np.float32)
    col = (t % 64).astype(np.float32)

    def ang(n):
        fr = (np.float32(10000.0) ** (-np.arange(n, dtype=np.float32) / np.float32(n))).astype(np.float32)
        return np.concatenate([row[:, None] * fr, col[:, None] * fr], axis=-1).astype(np.float32)

    aA = ang(8)
    aB = ang(16)
    ropeA = np.zeros((2, 128, NS), np.float32)
    ropeB = np.zeros((2, 128, NS), np.float32)
    for i in range(128):
        idx = i % 32
        ropeA[0, i] = np.cos(aA[:, idx % 16])
        ropeA[1, i] = np.sin(aA[:, idx % 16]) * (-1.0 if idx < 16 else 1.0)
        idx = i % 64
        ropeB[0, i] = np.cos(aB[:, idx % 32])
        ropeB[1, i] = np.sin(aB[:, idx % 32]) * (-1.0 if idx < 32 else 1.0)
    return dict(c_ident=np.eye(128, dtype=np.float32), c_misc=misc, c_mats=mats, c_m16=m16, ropeA=ropeA, ropeB=ropeB, c_erow=erow)


def _win_cols():
    aq, ak, av, bq, bk, bv, cz, cx, cb, cc, cdt, du = 0, 256, 512, 768, 1024, 1152, 1280, 1536, 1792, 1920, 2048, 2056
    r = lambda a, n: list(range(a, a + n))
    cols = r(aq, 256) + r(ak, 256)
    cols += r(bq, 64) + r(bq + 128, 64) + r(bq + 64, 64) + r(bq + 192, 64)
    cols += r(bk, 128) + r(cz, 256) + r(cx, 256) + r(cb, 128) + r(cc, 128) + r(cdt, 8) + r(du, 256)
    cols += r(av, 256) + r(bv, 128) + r(ak, 256) + r(bk, 128)
    return np.array(cols)


def make_in_maps(inp, n_cores, NS, prompt_per_core=2):
    f = lambda a: np.ascontiguousarray(np.asarray(a, dtype=np.float32))
    L = DEPTH
    consts = _consts(NS)
    shared = dict(
        w_mod=f(inp["w_mod"]), b_mod=f(inp["b_mod"]), g_pre1=f(inp["g_pre1"]), g_post1=f(inp["g_post1"]),
        g_pre2=f(inp["g_pre2"]), g_post2=f(inp["g_post2"]), w_in=f(np.asarray(inp["w_in"])[:, :, _win_cols()]),
        a_lam=f(np.asarray(inp["a_lam"]).reshape(L, 128)), a_subln=f(inp["a_subln"]), b_qnorm=f(inp["b_qnorm"]),
        b_knorm=f(inp["b_knorm"]), c_conv_w=f(inp["c_conv_w"]), c_conv_b=f(inp["c_conv_b"]),
        c_dt_bias=f(np.asarray(inp["c_dt_bias"]).reshape(L, 8)), c_a_log=f(np.asarray(inp["c_a_log"]).reshape(L, 8)),
        c_d=f(inp["c_d"]), c_norm=f(inp["c_norm"]), d_lam_re=f(inp["d_lam_re"]), d_lam_im=f(inp["d_lam_im"]),
        d_log_step=f(inp["d_log_step"]), d_b=f(inp["d_b"]), d_c=f(np.asarray(inp["d_c"]).reshape(L, 2, 256, 128)),
        d_d=f(inp["d_d"]), d_glu=f(inp["d_glu"]), w_out=f(inp["w_out"]), w_up=f(inp["w_up"]),
        ffn_conv_w=f(inp["ffn_conv_w"]), ffn_conv_b=f(inp["ffn_conv_b"]), w_down=f(inp["w_down"]), **consts)
    xp = np.asarray(inp["x_prompt"], np.float32)
    xs = np.asarray(inp["x_sample"], np.float32)
    nsb = xs.shape[0]
    maps = []
    for c in range(n_cores):
        sb_ = (c * nsb) // n_cores
        m = dict(shared)
        m["xin"] = f(np.concatenate([xp[prompt_per_core * c + i] for i in range(prompt_per_core)] + [xs[sb_]], axis=0))
        m["cond"] = f(np.stack([np.asarray(inp["c_ctx"], np.float32), np.asarray(inp["c"], np.float32)[sb_]], axis=0))
        m["cak"] = f(np.asarray(inp["cache_a_k"])[sb_].reshape(L, 256, 256))
        m["cav"] = f(np.asarray(inp["cache_a_v"])[sb_].reshape(L, 256, 256))
        m["cbk"] = f(np.asarray(inp["cache_b_k"])[sb_].reshape(L, 256, 128))
        m["cbv"] = f(np.asarray(inp["cache_b_v"])[sb_].reshape(L, 256, 128))
        m["sssd"] = f(np.asarray(inp["state_ssd"])[sb_])
        m["ss5"] = f(np.asarray(inp["state_s5"])[sb_])
        maps.append(m)
    return maps


def gather(results, n_cores, NS, nsb, prompt_per_core=2):
    L = DEPTH
    yp = np.concatenate([r["yout"][0:512].reshape(2, 256, D) for r in results], axis=0)
    per = n_cores // nsb
    ys = np.stack([results[b * per]["yout"][512:512 + NS] for b in range(nsb)], axis=0)
    cat = lambda k: np.concatenate([r[k] for r in results], axis=0)
    nak = cat("o_ak").reshape(-1, L, 256, 4, 64)
    nav = cat("o_av").reshape(-1, L, 256, 4, 64)
    nbk = cat("o_bk").reshape(-1, L, 256, 2, 64)
    nbv = cat("o_bv").reshape(-1, L, 256, 2, 64)
    nssd = cat("o_ssd")
    ns5 = cat("o_s5")
    return tuple(np.ascontiguousarray(a.astype(np.float32)) for a in (yp, ys, nak, nav, nbk, nbv, nssd, ns5))


_CACHE = {}


def kernel(**inputs):
    NS = 4096
    n = 8
    if NS not in _CACHE:
        _CACHE[NS] = build(NS)
    nc, P = _CACHE[NS]
    maps = make_in_maps(inputs, n, NS)
    res = run_bass_kernel_spmd(nc, maps, core_ids=list(range(n)))
    return gather(res.results, n, NS, 2)
```

```python
import math
from contextlib import ExitStack
import numpy as np
import concourse.bass as bass
import concourse.mybir as mybir
from concourse.bass_utils import run_bass_kernel_spmd

F32 = mybir.dt.float32
BF16 = mybir.dt.bfloat16
I32 = mybir.dt.int32
ALU = mybir.AluOpType
AF = mybir.ActivationFunctionType
AX = mybir.AxisListType

D = 1024
DEPTH = 2
EPS = 1e-6
PI = math.pi


class Buf:
    __slots__ = ("w", "r")

    def __init__(self):
        self.w = None
        self.r = {}


class A:
    def __init__(self, ap, bufs):
        self.ap = ap
        self.bufs = bufs

    def __getitem__(self, k):
        return A(self.ap[k], self.bufs)

    def unsqueeze(self, a):
        return A(self.ap.unsqueeze(a), self.bufs)

    def bc(self, shape):
        return A(self.ap.to_broadcast(list(shape)), self.bufs)

    def rr(self, s, **kw):
        return A(self.ap.rearrange(s, **kw), self.bufs)

    def pb(self, n):
        return A(self.ap.partition_broadcast(n), self.bufs)


class T:
    def __init__(self, tensor, ap=None):
        self.t = tensor
        self.apx = ap
        self.buf = Buf()
        self.keys = {}

    def __getitem__(self, k):
        base = self.apx if self.apx is not None else self.t
        return A(base[k], [self.buf])

    def k(self, key):
        if key not in self.keys:
            v = T(self.t, self.apx)
            self.keys[key] = v
        return self.keys[key]


class Pair:
    def __init__(self, parts):
        self.parts = parts

    def __getitem__(self, key):
        p, ri = key[0], key[1]
        return self.parts[ri][(p,) + tuple(key[2:])]


class Eng:
    def __init__(self, prog, name, obj, is_dma=False, nlanes=8):
        self.e = obj
        self.is_dma = is_dma
        self.seen = {}
        if is_dma:
            self.lanes = [prog.new_sem("%s_l%d" % (name, i)) for i in range(nlanes)]
            self.lane_cnt = [0] * nlanes
            self.next = 0
        else:
            self.sem = prog.new_sem("s_" + name)
            self.n = 0


class Prog:
    def __init__(self, nc, es):
        self.nc = nc
        self.es = es
        self.sems = {}
        self.ninst = 0
        self.engs = {}
        self.engs["pe"] = Eng(self, "pe", nc.tensor)
        self.engs["act"] = Eng(self, "act", nc.scalar)
        self.engs["dve"] = Eng(self, "dve", nc.vector)
        self.engs["pool"] = Eng(self, "pool", nc.gpsimd)
        self.engs["q0"] = Eng(self, "q0", nc.sync, True, 12)
        self.engs["q1"] = Eng(self, "q1", nc.gpsimd, True, 8)
        self.stream = {"pe": "pe", "act": "act", "dve": "dve", "pool": "pool", "q0": "q0", "q1": "pool"}

    def new_sem(self, name):
        self.sems[name] = self.es.enter_context(self.nc.semaphore(name))
        return name

    def _wait(self, st, key, val):
        s = self.engs[st]
        if s.seen.get(key, 0) >= val:
            return
        s.e.wait_ge(self.sems[key], val)
        s.seen[key] = val

    def _deps(self, st, r, w, skip=None):
        deps = {}
        for b in r:
            if b.w is not None:
                deps[b.w[0]] = max(deps.get(b.w[0], 0), b.w[1])
        for b in w:
            if b.w is not None:
                deps[b.w[0]] = max(deps.get(b.w[0], 0), b.w[1])
            for k, v in b.r.items():
                deps[k] = max(deps.get(k, 0), v)
        for k, v in deps.items():
            if k != skip:
                self._wait(st, k, v)

    def _mark(self, key, val, r, w):
        for b in r:
            b.r[key] = max(b.r.get(key, 0), val)
        for b in w:
            b.w = (key, val)
            b.r = {}

    def op(self, eng, fn, r, w):
        e = self.engs[eng]
        self._deps(self.stream[eng], r, w, skip=(e.sem if eng == "pe" else None))
        inst = fn()
        e.n += 1
        inst.then_inc(self.sems[e.sem], 1)
        self._mark(e.sem, e.n, r, w)
        self.ninst += 1

    def dma(self, q, out, in_, **kw):
        e = self.engs[q]
        st = self.stream[q]
        lane = e.next
        e.next = (e.next + 1) % len(e.lanes)
        key = e.lanes[lane]
        if e.lane_cnt[lane] > 0:
            self._wait(st, key, 16 * e.lane_cnt[lane])
        self._deps(st, in_.bufs, out.bufs)
        inst = e.e.dma_start(out=out.ap, in_=in_.ap, **kw)
        e.lane_cnt[lane] += 1
        inst.then_inc(self.sems[key], 16)
        self._mark(key, 16 * e.lane_cnt[lane], in_.bufs, out.bufs)
        self.ninst += 1

    def barrier(self):
        tg = []
        for e in self.engs.values():
            if e.is_dma:
                for i, k in enumerate(e.lanes):
                    if e.lane_cnt[i]:
                        tg.append((k, 16 * e.lane_cnt[i]))
            elif e.n:
                tg.append((e.sem, e.n))
        for st in ("pe", "act", "dve", "pool", "q0"):
            for k, v in tg:
                self._wait(st, k, v)


STUB_C = False
STUB_D = False


def build(NS):
    nc = bass.Bass("TRN2", target_bir_lowering=False)
    TT = 512 + NS
    NBLK = TT // 256
    NCH = TT // 128
    LK = NS + 256
    L = DEPTH
    es = ExitStack()
    P = Prog(nc, es)

    def din(name, shape, dt=F32):
        return T(nc.dram_tensor(name, list(shape), dt, kind="ExternalInput").ap())

    def dout(name, shape, dt=F32):
        return T(nc.dram_tensor(name, list(shape), dt, kind="ExternalOutput").ap())

    def dscr(name, shape, dt=F32):
        return T(nc.dram_tensor(name, list(shape), dt, kind="Internal").ap())

    xin = din("xin", [TT, D])
    cond = din("cond", [2, D])
    w_mod = din("w_mod", [L, D, 6 * D]); b_mod = din("b_mod", [L, 6 * D])
    g_pre1 = din("g_pre1", [L, D]); g_post1 = din("g_post1", [L, D])
    g_pre2 = din("g_pre2", [L, D]); g_post2 = din("g_post2", [L, D])
    w_in = din("w_in", [L, D, 2696])
    a_lam = din("a_lam", [L, 128]); a_subln = din("a_subln", [L, 64])
    b_qnorm = din("b_qnorm", [L, 64]); b_knorm = din("b_knorm", [L, 64])
    c_conv_w = din("c_conv_w", [L, 3, 512]); c_conv_b = din("c_conv_b", [L, 512])
    c_dt_bias = din("c_dt_bias", [L, 8]); c_a_log = din("c_a_log", [L, 8]); c_d = din("c_d", [L, 4])
    c_norm = din("c_norm", [L, 256])
    d_lam_re = din("d_lam_re", [L, 2, 16, 64]); d_lam_im = din("d_lam_im", [L, 2, 16, 64])
    d_log_step = din("d_log_step", [L, 2, 16])
    d_b = din("d_b", [L, 2, 16, 64, 16, 2]); d_c = din("d_c", [L, 2, 256, 128])
    d_d = din("d_d", [L, 256]); d_glu = din("d_glu", [L, 256, 512])
    w_out = din("w_out", [L, D, D]); w_up = din("w_up", [L, D, 5632])
    ffn_conv_w = din("ffn_conv_w", [L, 3, 5632]); ffn_conv_b = din("ffn_conv_b", [L, 5632])
    w_down = din("w_down", [L, 2816, D])
    cak = din("cak", [L, 256, 256]); cav = din("cav", [L, 256, 256])
    cbk = din("cbk", [L, 256, 128]); cbv = din("cbv", [L, 256, 128])
    sssd = din("sssd", [L, 2, 4, 64, 64]); ss5 = din("ss5", [L, 2, 16, 64, 2])
    c_ident = din("c_ident", [128, 128])
    c_misc = din("c_misc", [128, 64])
    c_mats = din("c_mats", [11, 128, 128])
    c_erow = din("c_erow", [2, 128, 129])
    c_m16 = din("c_m16", [128, 8, 128])
    ropeA = din("ropeA", [2, 128, NS]); ropeB = din("ropeB", [2, 128, NS])

    yout = dout("yout", [TT, D])
    o_ak = dout("o_ak", [2, L, 256, 256]); o_av = dout("o_av", [2, L, 256, 256])
    o_bk = dout("o_bk", [2, L, 256, 128]); o_bv = dout("o_bv", [2, L, 256, 128])
    o_ssd = dout("o_ssd", [2, L, 2, 4, 64, 64]); o_s5 = dout("o_s5", [2, L, 2, 16, 64, 2])

    XT = [dscr("XT%d" % i, [D, TT]) for i in range(3)]
    XM = dscr("XM", [D, TT])
    NPJ = 16
    PJ = dscr("PJ", [NPJ * 128, TT])
    VT = dscr("VT", [TT, 384])
    YC = dscr("YC", [D, TT], BF16)

    seqs = [(0, 256, 0, False, 0), (256, 256, 0, False, 1), (512, NS, 1, True, -1)]

    def bufs(*xs):
        out = []
        for x in xs:
            if isinstance(x, A):
                out += x.bufs
        return out

    def apx(x):
        return x.ap if isinstance(x, A) else x

    def mm(out, lhsT, rhs, start=True, stop=True):
        P.op("pe", lambda: nc.tensor.matmul(out.ap, lhsT=lhsT.ap, rhs=rhs.ap, start=start, stop=stop),
             bufs(lhsT, rhs), bufs(out))

    def act(out, in_, func, bias=0.0, scale=1.0):
        P.op("act", lambda: nc.scalar.activation(out=out.ap, in_=in_.ap, func=func, bias=apx(bias), scale=apx(scale)),
             bufs(in_, bias, scale), bufs(out))

    def engobj(e):
        return nc.vector if e == "dve" else nc.gpsimd

    def tt(e, out, in0, in1, op):
        P.op(e, lambda: engobj(e).tensor_tensor(out=out.ap, in0=in0.ap, in1=in1.ap, op=op), bufs(in0, in1), bufs(out))

    def ts(e, out, in0, s1, s2, op0, op1=None):
        if op1 is None:
            P.op(e, lambda: engobj(e).tensor_scalar(out=out.ap, in0=in0.ap, scalar1=apx(s1), scalar2=None, op0=op0),
                 bufs(in0, s1), bufs(out))
        else:
            P.op(e, lambda: engobj(e).tensor_scalar(out=out.ap, in0=in0.ap, scalar1=apx(s1), scalar2=apx(s2), op0=op0, op1=op1),
                 bufs(in0, s1, s2), bufs(out))

    def stt(e, out, in0, sc, in1, op0, op1):
        P.op(e, lambda: engobj(e).scalar_tensor_tensor(out=out.ap, in0=in0.ap, scalar=apx(sc), in1=in1.ap, op0=op0, op1=op1),
             bufs(in0, sc, in1), bufs(out))

    def cp(e, out, in_):
        if e == "act":
            act(out, in_, AF.Copy)
        else:
            P.op(e, lambda: engobj(e).tensor_copy(out=out.ap, in_=in_.ap), bufs(in_), bufs(out))

    def memset(e, out, val):
        P.op(e, lambda: engobj(e).memset(out.ap, val), [], bufs(out))

    def recip(out, in_):
        P.op("dve", lambda: nc.vector.reciprocal(out=out.ap, in_=in_.ap), bufs(in_), bufs(out))

    def dma(out, in_, q="q0", **kw):
        P.dma(q, out, in_, **kw)

    uid = [0]

    def sb(stack, name, shape, dt=F32):
        uid[0] += 1
        return T(stack.enter_context(nc.sbuf_tensor("%s_%d" % (name, uid[0]), list(shape), dt)))

    def pst(stack, name, shape, dt=F32):
        uid[0] += 1
        return T(stack.enter_context(nc.psum_tensor("%s_%d" % (name, uid[0]), list(shape), dt)))

    ident = sb(es, "ident", [128, 128]); dma(ident[:], c_ident[:, :])
    identb = sb(es, "identb", [128, 128], BF16)
    misc = sb(es, "misc", [128, 64]); dma(misc[:], c_misc[:, :])
    cm = sb(es, "cmats", [128, 11, 128]); dma(cm[:], c_mats[:, :, :].rr("m p n -> p m n"))
    cmb = sb(es, "cmatsb", [128, 11, 128], BF16)
    m16 = sb(es, "m16", [128, 8, 128]); dma(m16[:], c_m16[:, :, :])
    cp("dve", identb[:], ident[:])
    cp("dve", cmb[:], cm[:])
    M_ONES1024, M_BLK64, M_ONES256, M_PERMA, M_PERMB, M_TRII, M_TRIE, M_NEGF, M_POSB, M_ONES, M_TRIGE = range(11)
    mod_sb = sb(es, "mod_sb", [128, L, 2, 48])
    gains = sb(es, "gains", [128, L, 4, 8])
    for l in range(L):
        for i, g in enumerate((g_pre1, g_post1, g_pre2, g_post2)):
            dma(gains[:, l, i, :], g[l, :].rr("(k p) -> p k", p=128), allow_slow_non_contiguous=True)

    def stage_mod():
        with ExitStack() as s:
            cT = sb(s, "cT", [128, 8, 2])
            sT = sb(s, "sT", [128, 8, 2])
            for c_ in range(2):
                dma(cT[:, :, c_], cond[c_, :].rr("(k p) -> p k", p=128), allow_slow_non_contiguous=True)
            act(sT[:], cT[:], AF.Silu)
            bm = sb(s, "bm", [128, L, 48])
            for l_ in range(L):
                dma(bm[:, l_, :], b_mod[l_, :].rr("(m p) -> p m", p=128), allow_slow_non_contiguous=True)
            wk = [sb(s, "wmk%d" % i, [128, 8, 1536]) for i in range(2)]
            pm = pst(s, "pm", [128, 48, 2])
            gi_ = 0
            for l in range(L):
                for grp in range(4):
                    w = wk[gi_ % 2]; gi_ += 1
                    for k in range(8):
                        dma(w[:, k, :], w_mod[l, k * 128:(k + 1) * 128, grp * 1536:(grp + 1) * 1536], q=("q0" if k % 2 == 0 else "q1"))
                    for mi in range(12):
                        m = grp * 12 + mi
                        for k in range(8):
                            mm(pm[:, m, :], w[:, k, mi * 128:(mi + 1) * 128], sT[:, k, :], start=(k == 0), stop=(k == 7))
                for c in range(2):
                    tt("dve", mod_sb[:, l, c, :], pm[:, :, c], bm[:, l, :], ALU.add)
                for c in range(2):
                    for (six, gi) in ((1, 0), (4, 2)):
                        ts("dve", mod_sb[:, l, c, six * 8:(six + 1) * 8], mod_sb[:, l, c, six * 8:(six + 1) * 8], 1.0, None, ALU.add)
                        tt("dve", mod_sb[:, l, c, six * 8:(six + 1) * 8], mod_sb[:, l, c, six * 8:(six + 1) * 8], gains[:, l, gi, :], ALU.mult)
                    for (six, gi) in ((2, 1), (5, 3)):
                        tt("dve", mod_sb[:, l, c, six * 8:(six + 1) * 8], mod_sb[:, l, c, six * 8:(six + 1) * 8], gains[:, l, gi, :], ALU.mult)
        P.barrier()

    def stage_t_in():
        with ExitStack() as s:
            xt = [sb(s, "tin%d" % i, [128, D]) for i in range(2)]
            xo = [sb(s, "tio%d" % i, [128, 8, 128]) for i in range(2)]
            pp = [pst(s, "tip%d" % i, [128, 4, 128]) for i in range(2)]
            for c in range(NCH):
                a = xt[c % 2]; o = xo[c % 2]
                dma(a[:], xin[c * 128:(c + 1) * 128, :])
                for hlf in range(2):
                    p_ = pp[hlf]
                    for j in range(4):
                        k = hlf * 4 + j
                        P.op("pe", lambda: nc.tensor.transpose(p_[:, j, :].ap, a[:, k * 128:(k + 1) * 128].ap, ident[:].ap),
                             bufs(a[:], ident[:]), bufs(p_[:]))
                    cp("act" if hlf else "dve", o[:, hlf * 4:(hlf + 1) * 4, :], p_[:])
                dma(XT[0].k(c // 2)[:, c * 128:(c + 1) * 128].rr("(k p) n -> p k n", p=128), o[:], q="q1")
        P.barrier()

    def stage_t_out():
        with ExitStack() as s:
            xi = [sb(s, "toi%d" % i, [128, 8, 128]) for i in range(2)]
            xo = [sb(s, "too%d" % i, [128, D]) for i in range(2)]
            pp = [pst(s, "top%d" % i, [128, 512]) for i in range(2)]
            for c in range(NCH):
                a = xi[c % 2]; o = xo[c % 2]
                dma(a[:], XT[2].k(c // 2)[:, c * 128:(c + 1) * 128].rr("(k p) n -> p k n", p=128))
                for hlf in range(2):
                    p_ = pp[hlf]
                    for j in range(4):
                        k = hlf * 4 + j
                        P.op("pe", lambda: nc.tensor.transpose(p_[:, j * 128:(j + 1) * 128].ap, a[:, k, :].ap, ident[:].ap),
                             bufs(a[:], ident[:]), bufs(p_[:]))
                    cp("act" if hlf else "dve", o[:, hlf * 512:(hlf + 1) * 512], p_[:])
                dma(yout[c * 128:(c + 1) * 128, :], o[:], q="q1")
        P.barrier()

    def rstd_from_ps(s_out, ps_in):
        act(s_out, ps_in, AF.Sqrt, bias=misc[:, 15:16], scale=1.0)
        recip(s_out, s_out)

    def load_weight_bf16(s, name, src_rows, ncols, nk, dst=None, chunk=1024):
        w = dst if dst is not None else sb(s, name, [128, nk, ncols], BF16)
        with ExitStack() as s2:
            stg = [sb(s2, name + "_st%d" % i, [128, chunk]) for i in range(2)]
            i = 0
            for k in range(nk):
                for c0 in range(0, ncols, chunk):
                    c1 = min(ncols, c0 + chunk)
                    st = stg[i % 2]
                    dma(st[:, 0:c1 - c0], src_rows(k)[:, c0:c1], q=("q0" if i % 2 == 0 else "q1"))
                    cp("pool" if i % 2 == 0 else "dve", w[:, k, c0:c1], st[:, 0:c1 - c0])
                    i += 1
            P.barrier()
        return w

    def weight_loader_gen(s, s_stage, name, src_rows, ncols, nk, chunk=1024):
        w = sb(s, name, [128, nk, ncols], BF16)
        stg = [sb(s_stage, name + "_st%d" % i, [128, chunk]) for i in range(2)]

        def gen():
            i = 0
            for k in range(nk):
                for c0 in range(0, ncols, chunk):
                    c1 = min(ncols, c0 + chunk)
                    st = stg[i % 2]
                    dma(st[:, 0:c1 - c0], src_rows(k)[:, c0:c1], q=("q0" if i % 2 == 0 else "q1"))
                    cp("pool" if i % 2 == 0 else "dve", w[:, k, c0:c1], st[:, 0:c1 - c0])
                    i += 1
                    yield
        return w, gen()

    def stage_inproj(l):
        with ExitStack() as s:
            win = load_weight_bf16(s, "win", lambda k: w_in.k(0)[l, k * 128:(k + 1) * 128, :], 2696, 8)
            gkb = sb(s, "gkb", [128, 64])
            dma(gkb[:], b_knorm[l, :].pb(128))
            xT = [sb(s, "ip_x%d" % i, [128, 8, 256]) for i in range(2)]
            sq = sb(s, "ip_sq", [128, 8, 256], BF16)
            rs = sb(s, "ip_rs", [128, 256])
            hn = sb(s, "ip_hn", [128, 8, 256])
            hT = sb(s, "ip_h", [128, 8, 256], BF16)
            pj = [sb(s, "ip_pj%d" % i, [128, NPJ, 256]) for i in range(2)]
            vt = [sb(s, "ip_vt%d" % i, [128, 768]) for i in range(2)]
            kn = sb(s, "ip_kn", [128, 128]); kq = sb(s, "ip_kq", [128, 128]); kr = sb(s, "ip_kr", [128, 2])
            pms = pst(s, "ip_pms", [128, 256])
            pps = [pst(s, "ip_pp%d" % i, [128, 256]) for i in range(5)]
            pvs = [pst(s, "ip_pv%d" % i, [128, 384]) for i in range(2)]
            tile_cols = [(i * 128, 128) for i in range(13)] + [(13 * 128, 8), (13 * 128 + 8, 128), (13 * 128 + 136, 128)]
            VOFF = 13 * 128 + 8 + 256
            cnt = 0
            for b in range(NBLK):
                c0 = b * 256
                ci = 0 if b < 2 else 1
                x = xT[b % 2]
                dma(x[:], XT[l].k(b)[:, c0:c0 + 256].rr("(k p) n -> p k n", p=128))
                act(sq[:], x[:], AF.Square)
                for k in range(8):
                    mm(pms[:], cmb[:, M_ONES1024, :], sq[:, k, :], start=(k == 0), stop=(k == 7))
                rstd_from_ps(rs[:], pms[:])
                tt("dve", hn[:], x[:], rs[:].unsqueeze(1).bc([128, 8, 256]), ALU.mult)
                for k in range(8):
                    act(hT[:, k, :], hn[:, k, :], AF.Identity, bias=mod_sb[:, l, ci, k:k + 1], scale=mod_sb[:, l, ci, 8 + k:9 + k])
                o = pj[b % 2]
                for m, (cc0, cw) in enumerate(tile_cols):
                    pp = pps[cnt % 5]; cnt += 1
                    for k in range(8):
                        mm(pp[0:cw, :], win[:, k, cc0:cc0 + cw], hT[:, k, :], start=(k == 0), stop=(k == 7))
                    cp("act" if m % 2 else "dve", o[0:cw, m, :], pp[0:cw, :])
                dma(PJ.k(b)[:, c0:c0 + 256].rr("(m p) n -> p m n", p=128), o[:], q="q1")
                for t2 in range(2):
                    v = vt[t2]
                    nhalf = 2 if b < 2 else 1
                    for h2 in range(nhalf):
                        pv = pvs[h2]
                        for k in range(8):
                            mm(pv[:], hT[:, k, t2 * 128:(t2 + 1) * 128], win[:, k, VOFF + h2 * 384:VOFF + (h2 + 1) * 384],
                               start=(k == 0), stop=(k == 7))
                        cp("dve" if h2 else "act", v[:, h2 * 384:(h2 + 1) * 384], pv[:])
                    r0 = c0 + t2 * 128
                    dma(VT.k(b)[r0:r0 + 128, :], v[:, 0:384], q="q1")
                    if b < 2:
                        pr = t2 * 128
                        dma(o_av[b, l, pr:pr + 128, :], v[:, 0:256], q="q1")
                        dma(o_bv[b, l, pr:pr + 128, :], v[:, 256:384], q="q1")
                        dma(o_ak[b, l, pr:pr + 128, :], v[:, 384:640], q="q1")
                        tt("dve", kq[:], v[:, 640:768], v[:, 640:768], ALU.mult)
                        P.op("dve", lambda: nc.vector.tensor_reduce(out=kr[:].ap, in_=kq[:].rr("p (h d) -> p h d", h=2).ap,
                                                                    axis=AX.X, op=ALU.add), bufs(kq[:]), bufs(kr[:]))
                        act(kr[:], kr[:], AF.Sqrt, bias=misc[:, 15:16], scale=1.0 / 64)
                        recip(kr[:], kr[:])
                        tt("dve", kn[:].rr("p (h d) -> p h d", h=2), v[:, 640:768].rr("p (h d) -> p h d", h=2),
                           kr[:].unsqueeze(2).bc([128, 2, 64]), ALU.mult)
                        tt("dve", kn[:].rr("p (h d) -> p h d", h=2), kn[:].rr("p (h d) -> p h d", h=2),
                           gkb[:].unsqueeze(1).bc([128, 2, 64]), ALU.mult)
                        dma(o_bk[b, l, pr:pr + 128, :], kn[:], q="q1")
        P.barrier()


    def stage_attn(l):
        lam_init = 0.8 - 0.6 * math.exp(-0.3 * l)
        with ExitStack() as s:
            alb = sb(s, "alb", [128, 128]); dma(alb[:], a_lam[l, :].pb(128))
            al2 = sb(s, "al2", [128, 2, 32]); lamc = sb(s, "lamc", [128, 4])
            tt("dve", al2[:, 0, :], alb[:, 0:32], alb[:, 32:64], ALU.mult)
            tt("dve", al2[:, 1, :], alb[:, 64:96], alb[:, 96:128], ALU.mult)
            P.op("dve", lambda: nc.vector.tensor_reduce(out=lamc[:, 0:2].ap, in_=al2[:].ap, axis=AX.X, op=ALU.add), bufs(al2[:]), bufs(lamc[:]))
            act(lamc[:, 0:2], lamc[:, 0:2], AF.Exp)
            tt("dve", lamc[:, 2:3], lamc[:, 1:2], lamc[:, 0:1], ALU.subtract)
            ts("dve", lamc[:, 3:4], lamc[:, 2:3], -lam_init, None, ALU.add)
            gcol = sb(s, "gcol", [128, 3])
            for hh in range(2):
                dma(gcol[hh * 64:(hh + 1) * 64, 0:1], a_subln[l, :].rr("(d o) -> d o", o=1), allow_slow_non_contiguous=True)
                dma(gcol[hh * 64:(hh + 1) * 64, 1:2], b_qnorm[l, :].rr("(d o) -> d o", o=1), allow_slow_non_contiguous=True)
                dma(gcol[hh * 64:(hh + 1) * 64, 2:3], b_knorm[l, :].rr("(d o) -> d o", o=1), allow_slow_non_contiguous=True)
            ts("dve", gcol[:, 0:1], gcol[:, 0:1], 1.0 - lam_init, None, ALU.mult)
            KT = sb(s, "KT", [128, 3, LK], BF16)
            VP = sb(s, "VP", [128, LK // 128, 8, 128], BF16)
            memset("pool", VP[:], 1.0)
            xk = [sb(s, "at_xk%d" % i, [128, 4, 512]) for i in range(2)]
            rp = [sb(s, "at_rp%d" % i, [128, 4, 512]) for i in range(2)]
            sqb = sb(s, "at_sq", [128, 2, 512], BF16)
            rsb = sb(s, "at_rs", [128, 2, 512])
            tmp = sb(s, "at_tmp", [128, 512])
            vld = [sb(s, "at_v%d" % i, [128, 384]) for i in range(2)]
            ckl = sb(s, "at_ck", [128, 384])
            Qzs = [sb(s, "Qz%d" % i, [128, 12, 512], BF16) for i in range(2)]
            Pb = [sb(s, "at_P%d" % i, [128, 512], BF16) for i in range(3)]
            yab = sb(s, "at_ya", [128, 2, 512]); ybb = sb(s, "at_yb", [128, 2, 512])
            o1 = sb(s, "at_o1", [128, 512]); o2 = sb(s, "at_o2", [128, 512]); rr_ = sb(s, "at_rr", [128, 512])
            yob = sb(s, "at_yo", [128, 4, 512], BF16)
            pS = [pst(s, "at_pS%d" % i, [128, 512]) for i in range(3)]
            pA = [pst(s, "at_pA%d" % i, [128, 512]) for i in range(2)]
            pM = pst(s, "at_pM", [128, 512])

            def headnorm(x2, ntile, gidx, out2, n=256):
                act(sqb[:, 0:ntile, 0:n], x2, AF.Square)
                for t in range(ntile):
                    mm(pM[:, 0:n], cmb[:, M_BLK64, :], sqb[:, t, 0:n])
                    rstd_from_ps(rsb[:, t, 0:n], pM[:, 0:n])
                    stt("dve", out2[:, t, :], x2[:, t, :], gcol[:, gidx:gidx + 1], rsb[:, t, 0:n], ALU.mult, ALU.mult)

            def rope(xa, tab, midx, n=256):
                mm(pM[:, 0:n], cm[:, midx, :], xa)
                tt("dve", tmp[:, 0:n], pM[:, 0:n], tab[:, 1, :], ALU.mult)
                tt("dve", xa, xa, tab[:, 0, :], ALU.mult)
                tt("dve", xa, xa, tmp[:, 0:n], ALU.add)

            for (col0, Ls, ci, is_s, pidx) in seqs:
                nkt = (Ls + (256 if is_s else 0)) // 128
                for bb in range(Ls // 256):
                    b = (col0 + bb * 256) // 256
                    c0 = col0 + bb * 256
                    x = xk[bb % 2]
                    dma(x[:, 0:2, 0:256], PJ.k(b)[2 * 128:4 * 128, c0:c0 + 256].rr("(m p) n -> p m n", p=128))
                    dma(x[:, 2, 0:256], PJ.k(b)[6 * 128:7 * 128, c0:c0 + 256])
                    headnorm(x[:, 2:3, 0:256], 1, 2, x[:, 2:3, 0:256])
                    if is_s:
                        r_ = rp[bb % 2]
                        dma(r_[:, 0:2, 0:256], ropeA[:, :, bb * 256:(bb + 1) * 256].rr("c p n -> p c n"))
                        dma(r_[:, 2:4, 0:256], ropeB[:, :, bb * 256:(bb + 1) * 256].rr("c p n -> p c n"))
                        rope(x[:, 0, 0:256], r_[:, 0:2, 0:256], M_PERMA)
                        rope(x[:, 1, 0:256], r_[:, 0:2, 0:256], M_PERMA)
                        rope(x[:, 2, 0:256], r_[:, 2:4, 0:256], M_PERMB)
                    cp("act", KT[:, :, bb * 256:(bb + 1) * 256], x[:, 0:3, 0:256])
                    for t2 in range(2):
                        kt = bb * 2 + t2
                        v = vld[t2]
                        r0 = c0 + t2 * 128
                        dma(v[:], VT.k(b)[r0:r0 + 128, :])
                        for h in range(4):
                            o_ = 0 if h % 2 == 0 else 64
                            cp("pool", VP[:, kt, h, o_:o_ + 64], v[:, h * 64:(h + 1) * 64])
                        for g in range(2):
                            for par in range(2):
                                cp("pool", VP[:, kt, 4 + g * 2 + par, par * 64:par * 64 + 64], v[:, 256 + g * 64:256 + (g + 1) * 64])
                if is_s:
                    for t2 in range(2):
                        kt = Ls // 128 + t2
                        dma(ckl[:, 0:256], cak[l, t2 * 128:(t2 + 1) * 128, :])
                        dma(ckl[:, 256:384], cbk[l, t2 * 128:(t2 + 1) * 128, :])
                        for m in range(3):
                            P.op("pe", lambda: nc.tensor.transpose(pM[:, 0:128].ap, ckl[:, m * 128:(m + 1) * 128].ap, ident[:].ap),
                                 bufs(ckl[:], ident[:]), bufs(pM[:]))
                            cp("act", KT[:, m, Ls + t2 * 128:Ls + (t2 + 1) * 128], pM[:, 0:128])
                        v = vld[t2]
                        dma(v[:, 0:256], cav[l, t2 * 128:(t2 + 1) * 128, :])
                        dma(v[:, 256:384], cbv[l, t2 * 128:(t2 + 1) * 128, :])
                        for h in range(4):
                            o_ = 0 if h % 2 == 0 else 64
                            cp("pool", VP[:, kt, h, o_:o_ + 64], v[:, h * 64:(h + 1) * 64])
                        for g in range(2):
                            for par in range(2):
                                cp("pool", VP[:, kt, 4 + g * 2 + par, par * 64:par * 64 + 64], v[:, 256 + g * 64:256 + (g + 1) * 64])
                QB = 512 if is_s else 256
                cS = 0; cA = 0; cP = 0

                def prep_q(bb):
                    c0 = col0 + bb * QB
                    x = xk[bb % 2]
                    Qz = Qzs[bb % 2]
                    for h_ in range(QB // 256):
                        b = (c0 + h_ * 256) // 256
                        hs = slice(h_ * 256, (h_ + 1) * 256)
                        dma(x[:, 0:2, hs], PJ.k(b)[0:256, c0 + h_ * 256:c0 + (h_ + 1) * 256].rr("(m p) n -> p m n", p=128))
                        dma(x[:, 2:4, hs], PJ.k(b)[4 * 128:6 * 128, c0 + h_ * 256:c0 + (h_ + 1) * 256].rr("(m p) n -> p m n", p=128))
                    headnorm(x[:, 2:4, 0:QB], 2, 1, x[:, 2:4, 0:QB], n=QB)
                    if is_s:
                        r_ = rp[bb % 2]
                        dma(r_[:, 0:2, 0:QB], ropeA[:, :, bb * QB:(bb + 1) * QB].rr("c p n -> p c n"))
                        dma(r_[:, 2:4, 0:QB], ropeB[:, :, bb * QB:(bb + 1) * QB].rr("c p n -> p c n"))
                        rope(x[:, 0, 0:QB], r_[:, 0:2, 0:QB], M_PERMA, n=QB)
                        rope(x[:, 1, 0:QB], r_[:, 0:2, 0:QB], M_PERMA, n=QB)
                        rope(x[:, 2, 0:QB], r_[:, 2:4, 0:QB], M_PERMB, n=QB)
                        rope(x[:, 3, 0:QB], r_[:, 2:4, 0:QB], M_PERMB, n=QB)
                    jobs = []
                    for t in range(2):
                        for hl in range(2):
                            for half in range(2):
                                j = len(jobs)
                                ts("dve", Qz[:, j, 0:QB], x[:, t, 0:QB], misc[:, 2 + hl * 2 + half:3 + hl * 2 + half], None, ALU.mult)
                                jobs.append((j, t, t * 2 + hl, 32 ** -0.5))
                    for qh in range(4):
                        j = len(jobs)
                        t = qh % 2; g = qh // 2
                        ts("dve", Qz[:, j, 0:QB], x[:, 2 + t, 0:QB], misc[:, 10 + g:11 + g], None, ALU.mult)
                        jobs.append((j, 2, 4 + g * 2 + (qh % 2), 64 ** -0.5))
                    return jobs

                nQ = Ls // QB
                jobs_next = prep_q(0)
                for bb in range(nQ):
                    c0 = col0 + bb * QB
                    Qz = Qzs[bb % 2]
                    jobs = jobs_next
                    if bb + 1 < nQ:
                        jobs_next = prep_q(bb + 1)
                    steps = [(jb, kt) for jb in jobs for kt in range(nkt)]
                    Sbuf = {}
                    AHEAD = 2

                    def issue_S(i):
                        (j, ktile, var, scl), kt = steps[i]
                        S = pS[i % 3]
                        mm(S[:, 0:QB], KT[:, ktile, kt * 128:(kt + 1) * 128], Qz[:, j, 0:QB])

                    for i in range(min(AHEAD, len(steps))):
                        issue_S(i)
                    acc = None
                    for i, ((j, ktile, var, scl), kt) in enumerate(steps):
                        if i + AHEAD < len(steps):
                            issue_S(i + AHEAD)
                        if kt == 0:
                            acc = pA[cA % 2]; cA += 1
                        S = pS[i % 3]
                        pb_ = Pb[i % 3]
                        act(pb_[:, 0:QB], S[:, 0:QB], AF.Exp, scale=scl)
                        mm(acc[:, 0:QB], VP[:, kt, var, :], pb_[:, 0:QB], start=(kt == 0), stop=(kt == nkt - 1))
                        if kt != nkt - 1:
                            continue
                        par = (var % 2) if var < 4 else ((var - 4) % 2)
                        nr = slice(64 * par, 64 * par + 64); sr = slice(64 - 64 * par, 128 - 64 * par)
                        recip(rr_[nr, 0:QB], acc[sr, 0:QB])
                        if j < 8:
                            half = j % 2
                            dst = o1 if half == 0 else o2
                            tt("dve", dst[nr, 0:QB], acc[nr, 0:QB], rr_[nr, 0:QB], ALU.mult)
                            if half == 1:
                                t = j // 4
                                stt("dve", yab[nr, t, 0:QB], o2[nr, 0:QB], lamc[nr, 3:4], o1[nr, 0:QB], ALU.mult, ALU.add)
                        else:
                            qh = j - 8
                            tt("dve", ybb[nr, qh // 2, 0:QB], acc[nr, 0:QB], rr_[nr, 0:QB], ALU.mult)
                    headnorm(yab[:, :, 0:QB], 2, 0, yab[:, :, 0:QB], n=QB)
                    cp("act", yob[:, 0:2, 0:QB], yab[:, :, 0:QB])
                    cp("act", yob[:, 2:4, 0:QB], ybb[:, :, 0:QB])
                    for h_ in range(QB // 256):
                        b = (c0 + h_ * 256) // 256
                        dma(YC.k(("a", b))[0:512, c0 + h_ * 256:c0 + (h_ + 1) * 256].rr("(m p) n -> p m n", p=128),
                            yob[:, :, h_ * 256:(h_ + 1) * 256], q="q1")
        P.barrier()

    def stage_outproj(l, bg=None, bg_per_block=0):
        with ExitStack() as s:
            wo = load_weight_bf16(s, "wo", lambda k: w_out.k(0)[l, k * 128:(k + 1) * 128, :], D, 8)
            yc = [sb(s, "op_yc%d" % i, [128, 8, 256], BF16) for i in range(2)]
            xo = [sb(s, "op_x%d" % i, [128, 8, 256]) for i in range(2)]
            ys_ = [sb(s, "op_y%d" % i, [128, 8, 256]) for i in range(2)]; sqs_ = [sb(s, "op_sq%d" % i, [128, 8, 256], BF16) for i in range(2)]; rss_ = [sb(s, "op_rs%d" % i, [128, 256]) for i in range(2)]
            pps = [pst(s, "op_pp%d" % i, [128, 256]) for i in range(3)]
            pmss = [pst(s, "op_pms%d" % i, [128, 256]) for i in range(2)]
            cnt = 0
            for b in range(NBLK):
                c0 = b * 256; ci = 0 if b < 2 else 1
                a = yc[b % 2]; x = xo[b % 2]; y = ys_[b % 2]; sq = sqs_[b % 2]; rs = rss_[b % 2]; pms = pmss[b % 2]
                dma(a[:, 0:4, :], YC.k(("a", b))[0:512, c0:c0 + 256].rr("(m p) n -> p m n", p=128))
                dma(a[:, 4:6, :], YC.k(("c", b))[512:768, c0:c0 + 256].rr("(m p) n -> p m n", p=128))
                dma(a[:, 6:8, :], YC.k(("d", b))[768:1024, c0:c0 + 256].rr("(m p) n -> p m n", p=128))
                dma(x[:], XT[l].k(b)[:, c0:c0 + 256].rr("(k p) n -> p k n", p=128))
                for m in range(8):
                    pp = pps[cnt % 3]; cnt += 1
                    for k in range(8):
                        mm(pp[:], wo[:, k, m * 128:(m + 1) * 128], a[:, k, :], start=(k == 0), stop=(k == 7))
                    cp("act" if m % 2 else "dve", y[:, m, :], pp[:])
                act(sq[:], y[:], AF.Square)
                for k in range(8):
                    mm(pms[:], cmb[:, M_ONES1024, :], sq[:, k, :], start=(k == 0), stop=(k == 7))
                rstd_from_ps(rs[:], pms[:])
                tt("dve", y[:], y[:], rs[:].unsqueeze(1).bc([128, 8, 256]), ALU.mult)
                for k in range(8):
                    stt("dve", x[:, k, :], y[:, k, :], mod_sb[:, l, ci, 16 + k:17 + k], x[:, k, :], ALU.mult, ALU.add)
                dma(XM.k(b)[:, c0:c0 + 256].rr("(k p) n -> p k n", p=128), x[:], q="q1")
                if bg is not None:
                    for _ in range(bg_per_block):
                        next(bg, None)
            if bg is not None:
                for _ in bg:
                    pass
        P.barrier()

    def stage_ffn(l, wu):
        with ExitStack() as s:
            wd = load_weight_bf16(s, "wd", lambda k: w_down.k(0)[l, k * 128:(k + 1) * 128, :], D, 22)
            cw = sb(s, "ff_cw", [128, 3, 44]); cb = sb(s, "ff_cb", [128, 44])
            for w_ in range(3):
                dma(cw[:, w_, :], ffn_conv_w[l, w_, :].rr("(m p) -> p m", p=128), allow_slow_non_contiguous=True)
            dma(cb[:], ffn_conv_b[l, :].rr("(m p) -> p m", p=128), allow_slow_non_contiguous=True)
            xh = [sb(s, "ff_x%d" % i, [128, 8, 258]) for i in range(2)]
            sq = sb(s, "ff_sq", [128, 8, 258], BF16); rs = sb(s, "ff_rs", [128, 258])
            hn = sb(s, "ff_hn", [128, 8, 258]); h2 = sb(s, "ff_h2", [128, 8, 258], BF16)
            ug = [sb(s, "ff_ug%d" % i, [128, 258]) for i in range(2)]
            uv = [sb(s, "ff_uv%d" % i, [128, 258]) for i in range(2)]
            cg = sb(s, "ff_cg", [128, 256]); cv = sb(s, "ff_cv", [128, 256]); sg = sb(s, "ff_sg", [128, 256])
            aT = sb(s, "ff_a", [128, 22, 256], BF16)
            y = sb(s, "ff_y", [128, 8, 256])
            pms = pst(s, "ff_pms", [128, 258])
            pps = [pst(s, "ff_pp%d" % i, [128, 258]) for i in range(7)]
            cnt = 0
            for (col0, Ls, ci, is_s, pidx) in seqs:
                for bb in range(Ls // 256):
                    b = (col0 + bb * 256) // 256
                    c0 = col0 + bb * 256
                    x = xh[b % 2]
                    first = bb == 0; last = bb == Ls // 256 - 1
                    lo = 1 if first else 0; hi = 257 if last else 258
                    if first:
                        memset("pool", x[:, :, 0:1], 0.0)
                    if last:
                        memset("pool", x[:, :, 257:258], 0.0)
                    srcT = XM.k(b) if (first and last) else XM.k("all")
                    for bq in range(max(0, b - 1), min(NBLK, b + 2)):
                        pass
                    a_in = A(XM.t[:, c0 - 1 + lo:c0 - 1 + hi].rearrange("(k p) n -> p k n", p=128),
                             [XM.k(b).buf] + ([XM.k(b - 1).buf] if not first else []) + ([XM.k(b + 1).buf] if not last else []))
                    dma(x[:, :, lo:hi], a_in)
                    act(sq[:], x[:], AF.Square)
                    for k in range(8):
                        mm(pms[:], cmb[:, M_ONES1024, :], sq[:, k, :], start=(k == 0), stop=(k == 7))
                    rstd_from_ps(rs[:], pms[:])
                    tt("dve", hn[:], x[:], rs[:].unsqueeze(1).bc([128, 8, 258]), ALU.mult)
                    for k in range(8):
                        act(h2[:, k, :], hn[:, k, :], AF.Identity, bias=mod_sb[:, l, ci, 24 + k:25 + k], scale=mod_sb[:, l, ci, 32 + k:33 + k])
                    if first:
                        memset("pool", h2[:, :, 0:1], 0.0)
                    if last:
                        memset("pool", h2[:, :, 257:258], 0.0)
                    for m in range(22):
                        for (which, mt, ubuf, cdst) in ((0, m, ug[m % 2], cg), (1, 22 + m, uv[m % 2], cv)):
                            pp = pps[cnt % 7]; cnt += 1
                            for k in range(8):
                                mm(pp[:], wu[:, k, mt * 128:(mt + 1) * 128], h2[:, k, :], start=(k == 0), stop=(k == 7))
                            cp("act", ubuf[:], pp[:])
                            e_ = "dve"
                            ts(e_, cdst[:], ubuf[:, 0:256], cw[:, 0, mt:mt + 1], cb[:, mt:mt + 1], ALU.mult, ALU.add)
                            stt(e_, cdst[:], ubuf[:, 1:257], cw[:, 1, mt:mt + 1], cdst[:], ALU.mult, ALU.add)
                            stt(e_, cdst[:], ubuf[:, 2:258], cw[:, 2, mt:mt + 1], cdst[:], ALU.mult, ALU.add)
                        act(sg[:], cg[:], AF.Silu)
                        tt("dve", aT[:, m, :], sg[:], cv[:], ALU.mult)
                    for mo in range(8):
                        pp = pps[cnt % 7]; cnt += 1
                        for k in range(22):
                            mm(pp[:, 0:256], wd[:, k, mo * 128:(mo + 1) * 128], aT[:, k, :], start=(k == 0), stop=(k == 21))
                        cp("act" if mo % 2 else "dve", y[:, mo, :], pp[:, 0:256])
                    act(sq[:, :, 0:256], y[:], AF.Square)
                    for k in range(8):
                        mm(pms[:, 0:256], cmb[:, M_ONES1024, :], sq[:, k, 0:256], start=(k == 0), stop=(k == 7))
                    rstd_from_ps(rs[:, 0:256], pms[:, 0:256])
                    tt("dve", y[:], y[:], rs[:, 0:256].unsqueeze(1).bc([128, 8, 256]), ALU.mult)
                    for k in range(8):
                        stt("dve", y[:, k, :], y[:, k, :], mod_sb[:, l, ci, 40 + k:41 + k], x[:, k, 1:257], ALU.mult, ALU.add)
                    dma(XT[l + 1].k(b)[:, c0:c0 + 256].rr("(k p) n -> p k n", p=128), y[:], q="q1")
        P.barrier()


    def stage_ssd(l):
        with ExitStack() as s:
            cw = sb(s, "sd_cw", [128, 3, 4]); cb = sb(s, "sd_cb", [128, 4])
            for w_ in range(3):
                dma(cw[:, w_, :], c_conv_w[l, w_, :].rr("(m p) -> p m", p=128), allow_slow_non_contiguous=True)
            dma(cb[:], c_conv_b[l, :].rr("(m p) -> p m", p=128), allow_slow_non_contiguous=True)
            dtb = sb(s, "sd_dtb", [8, 1]); dma(dtb[:], c_dt_bias[l, :].rr("(d o) -> d o", o=1), allow_slow_non_contiguous=True)
            An = sb(s, "sd_An", [128, 8]); dma(An[:], c_a_log[l, :].pb(128))
            act(An[:], An[:], AF.Exp)
            ts("dve", An[:], An[:], -1.0, None, ALU.mult)
            Dsk = sb(s, "sd_D", [128, 4]); dma(Dsk[:], c_d[l, :].pb(128))
            cn = sb(s, "sd_cn", [128, 2]); dma(cn[:], c_norm[l, :].rr("(m p) -> p m", p=128), allow_slow_non_contiguous=True)
            onesf = sb(s, "sd_1", [128, 128]); memset("pool", onesf[:], 1.0)
            HT = [sb(s, "sd_HT%d" % i, [128, 2, 64]) for i in range(2)]
            HTb = [sb(s, "sd_HTb%d" % i, [128, 2, 64], BF16) for i in range(2)]
            ysum = sb(s, "sd_ys", [128, NS // 128, 256])
            u = sb(s, "sd_u", [128, 4, 130]); xc = sb(s, "sd_xc", [128, 4, 128])
            zT = sb(s, "sd_z", [128, 2, 128]); dT = sb(s, "sd_dT", [8, 128])
            Xt = sb(s, "sd_Xt", [128, 256]); Btb = sb(s, "sd_Btb", [128, 128], BF16); dtt = sb(s, "sd_dtt", [128, 8])
            at = sb(s, "sd_at", [128, 8]); cums = sb(s, "sd_cums", [128, 3, 8]); nac = sb(s, "sd_nac", [128, 8])
            ecol = sb(s, "sd_e", [128, 8]); dcol = sb(s, "sd_d", [128, 8]); cdec = sb(s, "sd_cd", [128, 8])
            BCb = sb(s, "sd_BCb", [128, 2, 128], BF16)
            BCm = sb(s, "sd_BCm", [128, 2, 2, 128], BF16)
            CBs = sb(s, "sd_CB", [128, 2, 128])
            abc4 = sb(s, "sd_abc", [128, 4, 128]); Dm4 = sb(s, "sd_Dm", [128, 4, 128]); Gb4 = sb(s, "sd_Gb", [128, 4, 128], BF16)
            Xdt4 = sb(s, "sd_Xdt", [128, 4, 64], BF16); Xw4 = sb(s, "sd_Xw", [128, 4, 64], BF16); tmpy4 = sb(s, "sd_ty", [128, 4, 64])
            gz = sb(s, "sd_gz", [128, 2, 128]); sqz = sb(s, "sd_sq", [128, 2, 128], BF16); rz = sb(s, "sd_rz", [128, 128])
            yo = sb(s, "sd_yo", [128, 2, 128], BF16); hfo = sb(s, "sd_hf", [64, 64])
            pT = pst(s, "sd_pT", [128, 512]); pC = pst(s, "sd_pC", [128, 3, 8]); pCB = pst(s, "sd_pCB", [128, 2, 128])
            pD4 = pst(s, "sd_pD", [128, 4, 128]); pY4 = pst(s, "sd_pY", [128, 4, 128]); pS4 = pst(s, "sd_pS4", [128, 4, 64])
            pF = pst(s, "sd_pF", [128, 2, 128]); pM = pst(s, "sd_pM", [128, 128])
            PJa = PJ.k("all")

            def tr(out, in_, idn):
                P.op("pe", lambda: nc.tensor.transpose(out.ap, in_.ap, idn.ap), bufs(in_, idn), bufs(out))

            for (col0, Ls, ci, is_s, pidx) in seqs:
                nC = Ls // 128
                for dr in range(2):
                    if is_s:
                        for h in range(4):
                            g = h // 2; hh = h % 2
                            dma(hfo[:], sssd[l, dr, h, :, :])
                            tr(pS4[0:64, 0, :], hfo[:], ident[0:64, 0:64])
                            cp("dve", HT[dr][64 * g:64 * g + 64, hh, :], pS4[0:64, 0, :])
                    else:
                        memset("pool", HT[dr][:], 0.0)
                    cp("act", HTb[dr][:], HT[dr][:])

                def prep(c):
                    g0 = col0 + c * 128
                    lo = 1 if c == 0 else 0
                    hi = 129 if c == nC - 1 else 130
                    if c == 0:
                        memset("pool", u[:, :, 0:1], 0.0)
                    if c == nC - 1:
                        memset("pool", u[:, :, 129:130], 0.0)
                    dma(u[:, :, lo:hi], PJa[9 * 128:13 * 128, g0 - 1 + lo:g0 - 1 + hi].rr("(m p) n -> p m n", p=128))
                    dma(zT[:], PJa[7 * 128:9 * 128, g0:g0 + 128].rr("(m p) n -> p m n", p=128))
                    dma(dT[:], PJa[13 * 128:13 * 128 + 8, g0:g0 + 128])
                    for m in range(4):
                        ts("dve", xc[:, m, :], u[:, m, 0:128], cw[:, 0, m:m + 1], cb[:, m:m + 1], ALU.mult, ALU.add)
                        stt("dve", xc[:, m, :], u[:, m, 1:129], cw[:, 1, m:m + 1], xc[:, m, :], ALU.mult, ALU.add)
                        stt("dve", xc[:, m, :], u[:, m, 2:130], cw[:, 2, m:m + 1], xc[:, m, :], ALU.mult, ALU.add)
                    act(xc[:], xc[:], AF.Silu)
                    act(dT[:], dT[:], AF.Exp, bias=dtb[:, 0:1], scale=1.0)
                    act(dT[:], dT[:], AF.Ln, bias=1.0, scale=1.0)
                    for m in range(3):
                        tr(pT[:, m * 128:(m + 1) * 128], xc[:, m, :], ident[:])
                    tr(pT[:, 384:392], dT[:], ident[0:8, 0:8])
                    cp("act", Xt[:], pT[:, 0:256])
                    cp("act", Btb[:], pT[:, 256:384])
                    cp("dve", dtt[:], pT[:, 384:392])
                    tt("dve", at[:], dtt[:], An[:], ALU.mult)
                    mm(pC[:, 0, :], cm[:, M_TRII, :], at[:])
                    mm(pC[:, 1, :], cm[:, M_TRIE, :], at[:])
                    mm(pC[:, 2, :], cm[:, M_ONES, :], at[:])
                    cp("dve", cums[:], pC[:])
                    ts("dve", nac[:], cums[:, 0, :], -1.0, None, ALU.mult)
                    act(ecol[:, 0:4], cums[:, 0, 0:4], AF.Exp)
                    tt("dve", ecol[:, 4:8], cums[:, 2, 4:8], cums[:, 1, 4:8], ALU.subtract)
                    act(ecol[:, 4:8], ecol[:, 4:8], AF.Exp)
                    tt("dve", dcol[:, 0:4], cums[:, 2, 0:4], cums[:, 0, 0:4], ALU.subtract)
                    act(dcol[:, 0:4], dcol[:, 0:4], AF.Exp)
                    act(dcol[:, 4:8], cums[:, 1, 4:8], AF.Exp)
                    act(cdec[:], cums[:, 2, :], AF.Exp)
                    cp("act", BCb[:], xc[:, 2:4, :])
                    for g in range(2):
                        ts("dve", BCm[:, 0, g, :], xc[:, 2, :], misc[:, 10 + g:11 + g], None, ALU.mult)
                        ts("dve", BCm[:, 1, g, :], xc[:, 3, :], misc[:, 10 + g:11 + g], None, ALU.mult)
                    for g in range(2):
                        mm(pCB[:, g, :], BCm[:, 0, g, :], BCb[:, 1, :])
                    cp("dve", CBs[:], pCB[:])

                def heads(c, dr, first_pass):
                    H4 = range(4)
                    gs_ = [slice(64 * (h // 2), 64 * (h // 2) + 64) for h in H4]
                    cols = [dr * 4 + h for h in H4]
                    for h in H4:
                        act(abc4[:, h, :], onesf[:], AF.Identity, scale=at[:, cols[h]:cols[h] + 1])
                    for h in H4:
                        if dr == 0:
                            mm(pD4[:, h, :], abc4[:, h, :], cm[:, M_TRII, :], start=True, stop=False)
                            mm(pD4[:, h, :], ident[:], cm[:, M_NEGF, :], start=False, stop=True)
                        else:
                            mm(pD4[:, h, :], abc4[:, h, :], cm[:, M_TRIE, :], start=True, stop=False)
                            mm(pD4[:, h, :], ident[:], cm[:, M_POSB, :], start=False, stop=True)
                    for h in H4:
                        if dr == 0:
                            act(Dm4[:, h, :], pD4[:, h, :], AF.Exp, bias=nac[:, cols[h]:cols[h] + 1], scale=1.0)
                        else:
                            act(Dm4[:, h, :], pD4[:, h, :], AF.Exp, bias=cums[:, 1, cols[h]:cols[h] + 1], scale=-1.0)
                    for h in H4:
                        tt("dve", Gb4[:, h, :], CBs[:, h // 2, :], Dm4[:, h, :], ALU.mult)
                        ts("dve", Xdt4[:, h, :], Xt[:, h * 64:(h + 1) * 64], dtt[:, cols[h]:cols[h] + 1], None, ALU.mult)
                    for h in H4:
                        mm(pY4[:, h, 0:64], Gb4[:, h, :], Xdt4[:, h, :])
                        mm(pY4[:, h, 64:128], BCm[:, 1, h // 2, :], HTb[dr][:, h % 2, :])
                    for h in H4:
                        act(tmpy4[:, h, :], pY4[:, h, 64:128], AF.Identity, scale=ecol[:, cols[h]:cols[h] + 1])
                    for h in H4:
                        ysl = ysum[:, c, h * 64:(h + 1) * 64]
                        if first_pass:
                            tt("dve", ysl, tmpy4[:, h, :], pY4[:, h, 0:64], ALU.add)
                            stt("dve", ysl, Xt[:, h * 64:(h + 1) * 64], Dsk[:, h:h + 1], ysl, ALU.mult, ALU.add)
                        else:
                            tt("dve", ysl, ysl, tmpy4[:, h, :], ALU.add)
                            tt("dve", ysl, ysl, pY4[:, h, 0:64], ALU.add)
                    for h in H4:
                        ts("dve", Xw4[:, h, :], Xdt4[:, h, :], dcol[:, cols[h]:cols[h] + 1], None, ALU.mult)
                    for h in H4:
                        mm(pS4[:, h, :], Btb[:], Xw4[:, h, :])
                    for h in H4:
                        stt("dve", HT[dr][gs_[h], h % 2, :], HT[dr][gs_[h], h % 2, :], cdec[gs_[h], cols[h]:cols[h] + 1], pS4[gs_[h], h, :], ALU.mult, ALU.add)
                    for h in H4:
                        cp("act", HTb[dr][gs_[h], h % 2, :], HT[dr][gs_[h], h % 2, :])

                for c in range(nC):
                    prep(c)
                    heads(c, 0, True)
                for c in range(nC - 1, -1, -1):
                    prep(c)
                    heads(c, 1, False)
                    g0 = col0 + c * 128
                    b = g0 // 256
                    for m in range(2):
                        tr(pF[:, m, :], ysum[:, c, m * 128:(m + 1) * 128], ident[:])
                    act(gz[:], zT[:], AF.Silu)
                    tt("dve", gz[:], gz[:], pF[:], ALU.mult)
                    act(sqz[:], gz[:], AF.Square)
                    for m in range(2):
                        mm(pM[:], cmb[:, M_ONES256, :], sqz[:, m, :], start=(m == 0), stop=(m == 1))
                    rstd_from_ps(rz[:], pM[:])
                    for m in range(2):
                        stt("dve", yo[:, m, :], gz[:, m, :], cn[:, m:m + 1], rz[:], ALU.mult, ALU.mult)
                    dma(YC.k(("c", b))[512:768, g0:g0 + 128].rr("(m p) n -> p m n", p=128), yo[:], q="q1")
                if not is_s:
                    for dr in range(2):
                        for h in range(4):
                            g = h // 2; hh = h % 2
                            gs = slice(64 * g, 64 * g + 64)
                            tr(pM[0:64, :], HT[dr][:, hh, :], ident[:])
                            cp("dve", hfo[:], pM[0:64, 64 * g:64 * g + 64])
                            dma(o_ssd[pidx, l, dr, h, :, :], hfo[:], q="q1")
        P.barrier()

    def stage_s5(l):
        TWO_PI = 2.0 * PI
        with ExitStack() as s:
            W1 = [Pair([sb(s, "s5_W1_%d%d" % (d, ri), [128, 1024]) for ri in range(2)]) for d in range(2)]
            W2 = [Pair([sb(s, "s5_W2_%d%d" % (d, ri), [128, 8, 136]) for ri in range(2)]) for d in range(2)]
            BD = [[[sb(s, "s5_BD%d%d%d" % (d, ct, ri), [128, 512], BF16) for ri in range(2)] for ct in range(2)] for d in range(2)]
            Cm = [[sb(s, "s5_Cm%d%d" % (d, ri), [128, 8, 128], BF16) for ri in range(2)] for d in range(2)]
            ddc = sb(s, "s5_dd", [128, 2]); dma(ddc[:], d_d[l, :].rr("(m p) -> p m", p=128), allow_slow_non_contiguous=True)
            wg = load_weight_bf16(s, "s5_wg", lambda k: d_glu.k(0)[l, k * 128:(k + 1) * 128, :], 512, 2, chunk=512)

            def tr(out, in_, idn):
                P.op("pe", lambda: nc.tensor.transpose(out.ap, in_.ap, idn.ap), bufs(in_, idn), bufs(out))

            with ExitStack() as s2:
                W = 1032
                tf = sb(s2, "s5_tf", [128, W]); ti = sb(s2, "s5_ti", [128, W], I32); tm = sb(s2, "s5_tm", [128, W])
                rr_ = sb(s2, "s5_r", [128, W]); ph = sb(s2, "s5_ph", [128, W]); mg = sb(s2, "s5_mg", [128, W])
                sn = sb(s2, "s5_sn", [128, W]); cs = sb(s2, "s5_cs", [128, W])
                lre = sb(s2, "s5_lre", [128, 1024]); lim = sb(s2, "s5_lim", [128, 1024]); dl = sb(s2, "s5_dl", [128, 16])
                lrc = sb(s2, "s5_lrc", [128, 8]); lic = sb(s2, "s5_lic", [128, 8]); dlc = sb(s2, "s5_dlc", [128, 8])
                er = sb(s2, "s5_er", [128, 2, 129])
                for d in range(2):
                    dma(er[:, d, :], c_erow[d, :, :])
                Bn = sb(s2, "s5_Bn", [64, 16, 32]); Cn = sb(s2, "s5_Cn", [128, 2, 128])
                lr_r = sb(s2, "s5_lrr", [128, 64]); li_r = sb(s2, "s5_lir", [128, 64]); dl_r = sb(s2, "s5_dlr", [128, 1])
                q = [sb(s2, "s5_q%d" % i, [128, 64]) for i in range(10)]
                tmpC = sb(s2, "s5_tmpC", [128, 128])
                pTr = pst(s2, "s5_pTr", [128, 128])

                def trig(n):
                    for dst, off in ((sn, 0.0), (cs, PI / 2)):
                        ts("dve", rr_[:, 0:n], ph[:, 0:n], off, None, ALU.add)
                        ts("dve", tf[:, 0:n], rr_[:, 0:n], 1.0 / TWO_PI, None, ALU.mult)
                        cp("dve", ti[:, 0:n], tf[:, 0:n])
                        cp("dve", tf[:, 0:n], ti[:, 0:n])
                        stt("dve", rr_[:, 0:n], tf[:, 0:n], -TWO_PI, rr_[:, 0:n], ALU.mult, ALU.add)
                        ts("dve", tm[:, 0:n], rr_[:, 0:n], PI, -TWO_PI, ALU.is_gt, ALU.mult)
                        tt("dve", rr_[:, 0:n], rr_[:, 0:n], tm[:, 0:n], ALU.add)
                        ts("dve", tm[:, 0:n], rr_[:, 0:n], -PI, TWO_PI, ALU.is_lt, ALU.mult)
                        tt("dve", rr_[:, 0:n], rr_[:, 0:n], tm[:, 0:n], ALU.add)
                        act(dst[:, 0:n], rr_[:, 0:n], AF.Sin)

                for d in range(2):
                    dma(lre[:], d_lam_re[l, d, :, :].rr("g n -> (g n)").pb(128))
                    dma(lim[:], d_lam_im[l, d, :, :].rr("g n -> (g n)").pb(128))
                    dma(dl[:], d_log_step[l, d, :].pb(128))
                    act(dl[:], dl[:], AF.Exp)
                    dlb = dl[:].unsqueeze(2).bc([128, 16, 64])
                    tt("dve", lre[:].rr("p (g n) -> p g n", g=16), lre[:].rr("p (g n) -> p g n", g=16), dlb, ALU.mult)
                    tt("dve", lim[:].rr("p (g n) -> p g n", g=16), lim[:].rr("p (g n) -> p g n", g=16), dlb, ALU.mult)
                    ts("dve", ph[:, 0:1024], lim[:], misc[:, d:d + 1], None, ALU.mult)
                    act(mg[:, 0:1024], lre[:], AF.Exp, scale=misc[:, 16 + d:17 + d])
                    trig(1024)
                    tt("dve", W1[d][:, 0, :], mg[:, 0:1024], cs[:, 0:1024], ALU.mult)
                    stt("dve", W1[d][:, 1, :], mg[:, 0:1024], -1.0, sn[:, 0:1024], ALU.mult, ALU.mult)
                    for t in range(8):
                        dma(lrc[:, t:t + 1], d_lam_re[l, d, 2 * t:2 * t + 2, :].rr("g n -> (g n)").rr("(p o) -> p o", o=1), allow_slow_non_contiguous=True)
                        dma(lic[:, t:t + 1], d_lam_im[l, d, 2 * t:2 * t + 2, :].rr("g n -> (g n)").rr("(p o) -> p o", o=1), allow_slow_non_contiguous=True)
                        for g2 in range(2):
                            dma(dlc[64 * g2:64 * g2 + 64, t:t + 1], d_log_step[l, d, 2 * t + g2:2 * t + g2 + 1].pb(64))
                    act(dlc[:], dlc[:], AF.Exp)
                    tt("dve", lrc[:], lrc[:], dlc[:], ALU.mult)
                    tt("dve", lic[:], lic[:], dlc[:], ALU.mult)
                    erb = er[:, d, :].unsqueeze(1).bc([128, 8, 129])
                    tt("dve", ph[:, 0:1032].rr("p (t c) -> p t c", t=8), lic[:].unsqueeze(2).bc([128, 8, 129]), erb, ALU.mult)
                    tt("dve", mg[:, 0:1032].rr("p (t c) -> p t c", t=8), lrc[:].unsqueeze(2).bc([128, 8, 129]), erb, ALU.mult)
                    act(mg[:, 0:1032], mg[:, 0:1032], AF.Exp)
                    trig(1032)
                    tt("dve", W2[d][:, 0, :, 0:129], mg[:, 0:1032].rr("p (t c) -> p t c", t=8), cs[:, 0:1032].rr("p (t c) -> p t c", t=8), ALU.mult)
                    tt("dve", W2[d][:, 1, :, 0:129], mg[:, 0:1032].rr("p (t c) -> p t c", t=8), sn[:, 0:1032].rr("p (t c) -> p t c", t=8), ALU.mult)
                    dma(Bn[:], d_b[l, d, :, :, :, :].rr("g n c r -> n g (c r)"))
                    for ct in range(2):
                        dma(lr_r[:], d_lam_re[l, d, 8 * ct:8 * ct + 8, :].unsqueeze(1).bc([8, 16, 64]))
                        dma(li_r[:], d_lam_im[l, d, 8 * ct:8 * ct + 8, :].unsqueeze(1).bc([8, 16, 64]))
                        dma(dl_r[:], d_log_step[l, d, 8 * ct:8 * ct + 8].unsqueeze(1).unsqueeze(2).bc([8, 16, 1]))
                        act(dl_r[:], dl_r[:], AF.Exp)
                        ts("dve", ph[:, 0:64], li_r[:], dl_r[:, 0:1], None, ALU.mult)
                        act(mg[:, 0:64], lr_r[:], AF.Exp, scale=dl_r[:, 0:1])
                        trig(64)
                        tt("dve", q[0][:], mg[:, 0:64], cs[:, 0:64], ALU.mult)
                        ts("dve", q[0][:], q[0][:], -1.0, None, ALU.add)
                        tt("dve", q[1][:], mg[:, 0:64], sn[:, 0:64], ALU.mult)
                        tt("dve", q[2][:], lr_r[:], lr_r[:], ALU.mult)
                        tt("dve", q[3][:], li_r[:], li_r[:], ALU.mult)
                        tt("dve", q[2][:], q[2][:], q[3][:], ALU.add)
                        recip(q[2][:], q[2][:])
                        tt("dve", q[3][:], q[0][:], lr_r[:], ALU.mult)
                        tt("dve", q[4][:], q[1][:], li_r[:], ALU.mult)
                        tt("dve", q[3][:], q[3][:], q[4][:], ALU.add)
                        tt("dve", q[3][:], q[3][:], q[2][:], ALU.mult)
                        tt("dve", q[4][:], q[1][:], lr_r[:], ALU.mult)
                        tt("dve", q[5][:], q[0][:], li_r[:], ALU.mult)
                        tt("dve", q[4][:], q[4][:], q[5][:], ALU.subtract)
                        tt("dve", q[4][:], q[4][:], q[2][:], ALU.mult)
                        for r in range(2):
                            tr(pTr[:, 0:64], Bn[:, 8 * ct:8 * ct + 8, r:32:2], ident[0:64, 0:64])
                            cp("dve", q[6 + r][:], pTr[:, 0:64])
                        tt("dve", q[8][:], q[3][:], q[6][:], ALU.mult)
                        tt("dve", q[9][:], q[4][:], q[7][:], ALU.mult)
                        tt("dve", q[8][:], q[8][:], q[9][:], ALU.subtract)
                        tt("dve", q[9][:], q[3][:], q[7][:], ALU.mult)
                        tt("dve", q[5][:], q[4][:], q[6][:], ALU.mult)
                        tt("dve", q[9][:], q[9][:], q[5][:], ALU.add)
                        for ri in range(2):
                            tt("dve", BD[d][ct][ri][:].rr("p (g n) -> p g n", g=8), q[8 + ri][:].unsqueeze(1).bc([128, 8, 64]),
                               misc[:, 20:28].unsqueeze(2).bc([128, 8, 64]), ALU.mult)
                    dma(Cn[:], d_c[l, d, :, :].rr("(ct p) k -> p ct k", p=128))
                    for ct in range(2):
                        for ri in range(2):
                            tr(pTr[0:64, :], Cn[:, ct, ri:128:2], ident[:])
                            sgn = 1.0 if ri == 0 else -1.0
                            ts("dve", tmpC[0:64, :], pTr[0:64, :], sgn, None, ALU.mult)
                            ts("dve", tmpC[64:128, :], pTr[0:64, :], sgn, None, ALU.mult)
                            for tl in range(4):
                                t = 4 * ct + tl
                                for g2 in range(2):
                                    gl = 2 * tl + g2
                                    ps_ = slice(64 * g2, 64 * g2 + 64)
                                    tt("dve", Cm[d][ri][ps_, t, :], tmpC[ps_, :], m16[ps_, gl, :], ALU.mult)
                P.barrier()

            ub = sb(s, "s5_ub", [128, 2, NS], BF16)
            ys = sb(s, "s5_ys", [128, 2, NS])
            SL = []
            for ct in range(2):
                SL.append(dict(
                    bu=Pair([sb(s, "s5_bu%d%d" % (ct, ri), [128, 512]) for ri in range(2)]),
                    tA=[sb(s, "s5_t%d%d" % (ct, i), [128, 512]) for i in range(4)],
                    Z=Pair([sb(s, "s5_Z%d%d" % (ct, ri), [128, 512], BF16) for ri in range(2)]),
                    Sc=Pair([sb(s, "s5_Sc%d%d" % (ct, ri), [128, 4, 128]) for ri in range(2)]),
                    X=Pair([sb(s, "s5_X%d%d" % (ct, ri), [128, 4, 128], BF16) for ri in range(2)]),
                    cq=[sb(s, "s5_cq%d%d" % (ct, i), [128, 4]) for i in range(4)]))
            car = [[sb(s, "s5_car%d%d" % (d, ct), [128, 2, 4]) for ct in range(2)] for d in range(2)]
            h0t = sb(s, "s5_h0", [128, 2, 8]); cq0 = [sb(s, "s5_cqi%d" % i, [128, 4]) for i in range(2)]
            xfin = [sb(s, "s5_xf%d" % d, [128, 8, 2]) for d in range(2)]
            uf = sb(s, "s5_uf", [128, 2, 256]); ubs = sb(s, "s5_ubs", [128, 2, 256])
            yv = sb(s, "s5_yv", [128, 2, 256]); y2 = sb(s, "s5_y2", [128, 2, 256]); yg = sb(s, "s5_yg", [128, 2, 256], BF16)
            sgm = sb(s, "s5_sg", [128, 2, 256]); yd = sb(s, "s5_yd", [128, 2, 256], BF16)
            PJa = PJ.k("all")
            v4 = lambda a_: a_.rr("p (a b) -> p a b", a=4)
            for (col0, Ls, ci, is_s, pidx) in seqs:
                nC = Ls // 128
                for bb in range(Ls // 256):
                    dma(ubs[:], PJa[14 * 128:16 * 128, col0 + bb * 256:col0 + (bb + 1) * 256].rr("(m p) n -> p m n", p=128))
                    cp("act", ub[:, :, bb * 256:(bb + 1) * 256], ubs[:])
                for d in range(2):
                    a1c = 1 if d == 0 else 126
                    if is_s:
                        for t in range(8):
                            for r in range(2):
                                dma(h0t[:, r, t:t + 1], A(ss5.t[l, d, 2 * t:2 * t + 2, :, r].rearrange("g n -> (g n)").rearrange("(p o) -> p o", o=1), [ss5.buf]),
                                    allow_slow_non_contiguous=True)
                        for ct in range(2):
                            tsl = slice(4 * ct, 4 * ct + 4)
                            tt("dve", cq0[0][:], W2[d][:, 0, tsl, a1c], h0t[:, 0, tsl], ALU.mult)
                            tt("dve", cq0[1][:], W2[d][:, 1, tsl, a1c], h0t[:, 1, tsl], ALU.mult)
                            tt("dve", car[d][ct][:, 0, :], cq0[0][:], cq0[1][:], ALU.subtract)
                            tt("dve", cq0[0][:], W2[d][:, 0, tsl, a1c], h0t[:, 1, tsl], ALU.mult)
                            tt("dve", cq0[1][:], W2[d][:, 1, tsl, a1c], h0t[:, 0, tsl], ALU.mult)
                            tt("dve", car[d][ct][:, 1, :], cq0[0][:], cq0[1][:], ALU.add)
                    else:
                        for ct in range(2):
                            memset("pool", car[d][ct][:], 0.0)
                with ExitStack() as s3:
                    pBU = [pst(s3, "s5_pBU%d" % i, [128, 512]) for i in range(2)]
                    pSS = [pst(s3, "s5_pS%d" % i, [128, 2, 4, 128]) for i in range(2)]
                    pYY = [pst(s3, "s5_pY%d" % i, [128, 128]) for i in range(2)]
                    for d in range(2):
                        order = range(nC) if d == 0 else range(nC - 1, -1, -1)
                        last = 127 if d == 0 else 0
                        tri = M_TRII if d == 0 else M_TRIGE
                        for c in order:
                            is_final = (c == nC - 1) if d == 0 else (c == 0)
                            C2 = range(2)
                            for ri in range(2):
                                for ct in C2:
                                    mm(pBU[ct][:], ub[:, ct, c * 128:(c + 1) * 128], BD[d][ct][ri][:])
                                for ct in C2:
                                    cp("act", SL[ct]["bu"][:, ri, :], pBU[ct][:])
                            for ct in C2:
                                bu = SL[ct]["bu"]; tA = SL[ct]["tA"]
                                w1r = W1[d][:, 0, ct * 512:(ct + 1) * 512]; w1i = W1[d][:, 1, ct * 512:(ct + 1) * 512]
                                tt("dve", tA[0][:], bu[:, 0, :], w1r, ALU.mult)
                                tt("dve", tA[1][:], bu[:, 1, :], w1i, ALU.mult)
                                tt("dve", tA[2][:], bu[:, 0, :], w1i, ALU.mult)
                                tt("dve", tA[3][:], bu[:, 1, :], w1r, ALU.mult)
                            for ct in C2:
                                tA = SL[ct]["tA"]; Z = SL[ct]["Z"]
                                tt("pool", Z[:, 0, :], tA[0][:], tA[1][:], ALU.subtract)
                                tt("pool", Z[:, 1, :], tA[2][:], tA[3][:], ALU.add)
                            for ct in C2:
                                Z = SL[ct]["Z"]
                                for ri in range(2):
                                    for tl in range(4):
                                        mm(pSS[ct][:, ri, tl, :], Z[:, ri, tl * 128:(tl + 1) * 128], cmb[:, tri, :])
                            for ct in C2:
                                Sc = SL[ct]["Sc"]; cr = car[d][ct]
                                for ri in range(2):
                                    tt("dve", Sc[:, ri, :, :], pSS[ct][:, ri, :, :], cr[:, ri, :].unsqueeze(2).bc([128, 4, 128]), ALU.add)
                            for ct in C2:
                                Sc = SL[ct]["Sc"]; cq = SL[ct]["cq"]; cr = car[d][ct]
                                for (colx, dst_re, dst_im, doit) in ((127 if d == 0 else 0, xfin[d][:, 4 * ct:4 * ct + 4, 0], xfin[d][:, 4 * ct:4 * ct + 4, 1], is_final and not is_s),
                                                                     (128, cr[:, 0, :], cr[:, 1, :], True)):
                                    if not doit:
                                        continue
                                    ar = W2[d][:, 0, 4 * ct:4 * ct + 4, colx]; ai = W2[d][:, 1, 4 * ct:4 * ct + 4, colx]
                                    sr = Sc[:, 0, :, last]; si = Sc[:, 1, :, last]
                                    tt("dve", cq[0][:], ar, sr, ALU.mult)
                                    tt("dve", cq[1][:], ai, si, ALU.mult)
                                    tt("dve", cq[2][:], ar, si, ALU.mult)
                                    tt("dve", cq[3][:], ai, sr, ALU.mult)
                                    tt("dve", dst_re, cq[0][:], cq[1][:], ALU.subtract)
                                    tt("dve", dst_im, cq[2][:], cq[3][:], ALU.add)
                            for ct in C2:
                                Sc = SL[ct]["Sc"]; tA = SL[ct]["tA"]
                                w2r = W2[d][:, 0, 4 * ct:4 * ct + 4, 0:128]; w2i = W2[d][:, 1, 4 * ct:4 * ct + 4, 0:128]
                                tt("dve", v4(tA[0][:]), w2r, Sc[:, 0, :, :], ALU.mult)
                                tt("dve", v4(tA[1][:]), w2i, Sc[:, 1, :, :], ALU.mult)
                                tt("dve", v4(tA[2][:]), w2r, Sc[:, 1, :, :], ALU.mult)
                                tt("dve", v4(tA[3][:]), w2i, Sc[:, 0, :, :], ALU.mult)
                            for ct in C2:
                                tA = SL[ct]["tA"]; X = SL[ct]["X"]
                                tt("pool", X[:, 0, :, :], v4(tA[0][:]), v4(tA[1][:]), ALU.subtract)
                                tt("pool", X[:, 1, :, :], v4(tA[2][:]), v4(tA[3][:]), ALU.add)
                            for ct in C2:
                                X = SL[ct]["X"]
                                i_ = 0
                                for ri in range(2):
                                    for tl in range(4):
                                        mm(pYY[ct][:], Cm[d][ri][:, 4 * ct + tl, :], X[:, ri, tl, :], start=(i_ == 0), stop=(i_ == 7))
                                        i_ += 1
                            for ct in C2:
                                ysl = ys[:, ct, c * 128:(c + 1) * 128]
                                if d == 0:
                                    cp("act", ysl, pYY[ct][:])
                                else:
                                    tt("dve", ysl, ysl, pYY[ct][:], ALU.add)
                if not is_s:
                    for d in range(2):
                        for t in range(8):
                            for r in range(2):
                                dma(A(o_s5.t[pidx, l, d, 2 * t:2 * t + 2, :, r].rearrange("g n -> (g n)").rearrange("(p o) -> p o", o=1), [o_s5.buf]),
                                    xfin[d][:, t, r:r + 1], q="q1", allow_slow_non_contiguous=True)
                with ExitStack() as s3:
                    pG = pst(s3, "s5_pG", [128, 4, 256])
                    for bb in range(Ls // 256):
                        c0 = col0 + bb * 256
                        b = c0 // 256
                        sl = slice(bb * 256, (bb + 1) * 256)
                        dma(uf[:], PJa[14 * 128:16 * 128, c0:c0 + 256].rr("(m p) n -> p m n", p=128))
                        for ct in range(2):
                            stt("dve", yv[:, ct, :], uf[:, ct, :], ddc[:, ct:ct + 1], ys[:, ct, sl], ALU.mult, ALU.add)
                        act(y2[:], yv[:], AF.Square)
                        ts("dve", y2[:], y2[:], 0.044715, 1.0, ALU.mult, ALU.add)
                        tt("dve", y2[:], y2[:], yv[:], ALU.mult)
                        act(sgm[:], y2[:], AF.Sigmoid, scale=1.5957691216057308)
                        tt("dve", yg[:], yv[:], sgm[:], ALU.mult)
                        for mo in range(4):
                            for k in range(2):
                                mm(pG[:, mo, :], wg[:, k, mo * 128:(mo + 1) * 128], yg[:, k, :], start=(k == 0), stop=(k == 1))
                        act(sgm[:], pG[:, 2:4, :], AF.Sigmoid)
                        tt("dve", yd[:], sgm[:], pG[:, 0:2, :], ALU.mult)
                        dma(YC.k(("d", b))[768:1024, c0:c0 + 256].rr("(m p) n -> p m n", p=128), yd[:], q="q1")
        P.barrier()

    def stage_stub(l, which):
        with ExitStack() as s:
            z = sb(s, "stubz", [128, 2, 256], BF16)
            memset("pool", z[:], 0.0)
            r0 = 512 if which == "c" else 768
            for b in range(NBLK):
                c0 = b * 256
                dma(YC.k((which, b))[r0:r0 + 256, c0:c0 + 256].rr("(m p) n -> p m n", p=128), z[:], q="q1")
        P.barrier()

    P.marks = []

    def mark(name):
        P.marks.append((name, P.engs["pe"].n, P.engs["act"].n, P.engs["dve"].n))

    stage_mod(); mark("mod")
    stage_t_in(); mark("t_in")
    for l in range(L):
        stage_inproj(l); mark('inproj%d' % l)
        stage_attn(l); mark('attn%d' % l)
        if STUB_C:
            stage_stub(l, "c")
        else:
            stage_ssd(l); mark('ssd%d' % l)
        if STUB_D:
            stage_stub(l, "d")
        else:
            stage_s5(l); mark('s5_%d' % l)
        with ExitStack() as sw:
            with ExitStack() as sw2:
                wu_, g_ = weight_loader_gen(sw, sw2, "wu", lambda k: w_up.k(0)[l, k * 128:(k + 1) * 128, :], 5632, 8)
                stage_outproj(l, bg=g_, bg_per_block=3); mark('outproj%d' % l)
            stage_ffn(l, wu_); mark('ffn%d' % l)
    stage_t_out(); mark('t_out')
    P.barrier()
    es.close()
    return nc, P


def _consts(NS):
    p = np.arange(128)
    misc = np.zeros((128, 64), np.float32)
    misc[:, 0] = p
    misc[:, 1] = 127 - p
    for hl in range(2):
        for half in range(2):
            lo = 64 * hl + 32 * half
            misc[lo:lo + 32, 2 + hl * 2 + half] = 1.0
    misc[0:64, 10] = 1.0
    misc[64:128, 11] = 1.0
    misc[:, 13] = 1.0
    misc[:, 15] = EPS
    mats = np.zeros((11, 128, 128), np.float32)
    mats[0] = 1.0 / 1024
    mats[1] = (p[:, None] // 64 == p[None, :] // 64) / 64.0
    mats[2] = 1.0 / 256
    pa = (p // 32) * 32 + ((p % 32) + 16) % 32
    pb = (p // 64) * 64 + ((p % 64) + 32) % 64
    mats[3][pa, p] = 1.0
    mats[4][pb, p] = 1.0
    mats[5] = (p[:, None] <= p[None, :])
    mats[6] = (p[:, None] < p[None, :])
    mats[7] = np.where(p[None, :] < p[:, None], -30000.0, 0.0)
    mats[8] = np.where(p[:, None] < p[None, :], 30000.0, 0.0)
    mats[9] = 1.0
    mats[10] = (p[:, None] >= p[None, :])
    misc[:, 16] = -p
    misc[:, 17] = -(127 - p)
    for q in range(8):
        misc[q * 16:(q + 1) * 16, 20 + q] = 1.0
    erow = np.zeros((2, 128, 129), np.float32)
    erow[0, :, 0:128] = np.arange(128)[None, :]
    erow[1, :, 0:128] = (127 - np.arange(128))[None, :]
    erow[:, :, 128] = 128.0
    m16 = np.zeros((128, 8, 128), np.float32)
    for q in range(8):
        m16[:, q, q * 16:(q + 1) * 16] = 1.0
    t = np.arange(NS)
    row = (t // 64).astype(np.float32)
    col = (t % 64).astype(np.float32)

    def ang(n):
        fr = (np.float32(10000.0) ** (-np.arange(n, dtype=np.float32) / np.float32(n))).astype(np.float32)
        return np.concatenate([row[:, None] * fr, col[:, None] * fr], axis=-1).astype(np.float32)

    aA = ang(8)
    aB = ang(16)
    ropeA = np.zeros((2, 128, NS), np.float32)
    ropeB = np.zeros((2, 128, NS), np.float32)
    for i in range(128):
        idx = i % 32
        ropeA[0, i] = np.cos(aA[:, idx % 16])
        ropeA[1, i] = np.sin(aA[:, idx % 16]) * (-1.0 if idx < 16 else 1.0)
        idx = i % 64
        ropeB[0, i] = np.cos(aB[:, idx % 32])
        ropeB[1, i] = np.sin(aB[:, idx % 32]) * (-1.0 if idx < 32 else 1.0)
    return dict(c_ident=np.eye(128, dtype=np.float32), c_misc=misc, c_mats=mats, c_m16=m16, ropeA=ropeA, ropeB=ropeB, c_erow=erow)


def _win_cols():
    aq, ak, av, bq, bk, bv, cz, cx, cb, cc, cdt, du = 0, 256, 512, 768, 1024, 1152, 1280, 1536, 1792, 1920, 2048, 2056
    r = lambda a, n: list(range(a, a + n))
    cols = r(aq, 256) + r(ak, 256)
    cols += r(bq, 64) + r(bq + 128, 64) + r(bq + 64, 64) + r(bq + 192, 64)
    cols += r(bk, 128) + r(cz, 256) + r(cx, 256) + r(cb, 128) + r(cc, 128) + r(cdt, 8) + r(du, 256)
    cols += r(av, 256) + r(bv, 128) + r(ak, 256) + r(bk, 128)
    return np.array(cols)


def make_in_maps(inp, n_cores, NS, prompt_per_core=2):
    f = lambda a: np.ascontiguousarray(np.asarray(a, dtype=np.float32))
    L = DEPTH
    consts = _consts(NS)
    shared = dict(
        w_mod=f(inp["w_mod"]), b_mod=f(inp["b_mod"]), g_pre1=f(inp["g_pre1"]), g_post1=f(inp["g_post1"]),
        g_pre2=f(inp["g_pre2"]), g_post2=f(inp["g_post2"]), w_in=f(np.asarray(inp["w_in"])[:, :, _win_cols()]),
        a_lam=f(np.asarray(inp["a_lam"]).reshape(L, 128)), a_subln=f(inp["a_subln"]), b_qnorm=f(inp["b_qnorm"]),
        b_knorm=f(inp["b_knorm"]), c_conv_w=f(inp["c_conv_w"]), c_conv_b=f(inp["c_conv_b"]),
        c_dt_bias=f(np.asarray(inp["c_dt_bias"]).reshape(L, 8)), c_a_log=f(np.asarray(inp["c_a_log"]).reshape(L, 8)),
        c_d=f(inp["c_d"]), c_norm=f(inp["c_norm"]), d_lam_re=f(inp["d_lam_re"]), d_lam_im=f(inp["d_lam_im"]),
        d_log_step=f(inp["d_log_step"]), d_b=f(inp["d_b"]), d_c=f(np.asarray(inp["d_c"]).reshape(L, 2, 256, 128)),
        d_d=f(inp["d_d"]), d_glu=f(inp["d_glu"]), w_out=f(inp["w_out"]), w_up=f(inp["w_up"]),
        ffn_conv_w=f(inp["ffn_conv_w"]), ffn_conv_b=f(inp["ffn_conv_b"]), w_down=f(inp["w_down"]), **consts)
    xp = np.asarray(inp["x_prompt"], np.float32)
    xs = np.asarray(inp["x_sample"], np.float32)
    nsb = xs.shape[0]
    maps = []
    for c in range(n_cores):
        sb_ = (c * nsb) // n_cores
        m = dict(shared)
        m["xin"] = f(np.concatenate([xp[prompt_per_core * c + i] for i in range(prompt_per_core)] + [xs[sb_]], axis=0))
        m["cond"] = f(np.stack([np.asarray(inp["c_ctx"], np.float32), np.asarray(inp["c"], np.float32)[sb_]], axis=0))
        m["cak"] = f(np.asarray(inp["cache_a_k"])[sb_].reshape(L, 256, 256))
        m["cav"] = f(np.asarray(inp["cache_a_v"])[sb_].reshape(L, 256, 256))
        m["cbk"] = f(np.asarray(inp["cache_b_k"])[sb_].reshape(L, 256, 128))
        m["cbv"] = f(np.asarray(inp["cache_b_v"])[sb_].reshape(L, 256, 128))
        m["sssd"] = f(np.asarray(inp["state_ssd"])[sb_])
        m["ss5"] = f(np.asarray(inp["state_s5"])[sb_])
        maps.append(m)
    return maps


def gather(results, n_cores, NS, nsb, prompt_per_core=2):
    L = DEPTH
    yp = np.concatenate([r["yout"][0:512].reshape(2, 256, D) for r in results], axis=0)
    per = n_cores // nsb
    ys = np.stack([results[b * per]["yout"][512:512 + NS] for b in range(nsb)], axis=0)
    cat = lambda k: np.concatenate([r[k] for r in results], axis=0)
    nak = cat("o_ak").reshape(-1, L, 256, 4, 64)
    nav = cat("o_av").reshape(-1, L, 256, 4, 64)
    nbk = cat("o_bk").reshape(-1, L, 256, 2, 64)
    nbv = cat("o_bv").reshape(-1, L, 256, 2, 64)
    nssd = cat("o_ssd")
    ns5 = cat("o_s5")
    return tuple(np.ascontiguousarray(a.astype(np.float32)) for a in (yp, ys, nak, nav, nbk, nbv, nssd, ns5))


_CACHE = {}


def kernel(**inputs):
    NS = 4096
    n = 8
    if NS not in _CACHE:
        _CACHE[NS] = build(NS)
    nc, P = _CACHE[NS]
    maps = make_in_maps(inputs, n, NS)
    res = run_bass_kernel_spmd(nc, maps, core_ids=list(range(n)))
    return gather(res.results, n, NS, 2)
```

```python
import math
from contextlib import ExitStack
import numpy as np
import concourse.bass as bass
import concourse.mybir as mybir
from concourse.bass_utils import run_bass_kernel_spmd

F32 = mybir.dt.float32
BF16 = mybir.dt.bfloat16
I32 = mybir.dt.int32
ALU = mybir.AluOpType
AF = mybir.ActivationFunctionType
AX = mybir.AxisListType

D = 1024
DEPTH = 2
EPS = 1e-6
PI = math.pi


class Buf:
    __slots__ = ("w", "r")

    def __init__(self):
        self.w = None
        self.r = {}


class A:
    def __init__(self, ap, bufs):
        self.ap = ap
        self.bufs = bufs

    def __getitem__(self, k):
        return A(self.ap[k], self.bufs)

    def unsqueeze(self, a):
        return A(self.ap.unsqueeze(a), self.bufs)

    def bc(self, shape):
        return A(self.ap.to_broadcast(list(shape)), self.bufs)

    def rr(self, s, **kw):
        return A(self.ap.rearrange(s, **kw), self.bufs)

    def pb(self, n):
        return A(self.ap.partition_broadcast(n), self.bufs)


class T:
    def __init__(self, tensor, ap=None):
        self.t = tensor
        self.apx = ap
        self.buf = Buf()
        self.keys = {}

    def __getitem__(self, k):
        base = self.apx if self.apx is not None else self.t
        return A(base[k], [self.buf])

    def k(self, key):
        if key not in self.keys:
            v = T(self.t, self.apx)
            self.keys[key] = v
        return self.keys[key]


class Pair:
    def __init__(self, parts):
        self.parts = parts

    def __getitem__(self, key):
        p, ri = key[0], key[1]
        return self.parts[ri][(p,) + tuple(key[2:])]


class Eng:
    def __init__(self, prog, name, obj, is_dma=False, nlanes=8):
        self.e = obj
        self.is_dma = is_dma
        self.seen = {}
        if is_dma:
            self.lanes = [prog.new_sem("%s_l%d" % (name, i)) for i in range(nlanes)]
            self.lane_cnt = [0] * nlanes
            self.next = 0
        else:
            self.sem = prog.new_sem("s_" + name)
            self.n = 0


class Prog:
    def __init__(self, nc, es):
        self.nc = nc
        self.es = es
        self.sems = {}
        self.ninst = 0
        self.engs = {}
        self.engs["pe"] = Eng(self, "pe", nc.tensor)
        self.engs["act"] = Eng(self, "act", nc.scalar)
        self.engs["dve"] = Eng(self, "dve", nc.vector)
        self.engs["pool"] = Eng(self, "pool", nc.gpsimd)
        self.engs["q0"] = Eng(self, "q0", nc.sync, True, 12)
        self.engs["q1"] = Eng(self, "q1", nc.gpsimd, True, 8)
        self.stream = {"pe": "pe", "act": "act", "dve": "dve", "pool": "pool", "q0": "q0", "q1": "pool"}

    def new_sem(self, name):
        self.sems[name] = self.es.enter_context(self.nc.semaphore(name))
        return name

    def _wait(self, st, key, val):
        s = self.engs[st]
        if s.seen.get(key, 0) >= val:
            return
        s.e.wait_ge(self.sems[key], val)
        s.seen[key] = val

    def _deps(self, st, r, w, skip=None):
        deps = {}
        for b in r:
            if b.w is not None:
                deps[b.w[0]] = max(deps.get(b.w[0], 0), b.w[1])
        for b in w:
            if b.w is not None:
                deps[b.w[0]] = max(deps.get(b.w[0], 0), b.w[1])
            for k, v in b.r.items():
                deps[k] = max(deps.get(k, 0), v)
        for k, v in deps.items():
            if k != skip:
                self._wait(st, k, v)

    def _mark(self, key, val, r, w):
        for b in r:
            b.r[key] = max(b.r.get(key, 0), val)
        for b in w:
            b.w = (key, val)
            b.r = {}

    def op(self, eng, fn, r, w):
        e = self.engs[eng]
        self._deps(self.stream[eng], r, w, skip=(e.sem if eng == "pe" else None))
        inst = fn()
        e.n += 1
        inst.then_inc(self.sems[e.sem], 1)
        self._mark(e.sem, e.n, r, w)
        self.ninst += 1

    def dma(self, q, out, in_, **kw):
        e = self.engs[q]
        st = self.stream[q]
        lane = e.next
        e.next = (e.next + 1) % len(e.lanes)
        key = e.lanes[lane]
        if e.lane_cnt[lane] > 0:
            self._wait(st, key, 16 * e.lane_cnt[lane])
        self._deps(st, in_.bufs, out.bufs)
        inst = e.e.dma_start(out=out.ap, in_=in_.ap, **kw)
        e.lane_cnt[lane] += 1
        inst.then_inc(self.sems[key], 16)
        self._mark(key, 16 * e.lane_cnt[lane], in_.bufs, out.bufs)
        self.ninst += 1

    def barrier(self):
        tg = []
        for e in self.engs.values():
            if e.is_dma:
                for i, k in enumerate(e.lanes):
                    if e.lane_cnt[i]:
                        tg.append((k, 16 * e.lane_cnt[i]))
            elif e.n:
                tg.append((e.sem, e.n))
        for st in ("pe", "act", "dve", "pool", "q0"):
            for k, v in tg:
                self._wait(st, k, v)


STUB_C = False
STUB_D = False


def build(NS):
    nc = bass.Bass("TRN2", target_bir_lowering=False)
    TT = 512 + NS
    NBLK = TT // 256
    NCH = TT // 128
    LK = NS + 256
    L = DEPTH
    es = ExitStack()
    P = Prog(nc, es)

    def din(name, shape, dt=F32):
        return T(nc.dram_tensor(name, list(shape), dt, kind="ExternalInput").ap())

    def dout(name, shape, dt=F32):
        return T(nc.dram_tensor(name, list(shape), dt, kind="ExternalOutput").ap())

    def dscr(name, shape, dt=F32):
        return T(nc.dram_tensor(name, list(shape), dt, kind="Internal").ap())

    xin = din("xin", [TT, D])
    cond = din("cond", [2, D])
    w_mod = din("w_mod", [L, D, 6 * D]); b_mod = din("b_mod", [L, 6 * D])
    g_pre1 = din("g_pre1", [L, D]); g_post1 = din("g_post1", [L, D])
    g_pre2 = din("g_pre2", [L, D]); g_post2 = din("g_post2", [L, D])
    w_in = din("w_in", [L, D, 2696])
    a_lam = din("a_lam", [L, 128]); a_subln = din("a_subln", [L, 64])
    b_qnorm = din("b_qnorm", [L, 64]); b_knorm = din("b_knorm", [L, 64])
    c_conv_w = din("c_conv_w", [L, 3, 512]); c_conv_b = din("c_conv_b", [L, 512])
    c_dt_bias = din("c_dt_bias", [L, 8]); c_a_log = din("c_a_log", [L, 8]); c_d = din("c_d", [L, 4])
    c_norm = din("c_norm", [L, 256])
    d_lam_re = din("d_lam_re", [L, 2, 16, 64]); d_lam_im = din("d_lam_im", [L, 2, 16, 64])
    d_log_step = din("d_log_step", [L, 2, 16])
    d_b = din("d_b", [L, 2, 16, 64, 16, 2]); d_c = din("d_c", [L, 2, 256, 128])
    d_d = din("d_d", [L, 256]); d_glu = din("d_glu", [L, 256, 512])
    w_out = din("w_out", [L, D, D]); w_up = din("w_up", [L, D, 5632])
    ffn_conv_w = din("ffn_conv_w", [L, 3, 5632]); ffn_conv_b = din("ffn_conv_b", [L, 5632])
    w_down = din("w_down", [L, 2816, D])
    cak = din("cak", [L, 256, 256]); cav = din("cav", [L, 256, 256])
    cbk = din("cbk", [L, 256, 128]); cbv = din("cbv", [L, 256, 128])
    sssd = din("sssd", [L, 2, 4, 64, 64]); ss5 = din("ss5", [L, 2, 16, 64, 2])
    c_ident = din("c_ident", [128, 128])
    c_misc = din("c_misc", [128, 64])
    c_mats = din("c_mats", [11, 128, 128])
    c_erow = din("c_erow", [2, 128, 129])
    c_m16 = din("c_m16", [128, 8, 128])
    ropeA = din("ropeA", [2, 128, NS]); ropeB = din("ropeB", [2, 128, NS])

    yout = dout("yout", [TT, D])
    o_ak = dout("o_ak", [2, L, 256, 256]); o_av = dout("o_av", [2, L, 256, 256])
    o_bk = dout("o_bk", [2, L, 256, 128]); o_bv = dout("o_bv", [2, L, 256, 128])
    o_ssd = dout("o_ssd", [2, L, 2, 4, 64, 64]); o_s5 = dout("o_s5", [2, L, 2, 16, 64, 2])

    XT = [dscr("XT%d" % i, [D, TT]) for i in range(3)]
    XM = dscr("XM", [D, TT])
    NPJ = 16
    PJ = dscr("PJ", [NPJ * 128, TT])
    VT = dscr("VT", [TT, 384])
    YC = dscr("YC", [D, TT], BF16)
    SSDFs = [dscr("SSDF%d" % l_, [NCH, 128, 848]) for l_ in range(L)]
    SSDBs = [dscr("SSDB%d" % l_, [NCH, 128, 896], BF16) for l_ in range(L)]

    seqs = [(0, 256, 0, False, 0), (256, 256, 0, False, 1), (512, NS, 1, True, -1)]

    def bufs(*xs):
        out = []
        for x in xs:
            if isinstance(x, A):
                out += x.bufs
        return out

    def apx(x):
        return x.ap if isinstance(x, A) else x

    def mm(out, lhsT, rhs, start=True, stop=True):
        P.op("pe", lambda: nc.tensor.matmul(out.ap, lhsT=lhsT.ap, rhs=rhs.ap, start=start, stop=stop),
             bufs(lhsT, rhs), bufs(out))

    def act(out, in_, func, bias=0.0, scale=1.0):
        P.op("act", lambda: nc.scalar.activation(out=out.ap, in_=in_.ap, func=func, bias=apx(bias), scale=apx(scale)),
             bufs(in_, bias, scale), bufs(out))

    def engobj(e):
        return nc.vector if e == "dve" else nc.gpsimd

    def tt(e, out, in0, in1, op):
        P.op(e, lambda: engobj(e).tensor_tensor(out=out.ap, in0=in0.ap, in1=in1.ap, op=op), bufs(in0, in1), bufs(out))

    def ts(e, out, in0, s1, s2, op0, op1=None):
        if op1 is None:
            P.op(e, lambda: engobj(e).tensor_scalar(out=out.ap, in0=in0.ap, scalar1=apx(s1), scalar2=None, op0=op0),
                 bufs(in0, s1), bufs(out))
        else:
            P.op(e, lambda: engobj(e).tensor_scalar(out=out.ap, in0=in0.ap, scalar1=apx(s1), scalar2=apx(s2), op0=op0, op1=op1),
                 bufs(in0, s1, s2), bufs(out))

    def stt(e, out, in0, sc, in1, op0, op1):
        P.op(e, lambda: engobj(e).scalar_tensor_tensor(out=out.ap, in0=in0.ap, scalar=apx(sc), in1=in1.ap, op0=op0, op1=op1),
             bufs(in0, sc, in1), bufs(out))

    def cp(e, out, in_):
        if e == "act":
            act(out, in_, AF.Copy)
        else:
            P.op(e, lambda: engobj(e).tensor_copy(out=out.ap, in_=in_.ap), bufs(in_), bufs(out))

    def memset(e, out, val):
        P.op(e, lambda: engobj(e).memset(out.ap, val), [], bufs(out))

    def recip(out, in_):
        P.op("dve", lambda: nc.vector.reciprocal(out=out.ap, in_=in_.ap), bufs(in_), bufs(out))

    def dma(out, in_, q="q0", **kw):
        P.dma(q, out, in_, **kw)

    uid = [0]

    def sb(stack, name, shape, dt=F32):
        uid[0] += 1
        return T(stack.enter_context(nc.sbuf_tensor("%s_%d" % (name, uid[0]), list(shape), dt)))

    def pst(stack, name, shape, dt=F32):
        uid[0] += 1
        return T(stack.enter_context(nc.psum_tensor("%s_%d" % (name, uid[0]), list(shape), dt)))

    ident = sb(es, "ident", [128, 128]); dma(ident[:], c_ident[:, :])
    identb = sb(es, "identb", [128, 128], BF16)
    misc = sb(es, "misc", [128, 64]); dma(misc[:], c_misc[:, :])
    cm = sb(es, "cmats", [128, 11, 128]); dma(cm[:], c_mats[:, :, :].rr("m p n -> p m n"))
    cmb = sb(es, "cmatsb", [128, 11, 128], BF16)
    m16 = sb(es, "m16", [128, 8, 128]); dma(m16[:], c_m16[:, :, :])
    cp("dve", identb[:], ident[:])
    cp("dve", cmb[:], cm[:])
    M_ONES1024, M_BLK64, M_ONES256, M_PERMA, M_PERMB, M_TRII, M_TRIE, M_NEGF, M_POSB, M_ONES, M_TRIGE = range(11)
    mod_sb = sb(es, "mod_sb", [128, L, 2, 48])
    gains = sb(es, "gains", [128, L, 4, 8])
    for l in range(L):
        for i, g in enumerate((g_pre1, g_post1, g_pre2, g_post2)):
            dma(gains[:, l, i, :], g[l, :].rr("(k p) -> p k", p=128), allow_slow_non_contiguous=True)

    def stage_mod():
        with ExitStack() as s:
            cT = sb(s, "cT", [128, 8, 2])
            sT = sb(s, "sT", [128, 8, 2])
            for c_ in range(2):
                dma(cT[:, :, c_], cond[c_, :].rr("(k p) -> p k", p=128), allow_slow_non_contiguous=True)
            act(sT[:], cT[:], AF.Silu)
            bm = sb(s, "bm", [128, L, 48])
            for l_ in range(L):
                dma(bm[:, l_, :], b_mod[l_, :].rr("(m p) -> p m", p=128), allow_slow_non_contiguous=True)
            wk = [sb(s, "wmk%d" % i, [128, 8, 1536]) for i in range(2)]
            pm = pst(s, "pm", [128, 48, 2])
            gi_ = 0
            for l in range(L):
                for grp in range(4):
                    w = wk[gi_ % 2]; gi_ += 1
                    for k in range(8):
                        dma(w[:, k, :], w_mod[l, k * 128:(k + 1) * 128, grp * 1536:(grp + 1) * 1536], q=("q0" if k % 2 == 0 else "q1"))
                    for mi in range(12):
                        m = grp * 12 + mi
                        for k in range(8):
                            mm(pm[:, m, :], w[:, k, mi * 128:(mi + 1) * 128], sT[:, k, :], start=(k == 0), stop=(k == 7))
                for c in range(2):
                    tt("dve", mod_sb[:, l, c, :], pm[:, :, c], bm[:, l, :], ALU.add)
                for c in range(2):
                    for (six, gi) in ((1, 0), (4, 2)):
                        ts("dve", mod_sb[:, l, c, six * 8:(six + 1) * 8], mod_sb[:, l, c, six * 8:(six + 1) * 8], 1.0, None, ALU.add)
                        tt("dve", mod_sb[:, l, c, six * 8:(six + 1) * 8], mod_sb[:, l, c, six * 8:(six + 1) * 8], gains[:, l, gi, :], ALU.mult)
                    for (six, gi) in ((2, 1), (5, 3)):
                        tt("dve", mod_sb[:, l, c, six * 8:(six + 1) * 8], mod_sb[:, l, c, six * 8:(six + 1) * 8], gains[:, l, gi, :], ALU.mult)
        P.barrier()

    def stage_t_in():
        with ExitStack() as s:
            xt = [sb(s, "tin%d" % i, [128, D]) for i in range(2)]
            xo = [sb(s, "tio%d" % i, [128, 8, 128]) for i in range(2)]
            pp = [pst(s, "tip%d" % i, [128, 4, 128]) for i in range(2)]
            for c in range(NCH):
                a = xt[c % 2]; o = xo[c % 2]
                dma(a[:], xin[c * 128:(c + 1) * 128, :])
                for hlf in range(2):
                    p_ = pp[hlf]
                    for j in range(4):
                        k = hlf * 4 + j
                        P.op("pe", lambda: nc.tensor.transpose(p_[:, j, :].ap, a[:, k * 128:(k + 1) * 128].ap, ident[:].ap),
                             bufs(a[:], ident[:]), bufs(p_[:]))
                    cp("act" if hlf else "dve", o[:, hlf * 4:(hlf + 1) * 4, :], p_[:])
                dma(XT[0].k(c // 2)[:, c * 128:(c + 1) * 128].rr("(k p) n -> p k n", p=128), o[:], q="q1")
        P.barrier()

    def stage_t_out():
        with ExitStack() as s:
            xi = [sb(s, "toi%d" % i, [128, 8, 128]) for i in range(2)]
            xo = [sb(s, "too%d" % i, [128, D]) for i in range(2)]
            pp = [pst(s, "top%d" % i, [128, 512]) for i in range(2)]
            for c in range(NCH):
                a = xi[c % 2]; o = xo[c % 2]
                dma(a[:], XT[2].k(c // 2)[:, c * 128:(c + 1) * 128].rr("(k p) n -> p k n", p=128))
                for hlf in range(2):
                    p_ = pp[hlf]
                    for j in range(4):
                        k = hlf * 4 + j
                        P.op("pe", lambda: nc.tensor.transpose(p_[:, j * 128:(j + 1) * 128].ap, a[:, k, :].ap, ident[:].ap),
                             bufs(a[:], ident[:]), bufs(p_[:]))
                    cp("act" if hlf else "dve", o[:, hlf * 512:(hlf + 1) * 512], p_[:])
                dma(yout[c * 128:(c + 1) * 128, :], o[:], q="q1")
        P.barrier()

    def rstd_from_ps(s_out, ps_in):
        act(s_out, ps_in, AF.Sqrt, bias=misc[:, 15:16], scale=1.0)
        recip(s_out, s_out)

    def load_weight_bf16(s, name, src_rows, ncols, nk, dst=None, chunk=1024):
        w = dst if dst is not None else sb(s, name, [128, nk, ncols], BF16)
        with ExitStack() as s2:
            stg = [sb(s2, name + "_st%d" % i, [128, chunk]) for i in range(2)]
            i = 0
            for k in range(nk):
                for c0 in range(0, ncols, chunk):
                    c1 = min(ncols, c0 + chunk)
                    st = stg[i % 2]
                    dma(st[:, 0:c1 - c0], src_rows(k)[:, c0:c1], q=("q0" if i % 2 == 0 else "q1"))
                    cp("pool" if i % 2 == 0 else "dve", w[:, k, c0:c1], st[:, 0:c1 - c0])
                    i += 1
            P.barrier()
        return w

    def stage_inproj(l):
        with ExitStack() as s:
            win = load_weight_bf16(s, "win", lambda k: w_in.k(0)[l, k * 128:(k + 1) * 128, :], 2696, 8)
            gkb = sb(s, "gkb", [128, 64])
            dma(gkb[:], b_knorm[l, :].pb(128))
            xT = [sb(s, "ip_x%d" % i, [128, 8, 256]) for i in range(2)]
            sq = sb(s, "ip_sq", [128, 8, 256], BF16)
            rs = sb(s, "ip_rs", [128, 256])
            hn = sb(s, "ip_hn", [128, 8, 256])
            hT = sb(s, "ip_h", [128, 8, 256], BF16)
            pj = [sb(s, "ip_pj%d" % i, [128, NPJ, 256]) for i in range(2)]
            vt = [sb(s, "ip_vt%d" % i, [128, 768]) for i in range(2)]
            kn = sb(s, "ip_kn", [128, 128]); kq = sb(s, "ip_kq", [128, 128]); kr = sb(s, "ip_kr", [128, 2])
            pms = pst(s, "ip_pms", [128, 256])
            pps = [pst(s, "ip_pp%d" % i, [128, 256]) for i in range(5)]
            pvs = [pst(s, "ip_pv%d" % i, [128, 384]) for i in range(2)]
            tile_cols = [(i * 128, 128) for i in range(13)] + [(13 * 128, 8), (13 * 128 + 8, 128), (13 * 128 + 136, 128)]
            VOFF = 13 * 128 + 8 + 256
            cnt = 0
            for b in range(NBLK):
                c0 = b * 256
                ci = 0 if b < 2 else 1
                x = xT[b % 2]
                dma(x[:], XT[l].k(b)[:, c0:c0 + 256].rr("(k p) n -> p k n", p=128))
                act(sq[:], x[:], AF.Square)
                for k in range(8):
                    mm(pms[:], cmb[:, M_ONES1024, :], sq[:, k, :], start=(k == 0), stop=(k == 7))
                rstd_from_ps(rs[:], pms[:])
                tt("dve", hn[:], x[:], rs[:].unsqueeze(1).bc([128, 8, 256]), ALU.mult)
                for k in range(8):
                    act(hT[:, k, :], hn[:, k, :], AF.Identity, bias=mod_sb[:, l, ci, k:k + 1], scale=mod_sb[:, l, ci, 8 + k:9 + k])
                o = pj[b % 2]
                for m, (cc0, cw) in enumerate(tile_cols):
                    pp = pps[cnt % 5]; cnt += 1
                    for k in range(8):
                        mm(pp[0:cw, :], win[:, k, cc0:cc0 + cw], hT[:, k, :], start=(k == 0), stop=(k == 7))
                    cp("act" if m % 2 else "dve", o[0:cw, m, :], pp[0:cw, :])
                dma(PJ.k(b)[:, c0:c0 + 256].rr("(m p) n -> p m n", p=128), o[:], q="q1")
                for t2 in range(2):
                    v = vt[t2]
                    nhalf = 2 if b < 2 else 1
                    for h2 in range(nhalf):
                        pv = pvs[h2]
                        for k in range(8):
                            mm(pv[:], hT[:, k, t2 * 128:(t2 + 1) * 128], win[:, k, VOFF + h2 * 384:VOFF + (h2 + 1) * 384],
                               start=(k == 0), stop=(k == 7))
                        cp("dve" if h2 else "act", v[:, h2 * 384:(h2 + 1) * 384], pv[:])
                    r0 = c0 + t2 * 128
                    dma(VT.k(b)[r0:r0 + 128, :], v[:, 0:384], q="q1")
                    if b < 2:
                        pr = t2 * 128
                        dma(o_av[b, l, pr:pr + 128, :], v[:, 0:256], q="q1")
                        dma(o_bv[b, l, pr:pr + 128, :], v[:, 256:384], q="q1")
                        dma(o_ak[b, l, pr:pr + 128, :], v[:, 384:640], q="q1")
                        tt("dve", kq[:], v[:, 640:768], v[:, 640:768], ALU.mult)
                        P.op("dve", lambda: nc.vector.tensor_reduce(out=kr[:].ap, in_=kq[:].rr("p (h d) -> p h d", h=2).ap,
                                                                    axis=AX.X, op=ALU.add), bufs(kq[:]), bufs(kr[:]))
                        act(kr[:], kr[:], AF.Sqrt, bias=misc[:, 15:16], scale=1.0 / 64)
                        recip(kr[:], kr[:])
                        tt("dve", kn[:].rr("p (h d) -> p h d", h=2), v[:, 640:768].rr("p (h d) -> p h d", h=2),
                           kr[:].unsqueeze(2).bc([128, 2, 64]), ALU.mult)
                        tt("dve", kn[:].rr("p (h d) -> p h d", h=2), kn[:].rr("p (h d) -> p h d", h=2),
                           gkb[:].unsqueeze(1).bc([128, 2, 64]), ALU.mult)
                        dma(o_bk[b, l, pr:pr + 128, :], kn[:], q="q1")
        P.barrier()


    def stage_attn(l):
        lam_init = 0.8 - 0.6 * math.exp(-0.3 * l)
        with ExitStack() as s:
            alb = sb(s, "alb", [128, 128]); dma(alb[:], a_lam[l, :].pb(128))
            al2 = sb(s, "al2", [128, 2, 32]); lamc = sb(s, "lamc", [128, 4])
            tt("dve", al2[:, 0, :], alb[:, 0:32], alb[:, 32:64], ALU.mult)
            tt("dve", al2[:, 1, :], alb[:, 64:96], alb[:, 96:128], ALU.mult)
            P.op("dve", lambda: nc.vector.tensor_reduce(out=lamc[:, 0:2].ap, in_=al2[:].ap, axis=AX.X, op=ALU.add), bufs(al2[:]), bufs(lamc[:]))
            act(lamc[:, 0:2], lamc[:, 0:2], AF.Exp)
            tt("dve", lamc[:, 2:3], lamc[:, 1:2], lamc[:, 0:1], ALU.subtract)
            ts("dve", lamc[:, 3:4], lamc[:, 2:3], -lam_init, None, ALU.add)
            gcol = sb(s, "gcol", [128, 3])
            for hh in range(2):
                dma(gcol[hh * 64:(hh + 1) * 64, 0:1], a_subln[l, :].rr("(d o) -> d o", o=1), allow_slow_non_contiguous=True)
                dma(gcol[hh * 64:(hh + 1) * 64, 1:2], b_qnorm[l, :].rr("(d o) -> d o", o=1), allow_slow_non_contiguous=True)
                dma(gcol[hh * 64:(hh + 1) * 64, 2:3], b_knorm[l, :].rr("(d o) -> d o", o=1), allow_slow_non_contiguous=True)
            ts("dve", gcol[:, 0:1], gcol[:, 0:1], 1.0 - lam_init, None, ALU.mult)
            KT = sb(s, "KT", [128, 3, LK], BF16)
            VP = sb(s, "VP", [128, LK // 128, 8, 128], BF16)
            memset("pool", VP[:], 1.0)
            xk = [sb(s, "at_xk%d" % i, [128, 4, 512]) for i in range(2)]
            rp = [sb(s, "at_rp%d" % i, [128, 4, 512]) for i in range(2)]
            sqb = sb(s, "at_sq", [128, 2, 512], BF16)
            rsb = sb(s, "at_rs", [128, 2, 512])
            tmp = sb(s, "at_tmp", [128, 512])
            vld = [sb(s, "at_v%d" % i, [128, 384]) for i in range(2)]
            ckl = sb(s, "at_ck", [128, 384])
            Qzs = [sb(s, "Qz%d" % i, [128, 12, 512], BF16) for i in range(2)]
            Pb = [sb(s, "at_P%d" % i, [128, 512], BF16) for i in range(3)]
            yab = sb(s, "at_ya", [128, 2, 512]); ybb = sb(s, "at_yb", [128, 2, 512])
            o1 = sb(s, "at_o1", [128, 512]); o2 = sb(s, "at_o2", [128, 512]); rr_ = sb(s, "at_rr", [128, 512])
            yob = sb(s, "at_yo", [128, 4, 512], BF16)
            pS = [pst(s, "at_pS%d" % i, [128, 512]) for i in range(3)]
            pA = [pst(s, "at_pA%d" % i, [128, 512]) for i in range(2)]
            pM = pst(s, "at_pM", [128, 512])

            def headnorm(x2, ntile, gidx, out2, n=256):
                act(sqb[:, 0:ntile, 0:n], x2, AF.Square)
                for t in range(ntile):
                    mm(pM[:, 0:n], cmb[:, M_BLK64, :], sqb[:, t, 0:n])
                    rstd_from_ps(rsb[:, t, 0:n], pM[:, 0:n])
                    stt("dve", out2[:, t, :], x2[:, t, :], gcol[:, gidx:gidx + 1], rsb[:, t, 0:n], ALU.mult, ALU.mult)

            def rope(xa, tab, midx, n=256):
                mm(pM[:, 0:n], cm[:, midx, :], xa)
                tt("dve", tmp[:, 0:n], pM[:, 0:n], tab[:, 1, :], ALU.mult)
                tt("dve", xa, xa, tab[:, 0, :], ALU.mult)
                tt("dve", xa, xa, tmp[:, 0:n], ALU.add)

            for (col0, Ls, ci, is_s, pidx) in seqs:
                nkt = (Ls + (256 if is_s else 0)) // 128
                for bb in range(Ls // 256):
                    b = (col0 + bb * 256) // 256
                    c0 = col0 + bb * 256
                    x = xk[bb % 2]
                    dma(x[:, 0:2, 0:256], PJ.k(b)[2 * 128:4 * 128, c0:c0 + 256].rr("(m p) n -> p m n", p=128))
                    dma(x[:, 2, 0:256], PJ.k(b)[6 * 128:7 * 128, c0:c0 + 256])
                    headnorm(x[:, 2:3, 0:256], 1, 2, x[:, 2:3, 0:256])
                    if is_s:
                        r_ = rp[bb % 2]
                        dma(r_[:, 0:2, 0:256], ropeA[:, :, bb * 256:(bb + 1) * 256].rr("c p n -> p c n"))
                        dma(r_[:, 2:4, 0:256], ropeB[:, :, bb * 256:(bb + 1) * 256].rr("c p n -> p c n"))
                        rope(x[:, 0, 0:256], r_[:, 0:2, 0:256], M_PERMA)
                        rope(x[:, 1, 0:256], r_[:, 0:2, 0:256], M_PERMA)
                        rope(x[:, 2, 0:256], r_[:, 2:4, 0:256], M_PERMB)
                    cp("act", KT[:, :, bb * 256:(bb + 1) * 256], x[:, 0:3, 0:256])
                    for t2 in range(2):
                        kt = bb * 2 + t2
                        v = vld[t2]
                        r0 = c0 + t2 * 128
                        dma(v[:], VT.k(b)[r0:r0 + 128, :])
                        for h in range(4):
                            o_ = 0 if h % 2 == 0 else 64
                            cp("pool", VP[:, kt, h, o_:o_ + 64], v[:, h * 64:(h + 1) * 64])
                        for g in range(2):
                            for par in range(2):
                                cp("pool", VP[:, kt, 4 + g * 2 + par, par * 64:par * 64 + 64], v[:, 256 + g * 64:256 + (g + 1) * 64])
                if is_s:
                    for t2 in range(2):
                        kt = Ls // 128 + t2
                        dma(ckl[:, 0:256], cak[l, t2 * 128:(t2 + 1) * 128, :])
                        dma(ckl[:, 256:384], cbk[l, t2 * 128:(t2 + 1) * 128, :])
                        for m in range(3):
                            P.op("pe", lambda: nc.tensor.transpose(pM[:, 0:128].ap, ckl[:, m * 128:(m + 1) * 128].ap, ident[:].ap),
                                 bufs(ckl[:], ident[:]), bufs(pM[:]))
                            cp("act", KT[:, m, Ls + t2 * 128:Ls + (t2 + 1) * 128], pM[:, 0:128])
                        v = vld[t2]
                        dma(v[:, 0:256], cav[l, t2 * 128:(t2 + 1) * 128, :])
                        dma(v[:, 256:384], cbv[l, t2 * 128:(t2 + 1) * 128, :])
                        for h in range(4):
                            o_ = 0 if h % 2 == 0 else 64
                            cp("pool", VP[:, kt, h, o_:o_ + 64], v[:, h * 64:(h + 1) * 64])
                        for g in range(2):
                            for par in range(2):
                                cp("pool", VP[:, kt, 4 + g * 2 + par, par * 64:par * 64 + 64], v[:, 256 + g * 64:256 + (g + 1) * 64])
                QB = 512 if is_s else 256
                cS = 0; cA = 0; cP = 0

                def prep_q(bb):
                    c0 = col0 + bb * QB
                    x = xk[bb % 2]
                    Qz = Qzs[bb % 2]
                    for h_ in range(QB // 256):
                        b = (c0 + h_ * 256) // 256
                        hs = slice(h_ * 256, (h_ + 1) * 256)
                        dma(x[:, 0:2, hs], PJ.k(b)[0:256, c0 + h_ * 256:c0 + (h_ + 1) * 256].rr("(m p) n -> p m n", p=128))
                        dma(x[:, 2:4, hs], PJ.k(b)[4 * 128:6 * 128, c0 + h_ * 256:c0 + (h_ + 1) * 256].rr("(m p) n -> p m n", p=128))
                    headnorm(x[:, 2:4, 0:QB], 2, 1, x[:, 2:4, 0:QB], n=QB)
                    if is_s:
                        r_ = rp[bb % 2]
                        dma(r_[:, 0:2, 0:QB], ropeA[:, :, bb * QB:(bb + 1) * QB].rr("c p n -> p c n"))
                        dma(r_[:, 2:4, 0:QB], ropeB[:, :, bb * QB:(bb + 1) * QB].rr("c p n -> p c n"))
                        rope(x[:, 0, 0:QB], r_[:, 0:2, 0:QB], M_PERMA, n=QB)
                        rope(x[:, 1, 0:QB], r_[:, 0:2, 0:QB], M_PERMA, n=QB)
                        rope(x[:, 2, 0:QB], r_[:, 2:4, 0:QB], M_PERMB, n=QB)
                        rope(x[:, 3, 0:QB], r_[:, 2:4, 0:QB], M_PERMB, n=QB)
                    jobs = []
                    for t in range(2):
                        for hl in range(2):
                            for half in range(2):
                                j = len(jobs)
                                ts("dve", Qz[:, j, 0:QB], x[:, t, 0:QB], misc[:, 2 + hl * 2 + half:3 + hl * 2 + half], None, ALU.mult)
                                jobs.append((j, t, t * 2 + hl, 32 ** -0.5))
                    for qh in range(4):
                        j = len(jobs)
                        t = qh % 2; g = qh // 2
                        ts("dve", Qz[:, j, 0:QB], x[:, 2 + t, 0:QB], misc[:, 10 + g:11 + g], None, ALU.mult)
                        jobs.append((j, 2, 4 + g * 2 + (qh % 2), 64 ** -0.5))
                    return jobs

                nQ = Ls // QB
                jobs_next = prep_q(0)
                for bb in range(nQ):
                    c0 = col0 + bb * QB
                    Qz = Qzs[bb % 2]
                    jobs = jobs_next
                    if bb + 1 < nQ:
                        jobs_next = prep_q(bb + 1)
                    steps = [(jb, kt) for jb in jobs for kt in range(nkt)]
                    Sbuf = {}
                    AHEAD = 2

                    def issue_S(i):
                        (j, ktile, var, scl), kt = steps[i]
                        S = pS[i % 3]
                        mm(S[:, 0:QB], KT[:, ktile, kt * 128:(kt + 1) * 128], Qz[:, j, 0:QB])

                    for i in range(min(AHEAD, len(steps))):
                        issue_S(i)
                    acc = None
                    for i, ((j, ktile, var, scl), kt) in enumerate(steps):
                        if i + AHEAD < len(steps):
                            issue_S(i + AHEAD)
                        if kt == 0:
                            acc = pA[cA % 2]; cA += 1
                        S = pS[i % 3]
                        pb_ = Pb[i % 3]
                        act(pb_[:, 0:QB], S[:, 0:QB], AF.Exp, scale=scl)
                        mm(acc[:, 0:QB], VP[:, kt, var, :], pb_[:, 0:QB], start=(kt == 0), stop=(kt == nkt - 1))
                        if kt != nkt - 1:
                            continue
                        par = (var % 2) if var < 4 else ((var - 4) % 2)
                        nr = slice(64 * par, 64 * par + 64); sr = slice(64 - 64 * par, 128 - 64 * par)
                        recip(rr_[nr, 0:QB], acc[sr, 0:QB])
                        if j < 8:
                            half = j % 2
                            dst = o1 if half == 0 else o2
                            tt("dve", dst[nr, 0:QB], acc[nr, 0:QB], rr_[nr, 0:QB], ALU.mult)
                            if half == 1:
                                t = j // 4
                                stt("dve", yab[nr, t, 0:QB], o2[nr, 0:QB], lamc[nr, 3:4], o1[nr, 0:QB], ALU.mult, ALU.add)
                        else:
                            qh = j - 8
                            tt("dve", ybb[nr, qh // 2, 0:QB], acc[nr, 0:QB], rr_[nr, 0:QB], ALU.mult)
                    headnorm(yab[:, :, 0:QB], 2, 0, yab[:, :, 0:QB], n=QB)
                    cp("act", yob[:, 0:2, 0:QB], yab[:, :, 0:QB])
                    cp("act", yob[:, 2:4, 0:QB], ybb[:, :, 0:QB])
                    for h_ in range(QB // 256):
                        b = (c0 + h_ * 256) // 256
                        dma(YC.k(("a", b))[0:512, c0 + h_ * 256:c0 + (h_ + 1) * 256].rr("(m p) n -> p m n", p=128),
                            yob[:, :, h_ * 256:(h_ + 1) * 256], q="q1")
        P.barrier()

    def stage_outproj(l):
        with ExitStack() as s:
            wo = load_weight_bf16(s, "wo", lambda k: w_out.k(0)[l, k * 128:(k + 1) * 128, :], D, 8)
            yc = [sb(s, "op_yc%d" % i, [128, 8, 256], BF16) for i in range(2)]
            xo = [sb(s, "op_x%d" % i, [128, 8, 256]) for i in range(2)]
            ys_ = [sb(s, "op_y%d" % i, [128, 8, 256]) for i in range(2)]; sqs_ = [sb(s, "op_sq%d" % i, [128, 8, 256], BF16) for i in range(2)]; rss_ = [sb(s, "op_rs%d" % i, [128, 256]) for i in range(2)]
            pps = [pst(s, "op_pp%d" % i, [128, 256]) for i in range(3)]
            pmss = [pst(s, "op_pms%d" % i, [128, 256]) for i in range(2)]
            cnt = 0
            for b in range(NBLK):
                c0 = b * 256; ci = 0 if b < 2 else 1
                a = yc[b % 2]; x = xo[b % 2]; y = ys_[b % 2]; sq = sqs_[b % 2]; rs = rss_[b % 2]; pms = pmss[b % 2]
                dma(a[:, 0:4, :], YC.k(("a", b))[0:512, c0:c0 + 256].rr("(m p) n -> p m n", p=128))
                dma(a[:, 4:6, :], YC.k(("c", b))[512:768, c0:c0 + 256].rr("(m p) n -> p m n", p=128))
                dma(a[:, 6:8, :], YC.k(("d", b))[768:1024, c0:c0 + 256].rr("(m p) n -> p m n", p=128))
                dma(x[:], XT[l].k(b)[:, c0:c0 + 256].rr("(k p) n -> p k n", p=128))
                for m in range(8):
                    pp = pps[cnt % 3]; cnt += 1
                    for k in range(8):
                        mm(pp[:], wo[:, k, m * 128:(m + 1) * 128], a[:, k, :], start=(k == 0), stop=(k == 7))
                    cp("act" if m % 2 else "dve", y[:, m, :], pp[:])
                act(sq[:], y[:], AF.Square)
                for k in range(8):
                    mm(pms[:], cmb[:, M_ONES1024, :], sq[:, k, :], start=(k == 0), stop=(k == 7))
                rstd_from_ps(rs[:], pms[:])
                tt("dve", y[:], y[:], rs[:].unsqueeze(1).bc([128, 8, 256]), ALU.mult)
                for k in range(8):
                    stt("dve", x[:, k, :], y[:, k, :], mod_sb[:, l, ci, 16 + k:17 + k], x[:, k, :], ALU.mult, ALU.add)
                dma(XM.k(b)[:, c0:c0 + 256].rr("(k p) n -> p k n", p=128), x[:], q="q1")
        P.barrier()

    def stage_ffn(l):
        with ExitStack() as s:
            wu = load_weight_bf16(s, "wu", lambda k: w_up.k(0)[l, k * 128:(k + 1) * 128, :], 5632, 8)
            wd = load_weight_bf16(s, "wd", lambda k: w_down.k(0)[l, k * 128:(k + 1) * 128, :], D, 22)
            cw = sb(s, "ff_cw", [128, 3, 44]); cb = sb(s, "ff_cb", [128, 44])
            for w_ in range(3):
                dma(cw[:, w_, :], ffn_conv_w[l, w_, :].rr("(m p) -> p m", p=128), allow_slow_non_contiguous=True)
            dma(cb[:], ffn_conv_b[l, :].rr("(m p) -> p m", p=128), allow_slow_non_contiguous=True)
            xh = [sb(s, "ff_x%d" % i, [128, 8, 258]) for i in range(2)]
            sq = sb(s, "ff_sq", [128, 8, 258], BF16); rs = sb(s, "ff_rs", [128, 258])
            hn = sb(s, "ff_hn", [128, 8, 258]); h2 = sb(s, "ff_h2", [128, 8, 258], BF16)
            ug = [sb(s, "ff_ug%d" % i, [128, 258]) for i in range(2)]
            uv = [sb(s, "ff_uv%d" % i, [128, 258]) for i in range(2)]
            cg = sb(s, "ff_cg", [128, 256]); cv = sb(s, "ff_cv", [128, 256]); sg = sb(s, "ff_sg", [128, 256])
            aT = sb(s, "ff_a", [128, 22, 256], BF16)
            y = sb(s, "ff_y", [128, 8, 256])
            pms = pst(s, "ff_pms", [128, 258])
            pps = [pst(s, "ff_pp%d" % i, [128, 258]) for i in range(7)]
            cnt = 0
            for (col0, Ls, ci, is_s, pidx) in seqs:
                for bb in range(Ls // 256):
                    b = (col0 + bb * 256) // 256
                    c0 = col0 + bb * 256
                    x = xh[b % 2]
                    first = bb == 0; last = bb == Ls // 256 - 1
                    lo = 1 if first else 0; hi = 257 if last else 258
                    if first:
                        memset("pool", x[:, :, 0:1], 0.0)
                    if last:
                        memset("pool", x[:, :, 257:258], 0.0)
                    srcT = XM.k(b) if (first and last) else XM.k("all")
                    for bq in range(max(0, b - 1), min(NBLK, b + 2)):
                        pass
                    a_in = A(XM.t[:, c0 - 1 + lo:c0 - 1 + hi].rearrange("(k p) n -> p k n", p=128),
                             [XM.k(b).buf] + ([XM.k(b - 1).buf] if not first else []) + ([XM.k(b + 1).buf] if not last else []))
                    dma(x[:, :, lo:hi], a_in)
                    act(sq[:], x[:], AF.Square)
                    for k in range(8):
                        mm(pms[:], cmb[:, M_ONES1024, :], sq[:, k, :], start=(k == 0), stop=(k == 7))
                    rstd_from_ps(rs[:], pms[:])
                    tt("dve", hn[:], x[:], rs[:].unsqueeze(1).bc([128, 8, 258]), ALU.mult)
                    for k in range(8):
                        act(h2[:, k, :], hn[:, k, :], AF.Identity, bias=mod_sb[:, l, ci, 24 + k:25 + k], scale=mod_sb[:, l, ci, 32 + k:33 + k])
                    if first:
                        memset("pool", h2[:, :, 0:1], 0.0)
                    if last:
                        memset("pool", h2[:, :, 257:258], 0.0)
                    for m in range(22):
                        for (which, mt, ubuf, cdst) in ((0, m, ug[m % 2], cg), (1, 22 + m, uv[m % 2], cv)):
                            pp = pps[cnt % 7]; cnt += 1
                            for k in range(8):
                                mm(pp[:], wu[:, k, mt * 128:(mt + 1) * 128], h2[:, k, :], start=(k == 0), stop=(k == 7))
                            cp("act", ubuf[:], pp[:])
                            e_ = "dve"
                            ts(e_, cdst[:], ubuf[:, 0:256], cw[:, 0, mt:mt + 1], cb[:, mt:mt + 1], ALU.mult, ALU.add)
                            stt(e_, cdst[:], ubuf[:, 1:257], cw[:, 1, mt:mt + 1], cdst[:], ALU.mult, ALU.add)
                            stt(e_, cdst[:], ubuf[:, 2:258], cw[:, 2, mt:mt + 1], cdst[:], ALU.mult, ALU.add)
                        act(sg[:], cg[:], AF.Silu)
                        tt("dve", aT[:, m, :], sg[:], cv[:], ALU.mult)
                    for mo in range(8):
                        pp = pps[cnt % 7]; cnt += 1
                        for k in range(22):
                            mm(pp[:, 0:256], wd[:, k, mo * 128:(mo + 1) * 128], aT[:, k, :], start=(k == 0), stop=(k == 21))
                        cp("act" if mo % 2 else "dve", y[:, mo, :], pp[:, 0:256])
                    act(sq[:, :, 0:256], y[:], AF.Square)
                    for k in range(8):
                        mm(pms[:, 0:256], cmb[:, M_ONES1024, :], sq[:, k, 0:256], start=(k == 0), stop=(k == 7))
                    rstd_from_ps(rs[:, 0:256], pms[:, 0:256])
                    tt("dve", y[:], y[:], rs[:, 0:256].unsqueeze(1).bc([128, 8, 256]), ALU.mult)
                    for k in range(8):
                        stt("dve", y[:, k, :], y[:, k, :], mod_sb[:, l, ci, 40 + k:41 + k], x[:, k, 1:257], ALU.mult, ALU.add)
                    dma(XT[l + 1].k(b)[:, c0:c0 + 256].rr("(k p) n -> p k n", p=128), y[:], q="q1")
        P.barrier()


    def stage_ssd(l):
        with ExitStack() as s:
            cw = sb(s, "sd_cw", [128, 3, 4]); cb = sb(s, "sd_cb", [128, 4])
            for w_ in range(3):
                dma(cw[:, w_, :], c_conv_w[l, w_, :].rr("(m p) -> p m", p=128), allow_slow_non_contiguous=True)
            dma(cb[:], c_conv_b[l, :].rr("(m p) -> p m", p=128), allow_slow_non_contiguous=True)
            dtb = sb(s, "sd_dtb", [8, 1]); dma(dtb[:], c_dt_bias[l, :].rr("(d o) -> d o", o=1), allow_slow_non_contiguous=True)
            An = sb(s, "sd_An", [128, 8]); dma(An[:], c_a_log[l, :].pb(128))
            act(An[:], An[:], AF.Exp)
            ts("dve", An[:], An[:], -1.0, None, ALU.mult)
            Dsk = sb(s, "sd_D", [128, 4]); dma(Dsk[:], c_d[l, :].pb(128))
            cn = sb(s, "sd_cn", [128, 2]); dma(cn[:], c_norm[l, :].rr("(m p) -> p m", p=128), allow_slow_non_contiguous=True)
            onesf = sb(s, "sd_1", [128, 128]); memset("pool", onesf[:], 1.0)
            HT = [sb(s, "sd_HT%d" % i, [128, 2, 64]) for i in range(2)]
            HTb = [sb(s, "sd_HTb%d" % i, [128, 2, 64], BF16) for i in range(2)]
            ysum = sb(s, "sd_ys", [128, NS // 128, 256])
            u = sb(s, "sd_u", [128, 4, 130]); xc = sb(s, "sd_xc", [128, 4, 128])
            dT = sb(s, "sd_dT", [8, 128])

            def mk_set(i):
                pkf = sb(s, "sd_pkf%d" % i, [128, 848]); pkb = sb(s, "sd_pkb%d" % i, [128, 896], BF16)
                vf = lambda lo_, hi_, pat=None, **kw: T(pkf.t, ap=(pkf.t[:, lo_:hi_] if pat is None else pkf.t[:, lo_:hi_].rearrange(pat, **kw)))
                vb = lambda lo_, hi_, pat=None, **kw: T(pkb.t, ap=(pkb.t[:, lo_:hi_] if pat is None else pkb.t[:, lo_:hi_].rearrange(pat, **kw)))
                d_ = dict(Xt=vf(0, 256), CBs=vf(256, 512, "p (a b) -> p a b", a=2), zT=vf(512, 768, "p (a b) -> p a b", a=2),
                          dtt=vf(768, 776), at=vf(776, 784), cums=vf(784, 808, "p (a b) -> p a b", a=3), nac=vf(808, 816),
                          ecol=vf(816, 824), dcol=vf(824, 832), cdec=vf(832, 840),
                          Btb=vb(0, 128), BCb=vb(128, 384, "p (a b) -> p a b", a=2), BCm=vb(384, 896, "p (a b c) -> p a b c", a=2, b=2))
                fb = [d_[k].buf for k in ("Xt", "CBs", "zT", "dtt", "at", "cums", "nac", "ecol", "dcol", "cdec")]
                bb_ = [d_[k].buf for k in ("Btb", "BCb", "BCm")]
                d_["pkf"] = A(pkf.t[:, :], fb); d_["pkb"] = A(pkb.t[:, :], bb_)
                return d_

            psets = [mk_set(0), mk_set(1)]
            zT = Xt = Btb = dtt = at = cums = nac = ecol = dcol = cdec = BCb = BCm = CBs = None
            SSDF = SSDFs[l]; SSDB = SSDBs[l]
            abc4 = sb(s, "sd_abc", [128, 4, 128]); Dm4 = sb(s, "sd_Dm", [128, 4, 128]); Gb4 = sb(s, "sd_Gb", [128, 4, 128], BF16)
            Xdt4 = sb(s, "sd_Xdt", [128, 4, 64], BF16); Xw4 = sb(s, "sd_Xw", [128, 4, 64], BF16); tmpy4 = sb(s, "sd_ty", [128, 4, 64])
            gz = sb(s, "sd_gz", [128, 2, 128]); sqz = sb(s, "sd_sq", [128, 2, 128], BF16); rz = sb(s, "sd_rz", [128, 128])
            yo = sb(s, "sd_yo", [128, 2, 128], BF16); hfo = sb(s, "sd_hf", [64, 64])
            pT = pst(s, "sd_pT", [128, 512]); pC = pst(s, "sd_pC", [128, 3, 8]); pCB = pst(s, "sd_pCB", [128, 2, 128])
            pD4 = pst(s, "sd_pD", [128, 4, 128]); pY4 = pst(s, "sd_pY", [128, 4, 128]); pS4 = pst(s, "sd_pS4", [128, 4, 64])
            pF = pst(s, "sd_pF", [128, 2, 128]); pM = pst(s, "sd_pM", [128, 128])
            PJa = PJ.k("all")

            def tr(out, in_, idn):
                P.op("pe", lambda: nc.tensor.transpose(out.ap, in_.ap, idn.ap), bufs(in_, idn), bufs(out))

            for (col0, Ls, ci, is_s, pidx) in seqs:
                nC = Ls // 128
                for dr in range(2):
                    if is_s:
                        for h in range(4):
                            g = h // 2; hh = h % 2
                            dma(hfo[:], sssd[l, dr, h, :, :])
                            tr(pS4[0:64, 0, :], hfo[:], ident[0:64, 0:64])
                            cp("dve", HT[dr][64 * g:64 * g + 64, hh, :], pS4[0:64, 0, :])
                    else:
                        memset("pool", HT[dr][:], 0.0)
                    cp("act", HTb[dr][:], HT[dr][:])

                def prep(c):
                    g0 = col0 + c * 128
                    lo = 1 if c == 0 else 0
                    hi = 129 if c == nC - 1 else 130
                    if c == 0:
                        memset("pool", u[:, :, 0:1], 0.0)
                    if c == nC - 1:
                        memset("pool", u[:, :, 129:130], 0.0)
                    dma(u[:, :, lo:hi], PJa[9 * 128:13 * 128, g0 - 1 + lo:g0 - 1 + hi].rr("(m p) n -> p m n", p=128))
                    dma(zT[:], PJa[7 * 128:9 * 128, g0:g0 + 128].rr("(m p) n -> p m n", p=128))
                    dma(dT[:], PJa[13 * 128:13 * 128 + 8, g0:g0 + 128])
                    for m in range(4):
                        ts("dve", xc[:, m, :], u[:, m, 0:128], cw[:, 0, m:m + 1], cb[:, m:m + 1], ALU.mult, ALU.add)
                        stt("dve", xc[:, m, :], u[:, m, 1:129], cw[:, 1, m:m + 1], xc[:, m, :], ALU.mult, ALU.add)
                        stt("dve", xc[:, m, :], u[:, m, 2:130], cw[:, 2, m:m + 1], xc[:, m, :], ALU.mult, ALU.add)
                    act(xc[:], xc[:], AF.Silu)
                    act(dT[:], dT[:], AF.Exp, bias=dtb[:, 0:1], scale=1.0)
                    act(dT[:], dT[:], AF.Ln, bias=1.0, scale=1.0)
                    for m in range(3):
                        tr(pT[:, m * 128:(m + 1) * 128], xc[:, m, :], ident[:])
                    tr(pT[:, 384:392], dT[:], ident[0:8, 0:8])
                    cp("act", Xt[:], pT[:, 0:256])
                    cp("act", Btb[:], pT[:, 256:384])
                    cp("dve", dtt[:], pT[:, 384:392])
                    tt("dve", at[:], dtt[:], An[:], ALU.mult)
                    mm(pC[:, 0, :], cm[:, M_TRII, :], at[:])
                    mm(pC[:, 1, :], cm[:, M_TRIE, :], at[:])
                    mm(pC[:, 2, :], cm[:, M_ONES, :], at[:])
                    cp("dve", cums[:], pC[:])
                    ts("dve", nac[:], cums[:, 0, :], -1.0, None, ALU.mult)
                    act(ecol[:, 0:4], cums[:, 0, 0:4], AF.Exp)
                    tt("dve", ecol[:, 4:8], cums[:, 2, 4:8], cums[:, 1, 4:8], ALU.subtract)
                    act(ecol[:, 4:8], ecol[:, 4:8], AF.Exp)
                    tt("dve", dcol[:, 0:4], cums[:, 2, 0:4], cums[:, 0, 0:4], ALU.subtract)
                    act(dcol[:, 0:4], dcol[:, 0:4], AF.Exp)
                    act(dcol[:, 4:8], cums[:, 1, 4:8], AF.Exp)
                    act(cdec[:], cums[:, 2, :], AF.Exp)
                    cp("act", BCb[:], xc[:, 2:4, :])
                    for g in range(2):
                        ts("dve", BCm[:, 0, g, :], xc[:, 2, :], misc[:, 10 + g:11 + g], None, ALU.mult)
                        ts("dve", BCm[:, 1, g, :], xc[:, 3, :], misc[:, 10 + g:11 + g], None, ALU.mult)
                    for g in range(2):
                        mm(pCB[:, g, :], BCm[:, 0, g, :], BCb[:, 1, :])
                    cp("dve", CBs[:], pCB[:])

                def heads(c, dr, first_pass):
                    H4 = range(4)
                    gs_ = [slice(64 * (h // 2), 64 * (h // 2) + 64) for h in H4]
                    cols = [dr * 4 + h for h in H4]
                    for h in H4:
                        act(abc4[:, h, :], onesf[:], AF.Identity, scale=at[:, cols[h]:cols[h] + 1])
                    for h in H4:
                        if dr == 0:
                            mm(pD4[:, h, :], abc4[:, h, :], cm[:, M_TRII, :], start=True, stop=False)
                            mm(pD4[:, h, :], ident[:], cm[:, M_NEGF, :], start=False, stop=True)
                        else:
                            mm(pD4[:, h, :], abc4[:, h, :], cm[:, M_TRIE, :], start=True, stop=False)
                            mm(pD4[:, h, :], ident[:], cm[:, M_POSB, :], start=False, stop=True)
                    for h in H4:
                        if dr == 0:
                            act(Dm4[:, h, :], pD4[:, h, :], AF.Exp, bias=nac[:, cols[h]:cols[h] + 1], scale=1.0)
                        else:
                            act(Dm4[:, h, :], pD4[:, h, :], AF.Exp, bias=cums[:, 1, cols[h]:cols[h] + 1], scale=-1.0)
                    for h in H4:
                        tt("dve", Gb4[:, h, :], CBs[:, h // 2, :], Dm4[:, h, :], ALU.mult)
                        ts("dve", Xdt4[:, h, :], Xt[:, h * 64:(h + 1) * 64], dtt[:, cols[h]:cols[h] + 1], None, ALU.mult)
                    for h in H4:
                        mm(pY4[:, h, 0:64], Gb4[:, h, :], Xdt4[:, h, :])
                        mm(pY4[:, h, 64:128], BCm[:, 1, h // 2, :], HTb[dr][:, h % 2, :])
                    for h in H4:
                        act(tmpy4[:, h, :], pY4[:, h, 64:128], AF.Identity, scale=ecol[:, cols[h]:cols[h] + 1])
                    for h in H4:
                        ysl = ysum[:, c, h * 64:(h + 1) * 64]
                        if first_pass:
                            tt("dve", ysl, tmpy4[:, h, :], pY4[:, h, 0:64], ALU.add)
                            stt("dve", ysl, Xt[:, h * 64:(h + 1) * 64], Dsk[:, h:h + 1], ysl, ALU.mult, ALU.add)
                        else:
                            tt("dve", ysl, ysl, tmpy4[:, h, :], ALU.add)
                            tt("dve", ysl, ysl, pY4[:, h, 0:64], ALU.add)
                    for h in H4:
                        ts("dve", Xw4[:, h, :], Xdt4[:, h, :], dcol[:, cols[h]:cols[h] + 1], None, ALU.mult)
                    for h in H4:
                        mm(pS4[:, h, :], Btb[:], Xw4[:, h, :])
                    for h in H4:
                        stt("dve", HT[dr][gs_[h], h % 2, :], HT[dr][gs_[h], h % 2, :], cdec[gs_[h], cols[h]:cols[h] + 1], pS4[gs_[h], h, :], ALU.mult, ALU.add)
                    for h in H4:
                        cp("act", HTb[dr][gs_[h], h % 2, :], HT[dr][gs_[h], h % 2, :])

                def use(ps):
                    nonlocal zT, Xt, Btb, dtt, at, cums, nac, ecol, dcol, cdec, BCb, BCm, CBs
                    zT, Xt, Btb, dtt, at, cums, nac = ps["zT"], ps["Xt"], ps["Btb"], ps["dtt"], ps["at"], ps["cums"], ps["nac"]
                    ecol, dcol, cdec, BCb, BCm, CBs = ps["ecol"], ps["dcol"], ps["cdec"], ps["BCb"], ps["BCm"], ps["CBs"]

                def gci(c):
                    return (col0 + c * 128) // 128

                for c in range(nC):
                    ps = psets[c % 2]
                    use(ps)
                    prep(c)
                    dma(SSDF.k(gci(c))[gci(c), :, :], ps["pkf"])
                    dma(SSDB.k(gci(c))[gci(c), :, :], ps["pkb"])
                    heads(c, 0, True)
                def load_set(c):
                    ps_ = psets[c % 2]
                    dma(ps_["pkf"], SSDF.k(gci(c))[gci(c), :, :])
                    dma(ps_["pkb"], SSDB.k(gci(c))[gci(c), :, :])

                load_set(nC - 1)
                for c in range(nC - 1, -1, -1):
                    if c - 1 >= 0:
                        load_set(c - 1)
                    use(psets[c % 2])
                    heads(c, 1, False)
                    g0 = col0 + c * 128
                    b = g0 // 256
                    for m in range(2):
                        tr(pF[:, m, :], ysum[:, c, m * 128:(m + 1) * 128], ident[:])
                    act(gz[:], zT[:], AF.Silu)
                    tt("dve", gz[:], gz[:], pF[:], ALU.mult)
                    act(sqz[:], gz[:], AF.Square)
                    for m in range(2):
                        mm(pM[:], cmb[:, M_ONES256, :], sqz[:, m, :], start=(m == 0), stop=(m == 1))
                    rstd_from_ps(rz[:], pM[:])
                    for m in range(2):
                        stt("dve", yo[:, m, :], gz[:, m, :], cn[:, m:m + 1], rz[:], ALU.mult, ALU.mult)
                    dma(YC.k(("c", b))[512:768, g0:g0 + 128].rr("(m p) n -> p m n", p=128), yo[:], q="q1")
                if not is_s:
                    for dr in range(2):
                        for h in range(4):
                            g = h // 2; hh = h % 2
                            gs = slice(64 * g, 64 * g + 64)
                            tr(pM[0:64, :], HT[dr][:, hh, :], ident[:])
                            cp("dve", hfo[:], pM[0:64, 64 * g:64 * g + 64])
                            dma(o_ssd[pidx, l, dr, h, :, :], hfo[:], q="q1")
        P.barrier()

    def stage_s5(l):
        TWO_PI = 2.0 * PI
        with ExitStack() as s:
            W1 = [Pair([sb(s, "s5_W1_%d%d" % (d, ri), [128, 1024]) for ri in range(2)]) for d in range(2)]
            W2 = [Pair([sb(s, "s5_W2_%d%d" % (d, ri), [128, 8, 136]) for ri in range(2)]) for d in range(2)]
            BD = [[[sb(s, "s5_BD%d%d%d" % (d, ct, ri), [128, 512], BF16) for ri in range(2)] for ct in range(2)] for d in range(2)]
            Cm = [[sb(s, "s5_Cm%d%d" % (d, ri), [128, 8, 128], BF16) for ri in range(2)] for d in range(2)]
            ddc = sb(s, "s5_dd", [128, 2]); dma(ddc[:], d_d[l, :].rr("(m p) -> p m", p=128), allow_slow_non_contiguous=True)
            wg = load_weight_bf16(s, "s5_wg", lambda k: d_glu.k(0)[l, k * 128:(k + 1) * 128, :], 512, 2, chunk=512)

            def tr(out, in_, idn):
                P.op("pe", lambda: nc.tensor.transpose(out.ap, in_.ap, idn.ap), bufs(in_, idn), bufs(out))

            with ExitStack() as s2:
                W = 1032
                tf = sb(s2, "s5_tf", [128, W]); ti = sb(s2, "s5_ti", [128, W], I32); tm = sb(s2, "s5_tm", [128, W])
                rr_ = sb(s2, "s5_r", [128, W]); ph = sb(s2, "s5_ph", [128, W]); mg = sb(s2, "s5_mg", [128, W])
                sn = sb(s2, "s5_sn", [128, W]); cs = sb(s2, "s5_cs", [128, W])
                lre = sb(s2, "s5_lre", [128, 1024]); lim = sb(s2, "s5_lim", [128, 1024]); dl = sb(s2, "s5_dl", [128, 16])
                lrc = sb(s2, "s5_lrc", [128, 8]); lic = sb(s2, "s5_lic", [128, 8]); dlc = sb(s2, "s5_dlc", [128, 8])
                er = sb(s2, "s5_er", [128, 2, 129])
                for d in range(2):
                    dma(er[:, d, :], c_erow[d, :, :])
                Bn = sb(s2, "s5_Bn", [64, 16, 32]); Cn = sb(s2, "s5_Cn", [128, 2, 128])
                lr_r = sb(s2, "s5_lrr", [128, 64]); li_r = sb(s2, "s5_lir", [128, 64]); dl_r = sb(s2, "s5_dlr", [128, 1])
                q = [sb(s2, "s5_q%d" % i, [128, 64]) for i in range(10)]
                tmpC = sb(s2, "s5_tmpC", [128, 128])
                pTr = pst(s2, "s5_pTr", [128, 128])

                def trig(n):
                    for dst, off in ((sn, 0.0), (cs, PI / 2)):
                        ts("dve", rr_[:, 0:n], ph[:, 0:n], off, None, ALU.add)
                        ts("dve", tf[:, 0:n], rr_[:, 0:n], 1.0 / TWO_PI, None, ALU.mult)
                        cp("dve", ti[:, 0:n], tf[:, 0:n])
                        cp("dve", tf[:, 0:n], ti[:, 0:n])
                        stt("dve", rr_[:, 0:n], tf[:, 0:n], -TWO_PI, rr_[:, 0:n], ALU.mult, ALU.add)
                        ts("dve", tm[:, 0:n], rr_[:, 0:n], PI, -TWO_PI, ALU.is_gt, ALU.mult)
                        tt("dve", rr_[:, 0:n], rr_[:, 0:n], tm[:, 0:n], ALU.add)
                        ts("dve", tm[:, 0:n], rr_[:, 0:n], -PI, TWO_PI, ALU.is_lt, ALU.mult)
                        tt("dve", rr_[:, 0:n], rr_[:, 0:n], tm[:, 0:n], ALU.add)
                        act(dst[:, 0:n], rr_[:, 0:n], AF.Sin)

                for d in range(2):
                    dma(lre[:], d_lam_re[l, d, :, :].rr("g n -> (g n)").pb(128))
                    dma(lim[:], d_lam_im[l, d, :, :].rr("g n -> (g n)").pb(128))
                    dma(dl[:], d_log_step[l, d, :].pb(128))
                    act(dl[:], dl[:], AF.Exp)
                    dlb = dl[:].unsqueeze(2).bc([128, 16, 64])
                    tt("dve", lre[:].rr("p (g n) -> p g n", g=16), lre[:].rr("p (g n) -> p g n", g=16), dlb, ALU.mult)
                    tt("dve", lim[:].rr("p (g n) -> p g n", g=16), lim[:].rr("p (g n) -> p g n", g=16), dlb, ALU.mult)
                    ts("dve", ph[:, 0:1024], lim[:], misc[:, d:d + 1], None, ALU.mult)
                    act(mg[:, 0:1024], lre[:], AF.Exp, scale=misc[:, 16 + d:17 + d])
                    trig(1024)
                    tt("dve", W1[d][:, 0, :], mg[:, 0:1024], cs[:, 0:1024], ALU.mult)
                    stt("dve", W1[d][:, 1, :], mg[:, 0:1024], -1.0, sn[:, 0:1024], ALU.mult, ALU.mult)
                    for t in range(8):
                        dma(lrc[:, t:t + 1], d_lam_re[l, d, 2 * t:2 * t + 2, :].rr("g n -> (g n)").rr("(p o) -> p o", o=1), allow_slow_non_contiguous=True)
                        dma(lic[:, t:t + 1], d_lam_im[l, d, 2 * t:2 * t + 2, :].rr("g n -> (g n)").rr("(p o) -> p o", o=1), allow_slow_non_contiguous=True)
                        for g2 in range(2):
                            dma(dlc[64 * g2:64 * g2 + 64, t:t + 1], d_log_step[l, d, 2 * t + g2:2 * t + g2 + 1].pb(64))
                    act(dlc[:], dlc[:], AF.Exp)
                    tt("dve", lrc[:], lrc[:], dlc[:], ALU.mult)
                    tt("dve", lic[:], lic[:], dlc[:], ALU.mult)
                    erb = er[:, d, :].unsqueeze(1).bc([128, 8, 129])
                    tt("dve", ph[:, 0:1032].rr("p (t c) -> p t c", t=8), lic[:].unsqueeze(2).bc([128, 8, 129]), erb, ALU.mult)
                    tt("dve", mg[:, 0:1032].rr("p (t c) -> p t c", t=8), lrc[:].unsqueeze(2).bc([128, 8, 129]), erb, ALU.mult)
                    act(mg[:, 0:1032], mg[:, 0:1032], AF.Exp)
                    trig(1032)
                    tt("dve", W2[d][:, 0, :, 0:129], mg[:, 0:1032].rr("p (t c) -> p t c", t=8), cs[:, 0:1032].rr("p (t c) -> p t c", t=8), ALU.mult)
                    tt("dve", W2[d][:, 1, :, 0:129], mg[:, 0:1032].rr("p (t c) -> p t c", t=8), sn[:, 0:1032].rr("p (t c) -> p t c", t=8), ALU.mult)
                    dma(Bn[:], d_b[l, d, :, :, :, :].rr("g n c r -> n g (c r)"))
                    for ct in range(2):
                        dma(lr_r[:], d_lam_re[l, d, 8 * ct:8 * ct + 8, :].unsqueeze(1).bc([8, 16, 64]))
                        dma(li_r[:], d_lam_im[l, d, 8 * ct:8 * ct + 8, :].unsqueeze(1).bc([8, 16, 64]))
                        dma(dl_r[:], d_log_step[l, d, 8 * ct:8 * ct + 8].unsqueeze(1).unsqueeze(2).bc([8, 16, 1]))
                        act(dl_r[:], dl_r[:], AF.Exp)
                        ts("dve", ph[:, 0:64], li_r[:], dl_r[:, 0:1], None, ALU.mult)
                        act(mg[:, 0:64], lr_r[:], AF.Exp, scale=dl_r[:, 0:1])
                        trig(64)
                        tt("dve", q[0][:], mg[:, 0:64], cs[:, 0:64], ALU.mult)
                        ts("dve", q[0][:], q[0][:], -1.0, None, ALU.add)
                        tt("dve", q[1][:], mg[:, 0:64], sn[:, 0:64], ALU.mult)
                        tt("dve", q[2][:], lr_r[:], lr_r[:], ALU.mult)
                        tt("dve", q[3][:], li_r[:], li_r[:], ALU.mult)
                        tt("dve", q[2][:], q[2][:], q[3][:], ALU.add)
                        recip(q[2][:], q[2][:])
                        tt("dve", q[3][:], q[0][:], lr_r[:], ALU.mult)
                        tt("dve", q[4][:], q[1][:], li_r[:], ALU.mult)
                        tt("dve", q[3][:], q[3][:], q[4][:], ALU.add)
                        tt("dve", q[3][:], q[3][:], q[2][:], ALU.mult)
                        tt("dve", q[4][:], q[1][:], lr_r[:], ALU.mult)
                        tt("dve", q[5][:], q[0][:], li_r[:], ALU.mult)
                        tt("dve", q[4][:], q[4][:], q[5][:], ALU.subtract)
                        tt("dve", q[4][:], q[4][:], q[2][:], ALU.mult)
                        for r in range(2):
                            tr(pTr[:, 0:64], Bn[:, 8 * ct:8 * ct + 8, r:32:2], ident[0:64, 0:64])
                            cp("dve", q[6 + r][:], pTr[:, 0:64])
                        tt("dve", q[8][:], q[3][:], q[6][:], ALU.mult)
                        tt("dve", q[9][:], q[4][:], q[7][:], ALU.mult)
                        tt("dve", q[8][:], q[8][:], q[9][:], ALU.subtract)
                        tt("dve", q[9][:], q[3][:], q[7][:], ALU.mult)
                        tt("dve", q[5][:], q[4][:], q[6][:], ALU.mult)
                        tt("dve", q[9][:], q[9][:], q[5][:], ALU.add)
                        for ri in range(2):
                            tt("dve", BD[d][ct][ri][:].rr("p (g n) -> p g n", g=8), q[8 + ri][:].unsqueeze(1).bc([128, 8, 64]),
                               misc[:, 20:28].unsqueeze(2).bc([128, 8, 64]), ALU.mult)
                    dma(Cn[:], d_c[l, d, :, :].rr("(ct p) k -> p ct k", p=128))
                    for ct in range(2):
                        for ri in range(2):
                            tr(pTr[0:64, :], Cn[:, ct, ri:128:2], ident[:])
                            sgn = 1.0 if ri == 0 else -1.0
                            ts("dve", tmpC[0:64, :], pTr[0:64, :], sgn, None, ALU.mult)
                            ts("dve", tmpC[64:128, :], pTr[0:64, :], sgn, None, ALU.mult)
                            for tl in range(4):
                                t = 4 * ct + tl
                                for g2 in range(2):
                                    gl = 2 * tl + g2
                                    ps_ = slice(64 * g2, 64 * g2 + 64)
                                    tt("dve", Cm[d][ri][ps_, t, :], tmpC[ps_, :], m16[ps_, gl, :], ALU.mult)
                P.barrier()

            ub = sb(s, "s5_ub", [128, 2, NS], BF16)
            ys = sb(s, "s5_ys", [128, 2, NS])
            SL = []
            for ct in range(2):
                SL.append(dict(
                    bu=Pair([sb(s, "s5_bu%d%d" % (ct, ri), [128, 512]) for ri in range(2)]),
                    tA=[sb(s, "s5_t%d%d" % (ct, i), [128, 512]) for i in range(4)],
                    Z=Pair([sb(s, "s5_Z%d%d" % (ct, ri), [128, 512], BF16) for ri in range(2)]),
                    Sc=Pair([sb(s, "s5_Sc%d%d" % (ct, ri), [128, 4, 128]) for ri in range(2)]),
                    X=Pair([sb(s, "s5_X%d%d" % (ct, ri), [128, 4, 128], BF16) for ri in range(2)]),
                    cq=[sb(s, "s5_cq%d%d" % (ct, i), [128, 4]) for i in range(4)]))
            car = [[sb(s, "s5_car%d%d" % (d, ct), [128, 2, 4]) for ct in range(2)] for d in range(2)]
            h0t = sb(s, "s5_h0", [128, 2, 8]); cq0 = [sb(s, "s5_cqi%d" % i, [128, 4]) for i in range(2)]
            xfin = [sb(s, "s5_xf%d" % d, [128, 8, 2]) for d in range(2)]
            uf = sb(s, "s5_uf", [128, 2, 256]); ubs = sb(s, "s5_ubs", [128, 2, 256])
            yv = sb(s, "s5_yv", [128, 2, 256]); y2 = sb(s, "s5_y2", [128, 2, 256]); yg = sb(s, "s5_yg", [128, 2, 256], BF16)
            sgm = sb(s, "s5_sg", [128, 2, 256]); yd = sb(s, "s5_yd", [128, 2, 256], BF16)
            PJa = PJ.k("all")
            v4 = lambda a_: a_.rr("p (a b) -> p a b", a=4)
            for (col0, Ls, ci, is_s, pidx) in seqs:
                nC = Ls // 128
                for bb in range(Ls // 256):
                    dma(ubs[:], PJa[14 * 128:16 * 128, col0 + bb * 256:col0 + (bb + 1) * 256].rr("(m p) n -> p m n", p=128))
                    cp("act", ub[:, :, bb * 256:(bb + 1) * 256], ubs[:])
                for d in range(2):
                    a1c = 1 if d == 0 else 126
                    if is_s:
                        for t in range(8):
                            for r in range(2):
                                dma(h0t[:, r, t:t + 1], A(ss5.t[l, d, 2 * t:2 * t + 2, :, r].rearrange("g n -> (g n)").rearrange("(p o) -> p o", o=1), [ss5.buf]),
                                    allow_slow_non_contiguous=True)
                        for ct in range(2):
                            tsl = slice(4 * ct, 4 * ct + 4)
                            tt("dve", cq0[0][:], W2[d][:, 0, tsl, a1c], h0t[:, 0, tsl], ALU.mult)
                            tt("dve", cq0[1][:], W2[d][:, 1, tsl, a1c], h0t[:, 1, tsl], ALU.mult)
                            tt("dve", car[d][ct][:, 0, :], cq0[0][:], cq0[1][:], ALU.subtract)
                            tt("dve", cq0[0][:], W2[d][:, 0, tsl, a1c], h0t[:, 1, tsl], ALU.mult)
                            tt("dve", cq0[1][:], W2[d][:, 1, tsl, a1c], h0t[:, 0, tsl], ALU.mult)
                            tt("dve", car[d][ct][:, 1, :], cq0[0][:], cq0[1][:], ALU.add)
                    else:
                        for ct in range(2):
                            memset("pool", car[d][ct][:], 0.0)
                with ExitStack() as s3:
                    pBU = [pst(s3, "s5_pBU%d" % i, [128, 512]) for i in range(2)]
                    pSS = [pst(s3, "s5_pS%d" % i, [128, 2, 4, 128]) for i in range(2)]
                    pYY = [pst(s3, "s5_pY%d" % i, [128, 128]) for i in range(2)]
                    for d in range(2):
                        order = range(nC) if d == 0 else range(nC - 1, -1, -1)
                        last = 127 if d == 0 else 0
                        tri = M_TRII if d == 0 else M_TRIGE
                        for c in order:
                            is_final = (c == nC - 1) if d == 0 else (c == 0)
                            C2 = range(2)
                            for ri in range(2):
                                for ct in C2:
                                    mm(pBU[ct][:], ub[:, ct, c * 128:(c + 1) * 128], BD[d][ct][ri][:])
                                for ct in C2:
                                    cp("act", SL[ct]["bu"][:, ri, :], pBU[ct][:])
                            for ct in C2:
                                bu = SL[ct]["bu"]; tA = SL[ct]["tA"]
                                w1r = W1[d][:, 0, ct * 512:(ct + 1) * 512]; w1i = W1[d][:, 1, ct * 512:(ct + 1) * 512]
                                tt("dve", tA[0][:], bu[:, 0, :], w1r, ALU.mult)
                                tt("dve", tA[1][:], bu[:, 1, :], w1i, ALU.mult)
                                tt("dve", tA[2][:], bu[:, 0, :], w1i, ALU.mult)
                                tt("dve", tA[3][:], bu[:, 1, :], w1r, ALU.mult)
                            for ct in C2:
                                tA = SL[ct]["tA"]; Z = SL[ct]["Z"]
                                tt("pool", Z[:, 0, :], tA[0][:], tA[1][:], ALU.subtract)
                                tt("pool", Z[:, 1, :], tA[2][:], tA[3][:], ALU.add)
                            for ct in C2:
                                Z = SL[ct]["Z"]
                                for ri in range(2):
                                    for tl in range(4):
                                        mm(pSS[ct][:, ri, tl, :], Z[:, ri, tl * 128:(tl + 1) * 128], cmb[:, tri, :])
                            for ct in C2:
                                Sc = SL[ct]["Sc"]; cr = car[d][ct]
                                for ri in range(2):
                                    tt("dve", Sc[:, ri, :, :], pSS[ct][:, ri, :, :], cr[:, ri, :].unsqueeze(2).bc([128, 4, 128]), ALU.add)
                            for ct in C2:
                                Sc = SL[ct]["Sc"]; cq = SL[ct]["cq"]; cr = car[d][ct]
                                for (colx, dst_re, dst_im, doit) in ((127 if d == 0 else 0, xfin[d][:, 4 * ct:4 * ct + 4, 0], xfin[d][:, 4 * ct:4 * ct + 4, 1], is_final and not is_s),
                                                                     (128, cr[:, 0, :], cr[:, 1, :], True)):
                                    if not doit:
                                        continue
                                    ar = W2[d][:, 0, 4 * ct:4 * ct + 4, colx]; ai = W2[d][:, 1, 4 * ct:4 * ct + 4, colx]
                                    sr = Sc[:, 0, :, last]; si = Sc[:, 1, :, last]
                                    tt("dve", cq[0][:], ar, sr, ALU.mult)
                                    tt("dve", cq[1][:], ai, si, ALU.mult)
                                    tt("dve", cq[2][:], ar, si, ALU.mult)
                                    tt("dve", cq[3][:], ai, sr, ALU.mult)
                                    tt("dve", dst_re, cq[0][:], cq[1][:], ALU.subtract)
                                    tt("dve", dst_im, cq[2][:], cq[3][:], ALU.add)
                            for ct in C2:
                                Sc = SL[ct]["Sc"]; tA = SL[ct]["tA"]
                                w2r = W2[d][:, 0, 4 * ct:4 * ct + 4, 0:128]; w2i = W2[d][:, 1, 4 * ct:4 * ct + 4, 0:128]
                                tt("dve", v4(tA[0][:]), w2r, Sc[:, 0, :, :], ALU.mult)
                                tt("dve", v4(tA[1][:]), w2i, Sc[:, 1, :, :], ALU.mult)
                                tt("dve", v4(tA[2][:]), w2r, Sc[:, 1, :, :], ALU.mult)
                                tt("dve", v4(tA[3][:]), w2i, Sc[:, 0, :, :], ALU.mult)
                            for ct in C2:
                                tA = SL[ct]["tA"]; X = SL[ct]["X"]
                                tt("pool", X[:, 0, :, :], v4(tA[0][:]), v4(tA[1][:]), ALU.subtract)
                                tt("pool", X[:, 1, :, :], v4(tA[2][:]), v4(tA[3][:]), ALU.add)
                            for ct in C2:
                                X = SL[ct]["X"]
                                i_ = 0
                                for ri in range(2):
                                    for tl in range(4):
                                        mm(pYY[ct][:], Cm[d][ri][:, 4 * ct + tl, :], X[:, ri, tl, :], start=(i_ == 0), stop=(i_ == 7))
                                        i_ += 1
                            for ct in C2:
                                ysl = ys[:, ct, c * 128:(c + 1) * 128]
                                if d == 0:
                                    cp("act", ysl, pYY[ct][:])
                                else:
                                    tt("dve", ysl, ysl, pYY[ct][:], ALU.add)
                if not is_s:
                    for d in range(2):
                        for t in range(8):
                            for r in range(2):
                                dma(A(o_s5.t[pidx, l, d, 2 * t:2 * t + 2, :, r].rearrange("g n -> (g n)").rearrange("(p o) -> p o", o=1), [o_s5.buf]),
                                    xfin[d][:, t, r:r + 1], q="q1", allow_slow_non_contiguous=True)
                with ExitStack() as s3:
                    pG = pst(s3, "s5_pG", [128, 4, 256])
                    for bb in range(Ls // 256):
                        c0 = col0 + bb * 256
                        b = c0 // 256
                        sl = slice(bb * 256, (bb + 1) * 256)
                        dma(uf[:], PJa[14 * 128:16 * 128, c0:c0 + 256].rr("(m p) n -> p m n", p=128))
                        for ct in range(2):
                            stt("dve", yv[:, ct, :], uf[:, ct, :], ddc[:, ct:ct + 1], ys[:, ct, sl], ALU.mult, ALU.add)
                        act(y2[:], yv[:], AF.Square)
                        ts("dve", y2[:], y2[:], 0.044715, 1.0, ALU.mult, ALU.add)
                        tt("dve", y2[:], y2[:], yv[:], ALU.mult)
                        act(sgm[:], y2[:], AF.Sigmoid, scale=1.5957691216057308)
                        tt("dve", yg[:], yv[:], sgm[:], ALU.mult)
                        for mo in range(4):
                            for k in range(2):
                                mm(pG[:, mo, :], wg[:, k, mo * 128:(mo + 1) * 128], yg[:, k, :], start=(k == 0), stop=(k == 1))
                        act(sgm[:], pG[:, 2:4, :], AF.Sigmoid)
                        tt("dve", yd[:], sgm[:], pG[:, 0:2, :], ALU.mult)
                        dma(YC.k(("d", b))[768:1024, c0:c0 + 256].rr("(m p) n -> p m n", p=128), yd[:], q="q1")
        P.barrier()

    def stage_stub(l, which):
        with ExitStack() as s:
            z = sb(s, "stubz", [128, 2, 256], BF16)
            memset("pool", z[:], 0.0)
            r0 = 512 if which == "c" else 768
            for b in range(NBLK):
                c0 = b * 256
                dma(YC.k((which, b))[r0:r0 + 256, c0:c0 + 256].rr("(m p) n -> p m n", p=128), z[:], q="q1")
        P.barrier()

    P.marks = []

    def mark(name):
        P.marks.append((name, P.engs["pe"].n, P.engs["act"].n, P.engs["dve"].n))

    stage_mod(); mark("mod")
    stage_t_in(); mark("t_in")
    for l in range(L):
        stage_inproj(l); mark('inproj%d' % l)
        stage_attn(l); mark('attn%d' % l)
        if STUB_C:
            stage_stub(l, "c")
        else:
            stage_ssd(l); mark('ssd%d' % l)
        if STUB_D:
            stage_stub(l, "d")
        else:
            stage_s5(l); mark('s5_%d' % l)
        stage_outproj(l); mark('outproj%d' % l)
        stage_ffn(l); mark('ffn%d' % l)
    stage_t_out(); mark('t_out')
    P.barrier()
    es.close()
    return nc, P


def _consts(NS):
    p = np.arange(128)
    misc = np.zeros((128, 64), np.float32)
    misc[:, 0] = p
    misc[:, 1] = 127 - p
    for hl in range(2):
        for half in range(2):
            lo = 64 * hl + 32 * half
            misc[lo:lo + 32, 2 + hl * 2 + half] = 1.0
    misc[0:64, 10] = 1.0
    misc[64:128, 11] = 1.0
    misc[:, 13] = 1.0
    misc[:, 15] = EPS
    mats = np.zeros((11, 128, 128), np.float32)
    mats[0] = 1.0 / 1024
    mats[1] = (p[:, None] // 64 == p[None, :] // 64) / 64.0
    mats[2] = 1.0 / 256
    pa = (p // 32) * 32 + ((p % 32) + 16) % 32
    pb = (p // 64) * 64 + ((p % 64) + 32) % 64
    mats[3][pa, p] = 1.0
    mats[4][pb, p] = 1.0
    mats[5] = (p[:, None] <= p[None, :])
    mats[6] = (p[:, None] < p[None, :])
    mats[7] = np.where(p[None, :] < p[:, None], -30000.0, 0.0)
    mats[8] = np.where(p[:, None] < p[None, :], 30000.0, 0.0)
    mats[9] = 1.0
    mats[10] = (p[:, None] >= p[None, :])
    misc[:, 16] = -p
    misc[:, 17] = -(127 - p)
    for q in range(8):
        misc[q * 16:(q + 1) * 16, 20 + q] = 1.0
    erow = np.zeros((2, 128, 129), np.float32)
    erow[0, :, 0:128] = np.arange(128)[None, :]
    erow[1, :, 0:128] = (127 - np.arange(128))[None, :]
    erow[:, :, 128] = 128.0
    m16 = np.zeros((128, 8, 128), np.float32)
    for q in range(8):
        m16[:, q, q * 16:(q + 1) * 16] = 1.0
    t = np.arange(NS)
    row = (t // 64).astype(np.float32)
    col = (t % 64).astype(np.float32)

    def ang(n):
        fr = (np.float32(10000.0) ** (-np.arange(n, dtype=np.float32) / np.float32(n))).astype(np.float32)
        return np.concatenate([row[:, None] * fr, col[:, None] * fr], axis=-1).astype(np.float32)

    aA = ang(8)
    aB = ang(16)
    ropeA = np.zeros((2, 128, NS), np.float32)
    ropeB = np.zeros((2, 128, NS), np.float32)
    for i in range(128):
        idx = i % 32
        ropeA[0, i] = np.cos(aA[:, idx % 16])
        ropeA[1, i] = np.sin(aA[:, idx % 16]) * (-1.0 if idx < 16 else 1.0)
        idx = i % 64
        ropeB[0, i] = np.cos(aB[:, idx % 32])
        ropeB[1, i] = np.sin(aB[:, idx % 32]) * (-1.0 if idx < 32 else 1.0)
    return dict(c_ident=np.eye(128, dtype=np.float32), c_misc=misc, c_mats=mats, c_m16=m16, ropeA=ropeA, ropeB=ropeB, c_erow=erow)


def _win_cols():
    aq, ak, av, bq, bk, bv, cz, cx, cb, cc, cdt, du = 0, 256, 512, 768, 1024, 1152, 1280, 1536, 1792, 1920, 2048, 2056
    r = lambda a, n: list(range(a, a + n))
    cols = r(aq, 256) + r(ak, 256)
    cols += r(bq, 64) + r(bq + 128, 64) + r(bq + 64, 64) + r(bq + 192, 64)
    cols += r(bk, 128) + r(cz, 256) + r(cx, 256) + r(cb, 128) + r(cc, 128) + r(cdt, 8) + r(du, 256)
    cols += r(av, 256) + r(bv, 128) + r(ak, 256) + r(bk, 128)
    return np.array(cols)


def make_in_maps(inp, n_cores, NS, prompt_per_core=2):
    f = lambda a: np.ascontiguousarray(np.asarray(a, dtype=np.float32))
    L = DEPTH
    consts = _consts(NS)
    shared = dict(
        w_mod=f(inp["w_mod"]), b_mod=f(inp["b_mod"]), g_pre1=f(inp["g_pre1"]), g_post1=f(inp["g_post1"]),
        g_pre2=f(inp["g_pre2"]), g_post2=f(inp["g_post2"]), w_in=f(np.asarray(inp["w_in"])[:, :, _win_cols()]),
        a_lam=f(np.asarray(inp["a_lam"]).reshape(L, 128)), a_subln=f(inp["a_subln"]), b_qnorm=f(inp["b_qnorm"]),
        b_knorm=f(inp["b_knorm"]), c_conv_w=f(inp["c_conv_w"]), c_conv_b=f(inp["c_conv_b"]),
        c_dt_bias=f(np.asarray(inp["c_dt_bias"]).reshape(L, 8)), c_a_log=f(np.asarray(inp["c_a_log"]).reshape(L, 8)),
        c_d=f(inp["c_d"]), c_norm=f(inp["c_norm"]), d_lam_re=f(inp["d_lam_re"]), d_lam_im=f(inp["d_lam_im"]),
        d_log_step=f(inp["d_log_step"]), d_b=f(inp["d_b"]), d_c=f(np.asarray(inp["d_c"]).reshape(L, 2, 256, 128)),
        d_d=f(inp["d_d"]), d_glu=f(inp["d_glu"]), w_out=f(inp["w_out"]), w_up=f(inp["w_up"]),
        ffn_conv_w=f(inp["ffn_conv_w"]), ffn_conv_b=f(inp["ffn_conv_b"]), w_down=f(inp["w_down"]), **consts)
    xp = np.asarray(inp["x_prompt"], np.float32)
    xs = np.asarray(inp["x_sample"], np.float32)
    nsb = xs.shape[0]
    maps = []
    for c in range(n_cores):
        sb_ = (c * nsb) // n_cores
        m = dict(shared)
        m["xin"] = f(np.concatenate([xp[prompt_per_core * c + i] for i in range(prompt_per_core)] + [xs[sb_]], axis=0))
        m["cond"] = f(np.stack([np.asarray(inp["c_ctx"], np.float32), np.asarray(inp["c"], np.float32)[sb_]], axis=0))
        m["cak"] = f(np.asarray(inp["cache_a_k"])[sb_].reshape(L, 256, 256))
        m["cav"] = f(np.asarray(inp["cache_a_v"])[sb_].reshape(L, 256, 256))
        m["cbk"] = f(np.asarray(inp["cache_b_k"])[sb_].reshape(L, 256, 128))
        m["cbv"] = f(np.asarray(inp["cache_b_v"])[sb_].reshape(L, 256, 128))
        m["sssd"] = f(np.asarray(inp["state_ssd"])[sb_])
        m["ss5"] = f(np.asarray(inp["state_s5"])[sb_])
        maps.append(m)
    return maps


def gather(results, n_cores, NS, nsb, prompt_per_core=2):
    L = DEPTH
    yp = np.concatenate([r["yout"][0:512].reshape(2, 256, D) for r in results], axis=0)
    per = n_cores // nsb
    ys = np.stack([results[b * per]["yout"][512:512 + NS] for b in range(nsb)], axis=0)
    cat = lambda k: np.concatenate([r[k] for r in results], axis=0)
    nak = cat("o_ak").reshape(-1, L, 256, 4, 64)
    nav = cat("o_av").reshape(-1, L, 256, 4, 64)
    nbk = cat("o_bk").reshape(-1, L, 256, 2, 64)
    nbv = cat("o_bv").reshape(-1, L, 256, 2, 64)
    nssd = cat("o_ssd")
    ns5 = cat("o_s5")
    return tuple(np.ascontiguousarray(a.astype(np.float32)) for a in (yp, ys, nak, nav, nbk, nbv, nssd, ns5))


_CACHE = {}


def kernel(**inputs):
    NS = 4096
    n = 8
    if NS not in _CACHE:
        _CACHE[NS] = build(NS)
    nc, P = _CACHE[NS]
    maps = make_in_maps(inputs, n, NS)
    res = run_bass_kernel_spmd(nc, maps, core_ids=list(range(n)))
    return gather(res.results, n, NS, 2)
```
